# Optimizing a Trainium2 kernel written in Bass

```python
import jax, jax.numpy as jnp
from jax import lax
import numpy as np

D_MODEL = 1024
BATCH = 8
SEQ = 4096
DEPTH = 2

CTX_LEN = 256
GRID_W = 64
HEAD_DIM = 64
D_FF = ((8 * D_MODEL // 3 + 127) // 128) * 128
D_LRU = D_MODEL // 2
N_LRU_HEADS = D_LRU // HEAD_DIM
D_GMLP = D_MODEL // 4
N_GMLP_GROUPS = D_GMLP // HEAD_DIM
GMLP_CHUNK = 128
D_FNET = D_MODEL // 4
N_FNET_GROUPS = D_FNET // HEAD_DIM
D_MIX = D_LRU + D_GMLP + D_FNET
D_IN = 2 * D_LRU + 2 * D_GMLP + D_FNET
CONV_W = 4
LRU_C = 8.0
N_MOD = 9
EPS = 1e-6

kernel_name = "hybrid_rglru_gmlp_fnet_macaron_dit"


def rmsnorm(x, g):
    xf = x.astype(jnp.float32)
    y = xf * lax.rsqrt(jnp.mean(xf * xf, axis=-1, keepdims=True) + EPS)
    return (y * g.astype(jnp.float32)).astype(x.dtype)


def adaln(c_vec, w, b):
    m = jax.nn.silu(c_vec) @ w + b
    return m.reshape(m.shape[:-1] + (1, N_MOD, D_MODEL))


def sincos_2d(rows, d):
    r, col = jnp.meshgrid(jnp.arange(rows, dtype=jnp.float32),
                          jnp.arange(GRID_W, dtype=jnp.float32), indexing="ij")
    q = d // 4
    freqs = 1.0 / (10000.0 ** (jnp.arange(q, dtype=jnp.float32) / q))
    er = r.reshape(-1, 1) * freqs
    ec = col.reshape(-1, 1) * freqs
    return jnp.concatenate([jnp.sin(er), jnp.cos(er), jnp.sin(ec), jnp.cos(ec)], axis=-1)


def ffn_sublayer(h, mod, k0, g, w_gu, w_down):
    shift, scale, gate = mod[..., k0, :], mod[..., k0 + 1, :], mod[..., k0 + 2, :]
    n = rmsnorm(h, g) * (1 + scale) + shift
    gt, up = jnp.split(n @ w_gu, 2, axis=-1)
    return h + 0.5 * gate * ((jax.nn.silu(gt) * up) @ w_down)


def mix_in(h, mod, g, w_in):
    n = rmsnorm(h, g) * (1 + mod[..., 4, :]) + mod[..., 3, :]
    z = n @ w_in
    return jnp.split(z, [D_LRU, 2 * D_LRU, 2 * D_LRU + D_GMLP, 2 * D_LRU + 2 * D_GMLP], axis=-1)


def centred_conv(x, w, b):
    L = x.shape[1]
    left = CONV_W // 2
    right = CONV_W - 1 - left
    xp = jnp.pad(x, ((0, 0), (left, right), (0, 0)))
    return b + sum(xp[:, k:k + L] * w[k] for k in range(CONV_W))


def _lin_combine(e1, e2):
    a1, b1 = e1
    a2, b2 = e2
    return a1 * a2, a2 * b1 + b2


def rglru_direction(xc, w_g, b_g, lam, h0, reverse):
    B_, L, _ = xc.shape
    xh = xc.reshape(B_, L, N_LRU_HEADS, HEAD_DIM)
    gates = jnp.einsum("blhi,ghij->gblhj", xh, w_g).reshape(2, B_, L, D_LRU) + b_g[:, None, None, :]
    gates = gates.astype(jnp.float32)
    r = jax.nn.sigmoid(gates[0])
    i = jax.nn.sigmoid(gates[1])
    log_a = -LRU_C * r * jax.nn.softplus(-lam.astype(jnp.float32))
    a = jnp.exp(log_a)
    b = jnp.sqrt(-jnp.expm1(2.0 * log_a)) * i * xc.astype(jnp.float32)
    if h0 is not None:
        edge = -1 if reverse else 0
        b = b.at[:, edge].add(a[:, edge] * h0)
    _, h = lax.associative_scan(_lin_combine, (a, b), axis=1, reverse=reverse)
    final = h[:, 0] if reverse else h[:, -1]
    return h, final


def lru_branch(xa, conv_w, conv_b, w_gates, b_gates, lam, h0):
    xc = centred_conv(xa, conv_w, conv_b)
    hf, sf = rglru_direction(xc, w_gates[0], b_gates[0], lam[0], None if h0 is None else h0[0], False)
    hb, sb = rglru_direction(xc, w_gates[1], b_gates[1], lam[1], None if h0 is None else h0[1], True)
    return hf + hb, jnp.stack([sf, sb])


def gmlp_chunk(u, v, ws, bs):
    B_, L, _ = v.shape
    vc = v.reshape(B_, L // GMLP_CHUNK, GMLP_CHUNK, N_GMLP_GROUPS, D_GMLP // N_GMLP_GROUPS)
    mixed = jnp.einsum("gpq,bcqgd->bcpgd", ws, vc) + bs.T[None, None, :, :, None]
    return u * mixed.reshape(B_, L, D_GMLP)


def fourier_mix(f):
    B_, L, _ = f.shape
    fg = f.reshape(B_, L, N_FNET_GROUPS, D_FNET // N_FNET_GROUPS).astype(jnp.float32)
    y = jnp.fft.fft2(fg, axes=(1, 3), norm="ortho").real
    return y.reshape(B_, L, D_FNET).astype(f.dtype)


def mix_out(h, mod, y_lru, ga, u, v, f, ws, bs, w_out):
    y_a = y_lru.astype(h.dtype) * jax.nn.gelu(ga)
    y_b = gmlp_chunk(jax.nn.gelu(u), jax.nn.gelu(v), ws, bs)
    y_c = fourier_mix(f)
    y = jnp.concatenate([y_a, y_b, y_c], axis=-1) @ w_out
    return h + mod[..., 5, :] * y


def setup_inputs(seed: int = 0) -> dict:
    key = jax.random.key(seed)
    ks = jax.random.split(key, 20)

    def nrm(k, shape, scale):
        return scale * jax.random.normal(k, shape, jnp.float32)

    x = nrm(ks[0], (BATCH, SEQ, D_MODEL), 1.0)
    c = nrm(ks[1], (BATCH, D_MODEL), 1.0)
    ctx = nrm(ks[2], (BATCH, CTX_LEN, D_MODEL), 1.0)
    c_ctx = nrm(ks[3], (D_MODEL,), 1.0)
    w_mod = nrm(ks[4], (DEPTH, D_MODEL, N_MOD * D_MODEL), 0.5 * D_MODEL ** -0.5)
    b_mod = nrm(ks[5], (DEPTH, N_MOD * D_MODEL), 0.02)
    norm_g = 1.0 + nrm(ks[6], (DEPTH, 3, D_MODEL), 0.02)
    ffn_w_gu = nrm(ks[7], (DEPTH, 2, D_MODEL, 2 * D_FF), D_MODEL ** -0.5)
    ffn_w_down = nrm(ks[8], (DEPTH, 2, D_FF, D_MODEL), D_FF ** -0.5)
    w_in = nrm(ks[9], (DEPTH, D_MODEL, D_IN), D_MODEL ** -0.5)
    w_out = nrm(ks[10], (DEPTH, D_MIX, D_MODEL), D_MIX ** -0.5)
    conv_w = nrm(ks[11], (DEPTH, CONV_W, D_LRU), CONV_W ** -0.5)
    conv_b = nrm(ks[12], (DEPTH, D_LRU), 0.02)
    lru_w_gates = nrm(ks[13], (DEPTH, 2, 2, N_LRU_HEADS, HEAD_DIM, HEAD_DIM), HEAD_DIM ** -0.5)
    lru_b_gates = nrm(ks[14], (DEPTH, 2, 2, D_LRU), 0.02)
    a_c = jax.random.uniform(ks[15], (DEPTH, 2, D_LRU), jnp.float32, 0.9, 0.999)
    a = a_c ** (1.0 / LRU_C)
    lru_lambda = jnp.log(a) - jnp.log1p(-a)
    gmlp_ws = nrm(ks[16], (DEPTH, N_GMLP_GROUPS, GMLP_CHUNK, GMLP_CHUNK), GMLP_CHUNK ** -0.5)
    gmlp_bs = 1.0 + nrm(ks[17], (DEPTH, N_GMLP_GROUPS, GMLP_CHUNK), 0.02)
    final_norm_g = 1.0 + nrm(ks[18], (D_MODEL,), 0.02)
    return {"x": x, "c": c, "ctx": ctx, "c_ctx": c_ctx, "w_mod": w_mod, "b_mod": b_mod,
            "norm_g": norm_g, "ffn_w_gu": ffn_w_gu, "ffn_w_down": ffn_w_down,
            "w_in": w_in, "w_out": w_out, "conv_w": conv_w, "conv_b": conv_b,
            "lru_w_gates": lru_w_gates, "lru_b_gates": lru_b_gates, "lru_lambda": lru_lambda,
            "gmlp_ws": gmlp_ws, "gmlp_bs": gmlp_bs, "final_norm_g": final_norm_g}


def reference(x, c, ctx, c_ctx, w_mod, b_mod, norm_g, ffn_w_gu, ffn_w_down, w_in, w_out,
              conv_w, conv_b, lru_w_gates, lru_b_gates, lru_lambda, gmlp_ws, gmlp_bs,
              final_norm_g):
    rows = x.shape[1] // GRID_W
    h = x + sincos_2d(rows, D_MODEL).astype(x.dtype)[None]
    hc = ctx
    for l in range(DEPTH):
        last = l == DEPTH - 1
        mod = adaln(c, w_mod[l], b_mod[l])
        mod_c = adaln(c_ctx, w_mod[l], b_mod[l])

        h = ffn_sublayer(h, mod, 0, norm_g[l, 0], ffn_w_gu[l, 0], ffn_w_down[l, 0])
        hc = ffn_sublayer(hc, mod_c, 0, norm_g[l, 0], ffn_w_gu[l, 0], ffn_w_down[l, 0])

        xa_c, ga_c, u_c, v_c, f_c = mix_in(hc, mod_c, norm_g[l, 1], w_in[l])
        y_lru_c, state_c = lru_branch(xa_c, conv_w[l], conv_b[l], lru_w_gates[l],
                                      lru_b_gates[l], lru_lambda[l], None)
        xa, ga, u, v, f = mix_in(h, mod, norm_g[l, 1], w_in[l])
        y_lru, _ = lru_branch(xa, conv_w[l], conv_b[l], lru_w_gates[l],
                              lru_b_gates[l], lru_lambda[l], state_c)
        h = mix_out(h, mod, y_lru, ga, u, v, f, gmlp_ws[l], gmlp_bs[l], w_out[l])

        h = ffn_sublayer(h, mod, 6, norm_g[l, 2], ffn_w_gu[l, 1], ffn_w_down[l, 1])
        if not last:
            hc = mix_out(hc, mod_c, y_lru_c, ga_c, u_c, v_c, f_c, gmlp_ws[l], gmlp_bs[l], w_out[l])
            hc = ffn_sublayer(hc, mod_c, 6, norm_g[l, 2], ffn_w_gu[l, 1], ffn_w_down[l, 1])
    return rmsnorm(h, final_norm_g)
```

```python
import contextlib
import numpy as np
import ml_dtypes
import concourse.bass as bass
import concourse.mybir as mybir
from concourse.bass_utils import run_bass_kernel_spmd

F32 = mybir.dt.float32
BF16 = mybir.dt.bfloat16
ALU = mybir.AluOpType
AF = mybir.ActivationFunctionType

D = 1024
KC = 8
FF = 2816
JC = 22
T = 256
NB = 17
NTOK = 4352
L = 4096
DEPTH = 2
EPS = 1e-6
SB_BASE = 16512
SB_END = 229344
ARENA_BYTES = SB_END - SB_BASE

SAME_ENGINE_SYNC = True


class Res:
    __slots__ = ("name", "w", "rs", "rd")

    def __init__(self, name):
        self.name = name
        self.w = None
        self.rs = {}
        self.rd = []


class Op:
    __slots__ = ("eng", "emit", "deps", "is_dma", "dsem", "needs_inc", "tok", "waits")

    def __init__(self, eng, emit, is_dma=False, dsem=None):
        self.eng = eng
        self.emit = emit
        self.deps = []
        self.is_dma = is_dma
        self.dsem = dsem
        self.needs_inc = False
        self.tok = None
        self.waits = []


class Prog:
    ENGS = ("pe", "act", "dve", "pool", "sp")

    def __init__(self, nc):
        self.nc = nc
        self.stack = contextlib.ExitStack()
        self.ops = []
        self.nsem = 0
        self.dsems = {}
        self.ntens = 0
        self.pending = {}
        self.last_c = {}
        self.dma_since = []

    def fence(self):
        deps = list(self.last_c.values()) + list(self.dma_since)
        self.dma_since = []
        for e in self.ENGS:
            self.pending[e] = list(self.pending.get(e, [])) + deps

    def sem(self, name):
        self.nsem += 1
        return self.stack.enter_context(self.nc.semaphore(f"{name}_{self.nsem}"))

    def dsem(self, key):
        if key not in self.dsems:
            self.dsems[key] = [self.sem("d"), 0]
        return self.dsems[key]

    def psum(self, shape, dtype=F32):
        self.ntens += 1
        return self.stack.enter_context(self.nc.psum_tensor(f"ps{self.ntens}", list(shape), dtype))

    def _track(self, op, reads, writes):
        deps = []
        for r in reads:
            if r.w is not None:
                deps.append(r.w)
        for w in writes:
            if w.w is not None and not (w.rs or w.rd):
                deps.append(w.w)
            deps.extend(w.rs.values())
            deps.extend(w.rd)
        if op.eng in self.pending:
            deps.extend(self.pending.pop(op.eng))
        seen = set()
        for d in deps:
            if id(d) in seen or d is op:
                continue
            seen.add(id(d))
            op.deps.append(d)
        if op.is_dma:
            self.dma_since.append(op)
        else:
            self.last_c[op.eng] = op
        for r in reads:
            if op.is_dma:
                r.rd.append(op)
            else:
                r.rs[op.eng] = op
        for w in writes:
            w.w = op
            w.rs = {}
            w.rd = []
        self.ops.append(op)
        return op

    def op(self, eng, emit, reads=(), writes=()):
        return self._track(Op(eng, emit), reads, writes)

    def dma(self, queue, out, in_, reads=(), writes=(), key=None):
        ds = self.dsem(key if key is not None else (writes[0].name if writes else reads[0].name))
        o = Op(queue, lambda e, out=out, in_=in_: e.dma_start(out=out, in_=in_), is_dma=True, dsem=ds)
        return self._track(o, reads, writes)

    def wait_all(self, eng, ops):
        o = Op(eng, None)
        o.deps = list(ops)
        self.ops.append(o)

    def emit(self):
        nc = self.nc
        csem = {e: self.sem("c" + e) for e in self.ENGS}

        def skip(d, o):
            if d.is_dma:
                return False
            if d.eng == "pe" and o.eng == "pe":
                return True
            if (not SAME_ENGINE_SYNC) and d.eng == o.eng and not o.is_dma:
                return True
            return False

        for o in self.ops:
            for d in o.deps:
                if not skip(d, o) and not d.is_dma:
                    d.needs_inc = True
        cnt = {e: 0 for e in self.ENGS}
        for o in self.ops:
            if o.is_dma:
                o.dsem[1] += 16
                o.tok = (o.dsem[0], o.dsem[1])
            elif o.needs_inc:
                cnt[o.eng] += 1
                o.tok = (csem[o.eng], cnt[o.eng])
        seen = {e: {} for e in self.ENGS}
        issued = {}
        for o in self.ops:
            s = seen[o.eng]
            need = {}
            for d in o.deps:
                if d.tok is None or skip(d, o):
                    continue
                sem, val = d.tok
                k = id(sem)
                if d.is_dma and not o.is_dma:
                    val = max(val, issued.get(k, 0))
                if k not in need or need[k][1] < val:
                    need[k] = (sem, val)
            if o.is_dma:
                issued[id(o.tok[0])] = o.tok[1]
            for k, (sem, val) in need.items():
                if s.get(k, 0) >= val:
                    continue
                s[k] = val
                o.waits.append((sem, val))
        per = {e: [o for o in self.ops if o.eng == e] for e in self.ENGS}
        self.counts = {e: len(per[e]) for e in self.ENGS}
        self.sem_counts = cnt

        def run(engobj, lst):
            for o in lst:
                for sem, val in o.waits:
                    engobj.wait_ge(sem, val)
                if o.emit is None:
                    continue
                ins = o.emit(engobj)
                if o.is_dma:
                    ins.then_inc(o.dsem[0], 16)
                elif o.needs_inc:
                    ins.then_inc(o.tok[0], 1)

        with nc.Block() as block:
            @block.tensor
            def _(e):
                run(e, per["pe"])

            @block.scalar
            def _(e):
                run(e, per["act"])

            @block.vector
            def _(e):
                run(e, per["dve"])

            @block.gpsimd
            def _(e):
                run(e, per["pool"])

            @block.sync
            def _(e):
                run(e, per["sp"])
        self.stack.close()


class Arena:
    def __init__(self, nc):
        self.t = nc.alloc_sbuf_tensor_at("arena", [128, ARENA_BYTES // 4], F32, offset=SB_BASE)
        self.off = 0

    def alloc(self, shape, dtype, parts=128):
        esz = 4 if dtype == F32 else 2
        n = int(np.prod(shape))
        nb = (n * esz + 31) // 32 * 32
        o4 = self.off // 4
        v = self.t[0:parts, o4:o4 + nb // 4]
        if dtype != F32:
            v = v.bitcast(dtype)
        v = v[:, 0:n]
        if len(shape) == 2:
            v = v.rearrange("p (a b) -> p a b", b=shape[1])
        elif len(shape) == 3:
            v = v.rearrange("p (a b c) -> p a b c", b=shape[1], c=shape[2])
        elif len(shape) == 4:
            v = v.rearrange("p (a b c d) -> p a b c d", b=shape[1], c=shape[2], d=shape[3])
        self.off += nb
        assert self.off <= ARENA_BYTES, f"SBUF arena overflow {self.off} > {ARENA_BYTES}"
        return v


SM = {}
_o = 0
for _n, _w in [("c", 8), ("cctx", 8), ("bmod", 144), ("ng", 48), ("fng", 8), ("convw", 32), ("convb", 8),
               ("lrub", 32), ("lam", 16), ("pr", 256), ("pc", 256), ("bsbc", 512)]:
    SM[_n] = (_o, _w)
    _o += _w
NS = _o


def _pm(v):
    v = np.asarray(v, np.float32)
    n = v.shape[-1] // 128
    v = v.reshape(v.shape[:-1] + (n, 128))
    return np.moveaxis(v, -1, 0)


def build_smalls(b, inp):
    s = np.zeros((128, NS), np.float32)

    def put(name, arr):
        o, w = SM[name]
        s[:, o:o + w] = np.asarray(arr, np.float32).reshape(128, w)

    put("c", _pm(inp["c"][b]))
    put("cctx", _pm(inp["c_ctx"]))
    put("bmod", _pm(inp["b_mod"]))
    put("ng", _pm(inp["norm_g"]))
    put("fng", _pm(inp["final_norm_g"]))
    cw = _pm(inp["conv_w"])
    put("convw", np.transpose(cw, (0, 1, 3, 2)))
    put("convb", _pm(inp["conv_b"]))
    put("lrub", _pm(inp["lru_b_gates"]))
    put("lam", _pm(inp["lru_lambda"]))
    q = D // 4
    freqs = (1.0 / (10000.0 ** (np.arange(q, dtype=np.float32) / np.float32(q)))).astype(np.float32)
    pos = np.arange(64, dtype=np.float32)
    e = pos[:, None] * freqs[None, :]
    tab = np.concatenate([np.sin(e), np.cos(e)], axis=1).astype(np.float32)
    tabT = tab.T.reshape(4, 128, 64).transpose(1, 0, 2)
    put("pr", tabT)
    put("pc", tabT)
    bs = np.asarray(inp["gmlp_bs"], np.float32)
    bsbc = np.zeros((128, 2, 2, 128), np.float32)
    for l in range(2):
        for j in range(2):
            bsbc[0:64, l, j, :] = bs[l, 2 * j][None, :]
            bsbc[64:128, l, j, :] = bs[l, 2 * j + 1][None, :]
    put("bsbc", bsbc)
    return s


def build_consts():
    bf = ml_dtypes.bfloat16
    p = np.arange(128, dtype=np.int64)
    kp = np.arange(512, dtype=np.int64)
    C0 = np.empty((2, 128, 32, 512), np.float32)
    for r in range(8):
        for qc in range(4):
            n = 8 * (128 * qc + p) + r
            ang = (2.0 * np.pi / L) * ((n[:, None] * kp[None, :]) % L).astype(np.float64)
            C0[0, :, 4 * r + qc, :] = np.cos(ang) / 64.0
            C0[1, :, 4 * r + qc, :] = -np.sin(ang) / 64.0
    C0 = C0.reshape(2, 128, 32 * 512).astype(bf)
    DG = np.zeros((128, 4, 128), np.float32)
    eye = np.eye(128, dtype=np.float32)
    for i, v in enumerate((1.0, -1.0, np.sqrt(0.5), -np.sqrt(0.5))):
        DG[:, i, :] = eye * np.float32(v)
    DG = DG.reshape(128, 512).astype(bf)
    n2 = np.arange(256, dtype=np.int64)
    ang2 = (2.0 * np.pi / 256) * ((n2[:, None] * n2[None, :]) % 256).astype(np.float64)
    C2 = (np.cos(ang2) / 16.0).astype(np.float32).reshape(2, 128, 256)
    S2 = (-np.sin(ang2) / 16.0).astype(np.float32).reshape(2, 128, 256)
    CLc = np.stack([C2, S2], 0).transpose(2, 0, 1, 3).reshape(128, 2 * 2 * 256).astype(bf)
    m = np.arange(64, dtype=np.int64)
    a64 = (2.0 * np.pi / 64) * ((m[:, None] * m[None, :]) % 64).astype(np.float64)
    c64 = np.cos(a64) / 8.0
    s64 = np.sin(a64) / 8.0
    CS = np.zeros((128, 3, 128), np.float32)
    for g in range(2):
        CS[g * 64:(g + 1) * 64, 0, g * 64:(g + 1) * 64] = c64
        CS[g * 64:(g + 1) * 64, 1, g * 64:(g + 1) * 64] = s64
        CS[g * 64:(g + 1) * 64, 2, g * 64:(g + 1) * 64] = -c64
    CS = CS.reshape(128, 384).astype(bf)
    return C0, DG, CLc, CS


def build(stop_after=None, debug=False):
    nc = bass.Bass("TRN2", target_bir_lowering=False)
    P = Prog(nc)
    A = Arena(nc)

    def din(name, shape, dt=F32):
        return nc.dram_tensor(name, list(shape), dt, kind="ExternalInput").ap()

    skind = "ExternalOutput" if debug else "Internal"

    def dscratch(name, shape, dt):
        return nc.dram_tensor(name, list(shape), dt, kind=skind).ap()

    xTb = din("xTb", [16, 128, KC * T])
    ctxTb = din("ctxTb", [128, KC * T])
    smalls_d = din("smalls", [128, NS])
    w_mod = din("w_mod", [2, D, 9 * D])
    w_gu = din("ffn_w_gu", [2, 2, D, 2 * FF])
    w_dn = din("ffn_w_down", [2, 2, FF, D])
    w_in = din("w_in", [2, D, 1792])
    w_out = din("w_out", [2, D, D])
    w_gates = din("lru_w_gates", [2, 2, 2, 8, 64, 64])
    wsT_d = din("gmlp_wsT", [2, 4, 128, 128])
    C0_d = din("C0", [2, 128, 32 * 512], BF16)
    DG_d = din("DG", [128, 512], BF16)
    CLc_d = din("CLc", [128, 1024], BF16)
    CS_d = din("CS", [128, 384], BF16)
    outTb = nc.dram_tensor("outTb", [16, 128, KC * T], F32, kind="ExternalOutput").ap()

    hTb = dscratch("hTb", [NB, 128, KC * T], F32)
    zxa = dscratch("zxa", [4, 128, NTOK], BF16)
    zga = dscratch("zga", [4, 128, NTOK], BF16)
    zu = dscratch("zu", [2, 128, NTOK], BF16)
    zf = dscratch("zf", [2, 128, NTOK], BF16)
    zv = dscratch("zv", [128, 34, 256], BF16)
    yT = dscratch("yT", [KC, 128, NTOK], BF16)

    wgu_bf = [nc.dram_tensor(f"wgu_bf{i}", [128, KC * 2 * FF], BF16, kind="Internal").ap() for i in range(3)]
    wdn_bf = [nc.dram_tensor(f"wdn_bf{i}", [128, JC * D], BF16, kind="Internal").ap() for i in range(3)]
    R_wgu_bf = [[Res(f"wgubf{i}_{kc}") for kc in range(KC)] for i in range(3)]
    R_wdn_bf = [[Res(f"wdnbf{i}_{h}") for h in range(2)] for i in range(3)]
    JG = [(0, 6), (6, 12), (12, 17), (17, 22)]

    R_h = [Res(f"hTb{b}") for b in range(NB)]
    R_y = [[Res(f"yTb{b}_{m}") for m in range(KC)] for b in range(NB)]
    R_zxa = [Res(f"zxa{c}") for c in range(4)]
    R_zga = [Res(f"zga{c}") for c in range(4)]
    R_zu = [Res(f"zu{c}") for c in range(2)]
    R_zf = [Res(f"zf{c}") for c in range(2)]
    R_zv = Res("zv")
    out_stores = []

    sm = A.alloc([NS], F32)
    R_sm = Res("sm")

    def S(name):
        o, w = SM[name]
        return sm[:, o:o + w]

    MOD = A.alloc([2, 72, 2], F32)
    AV = A.alloc([2, 2, 3, 8], F32)
    GV = A.alloc([2, 2, 3, 8], F32)
    CH2 = A.alloc([2, 2, 4], F32)
    HB = A.alloc([2, 2, 2, 4], F32)
    SC = A.alloc([8, 2], BF16)
    sctmp = A.alloc([8, 2], F32)
    eps_t = A.alloc([1], F32)
    R_const = Res("const")
    R_modl = [Res("mod0"), Res("mod1")]
    R_sc = Res("silu_c")
    cur = {"l": 0}

    def RM():
        return R_modl[cur["l"]]
    persist_off = A.off

    PS = [P.psum([128, 512]) for _ in range(8)]
    R_ps = [Res(f"psbank{i}") for i in range(8)]

    P.dma("sp", sm, smalls_d, writes=[R_sm])
    P.op("dve", lambda e: e.memset(eps_t, EPS), writes=[R_const])

    def SHIFT(l, s, which, m):
        j = (3 * which) * 8 + m
        return MOD[:, l, j, s:s + 1]

    def prologue():
        P.op("act", lambda e: e.activation(sctmp[:, :, 0], S("c"), AF.Silu), reads=[R_sm], writes=[R_sc])
        P.op("act", lambda e: e.activation(sctmp[:, :, 1], S("cctx"), AF.Silu), reads=[R_sm], writes=[R_sc])
        P.op("dve", lambda e: e.tensor_copy(SC, sctmp), reads=[R_sc], writes=[R_sc])
        lamv = S("lam")
        ch2f = CH2.rearrange("p a b c -> p (a b c)")
        sp_x = A.alloc([16], F32)
        sp_z = A.alloc([16], F32)
        sp_z2 = A.alloc([16], F32)
        sp_p = A.alloc([16], F32)
        rc = [R_sm, R_const]
        P.op("act", lambda e: e.activation(sp_x, lamv, AF.Abs), reads=rc, writes=[R_const])
        P.op("act", lambda e: e.activation(sp_x, sp_x, AF.Exp, scale=-1.0), reads=rc, writes=[R_const])
        P.op("dve", lambda e: e.tensor_scalar(sp_z, sp_x, 2.0, None, ALU.add), reads=rc, writes=[R_const])
        P.op("dve", lambda e: e.reciprocal(sp_z, sp_z), reads=rc, writes=[R_const])
        P.op("dve", lambda e: e.tensor_tensor(sp_z, sp_z, sp_x, ALU.mult), reads=rc, writes=[R_const])
        P.op("dve", lambda e: e.tensor_tensor(sp_z2, sp_z, sp_z, ALU.mult), reads=rc, writes=[R_const])
        P.op("dve", lambda e: e.memset(sp_p, 1.0 / 15.0), reads=rc, writes=[R_const])
        for cf in (1.0 / 13, 1.0 / 11, 1.0 / 9, 1.0 / 7, 1.0 / 5, 1.0 / 3, 1.0):
            P.op("dve", lambda e: e.tensor_tensor(sp_p, sp_p, sp_z2, ALU.mult), reads=rc, writes=[R_const])
            P.op("dve", lambda e, cf=cf: e.tensor_scalar(sp_p, sp_p, float(cf), None, ALU.add), reads=rc, writes=[R_const])
        P.op("dve", lambda e: e.tensor_tensor(sp_p, sp_p, sp_z, ALU.mult), reads=rc, writes=[R_const])
        P.op("dve", lambda e: e.tensor_scalar(sp_x, lamv, -1.0, 0.0, ALU.mult, ALU.max), reads=rc, writes=[R_const])
        P.op("dve", lambda e: e.scalar_tensor_tensor(sp_p, sp_p, 2.0, sp_x, ALU.mult, ALU.add), reads=rc, writes=[R_const])
        P.op("dve", lambda e: e.tensor_scalar(ch2f, sp_p, -4.0, None, ALU.mult), reads=rc, writes=[R_const])
        hbf = HB.rearrange("p a b c d -> p (a b c d)")
        P.op("dve", lambda e: e.tensor_scalar(hbf, S("lrub"), 0.5, None, ALU.mult), reads=[R_sm], writes=[R_const])
        mark = A.off
        for st in mod_layer_steps(0, 0):
            st()
        A.off = mark

    def mod_layer_steps(l, bank):
        NCH = 8
        wbuf = [A.alloc([KC, 1152], BF16) for _ in range(2)]
        R_wb = [[Res(f"wmod{l}_{i}_{kc}") for kc in range(KC)] for i in range(2)]
        bm = S("bmod").rearrange("p (l j) -> p l j", l=2)
        ps = PS[bank]
        ng = S("ng").rearrange("p (l w m) -> p l w m", l=2, w=3)

        def chunk(ch):
            bi = ch % 2
            for kc in range(KC):
                P.dma("pool", wbuf[bi][:, kc, :], w_mod[l, kc * 128:(kc + 1) * 128, ch * 1152:(ch + 1) * 1152],
                      writes=[R_wb[bi][kc]], key=f"wmod{l}_{bi}")
            for jj in range(9):
                j = ch * 9 + jj
                def emit(e, jj=jj, j=j):
                    for kc in range(KC):
                        ins = e.matmul(ps[:, 2 * j:2 * j + 2], wbuf[bi][:, kc, jj * 128:(jj + 1) * 128], SC[:, kc, :],
                                       start=(kc == 0), stop=(kc == KC - 1))
                    return ins
                P.op("pe", emit, reads=R_wb[bi] + [R_sc], writes=[R_ps[bank]])

        def epilogue():
            P.op("dve", lambda e: e.tensor_tensor(
                MOD[:, l], ps[:, 0:144].rearrange("p (j s) -> p j s", s=2),
                bm[:, l, :].unsqueeze(2).to_broadcast([128, 72, 2]), ALU.add),
                reads=[R_ps[bank], R_sm], writes=[R_modl[l]])
            for s_ in range(2):
                for w in range(3):
                    j0 = (3 * w + 1) * 8
                    P.op("dve", lambda e, s_=s_, w=w, j0=j0: e.scalar_tensor_tensor(
                        AV[:, l, s_, w, :], MOD[:, l, j0:j0 + 8, s_], 1.0, ng[:, l, w, :], ALU.add, ALU.mult),
                        reads=[R_modl[l], R_sm], writes=[R_modl[l]])
                    j2 = (3 * w + 2) * 8
                    P.op("dve", lambda e, s_=s_, w=w, j2=j2: e.tensor_scalar(
                        GV[:, l, s_, w, :], MOD[:, l, j2:j2 + 8, s_], (1.0 if w == 1 else 0.5), None, ALU.mult),
                        reads=[R_modl[l]], writes=[R_modl[l]])

        return [(lambda ch=ch: chunk(ch)) for ch in range(NCH)] + [epilogue]

    def norm_steps(hb, R_hb, nT, R_nT, sq, R_sq, rstd, R_rstd, tmp, R_tmp, stat_bank, scale_ap, shift_ap, n=T,
                   mode="pool"):
        ps = PS[stat_bank]
        steps = []

        nsq = len(sq)

        def sq_a(m0):
            for m in (m0, m0 + 1):
                if mode in ("pool", "split"):
                    P.op("pool", lambda e, m=m: e.tensor_tensor(sq[m % nsq][:, 0:n], hb[:, m, 0:n], hb[:, m, 0:n], ALU.mult),
                         reads=[R_hb[m]], writes=[R_sq[m % nsq]])
                else:
                    P.op("act", lambda e, m=m: e.activation(sq[m % nsq][:, 0:n], hb[:, m, 0:n], AF.Square),
                         reads=[R_hb[m]], writes=[R_sq[m % nsq]])

        def sq_b(m0):
            for m in (m0, m0 + 1):
                P.op("pe", lambda e, m=m: e.matmul(ps[:, 0:n], ones_mat, sq[m % nsq][:, 0:n],
                                                  start=(m == 0), stop=(m == KC - 1)),
                     reads=[R_sq[m % nsq], R_const], writes=[R_ps[stat_bank]])

        def sq_step(m0):
            sq_a(m0)
            sq_b(m0)

        def rstd_step():
            P.op("act", lambda e: e.activation(rstd[:, 0:n], ps[:, 0:n], AF.Ln, bias=eps_t[:, 0:1], scale=1.0 / D),
                 reads=[R_ps[stat_bank], R_const], writes=[R_rstd])
            P.op("act", lambda e: e.activation(rstd[:, 0:n], rstd[:, 0:n], AF.Exp, scale=-0.5),
                 reads=[R_rstd], writes=[R_rstd])

        def nrm_step(m):
            if mode == "split":
                P.op("dve", lambda e: e.tensor_tensor(tmp[m % 2][:, 0:n], hb[:, m, 0:n], rstd[:, 0:n], ALU.mult),
                     reads=[R_hb[m], R_rstd], writes=[R_tmp[m % 2]])
                P.op("pool", lambda e: e.tensor_scalar(nT[:, m, 0:n], tmp[m % 2][:, 0:n], scale_ap(m), shift_ap(m),
                                                       ALU.mult, ALU.add),
                     reads=[R_tmp[m % 2], RM()], writes=[R_nT[m]])
                return
            if mode != "pool":
                if shift_ap is None:
                    P.op("dve", lambda e: e.scalar_tensor_tensor(nT[:, m, 0:n], hb[:, m, 0:n], scale_ap(m), rstd[:, 0:n],
                                                                 ALU.mult, ALU.mult),
                         reads=[R_hb[m], R_rstd, RM()], writes=[R_nT[m]])
                    return
                P.op("dve", lambda e: e.tensor_tensor(tmp[m % 2][:, 0:n], hb[:, m, 0:n], rstd[:, 0:n], ALU.mult),
                     reads=[R_hb[m], R_rstd], writes=[R_tmp[m % 2]])
                P.op("act", lambda e: e.activation(nT[:, m, 0:n], tmp[m % 2][:, 0:n], AF.Identity,
                                                   bias=shift_ap(m), scale=scale_ap(m)),
                     reads=[R_tmp[m % 2], RM()], writes=[R_nT[m]])
                return
            P.op("pool", lambda e: e.tensor_tensor(tmp[m % 2][:, 0:n], hb[:, m, 0:n], rstd[:, 0:n], ALU.mult),
                 reads=[R_hb[m], R_rstd], writes=[R_tmp[m % 2]])
            if shift_ap is not None:
                P.op("pool", lambda e: e.tensor_scalar(nT[:, m, 0:n], tmp[m % 2][:, 0:n], scale_ap(m), shift_ap(m),
                                                       ALU.mult, ALU.add),
                     reads=[R_tmp[m % 2], RM()], writes=[R_nT[m]])
            else:
                P.op("pool", lambda e: e.tensor_scalar(nT[:, m, 0:n], tmp[m % 2][:, 0:n], scale_ap(m), None, ALU.mult),
                     reads=[R_tmp[m % 2], RM()], writes=[R_nT[m]])

        if nsq >= 8:
            for fn, m0 in ((sq_a, 0), (sq_a, 2), (sq_b, 0), (sq_a, 4), (sq_b, 2), (sq_a, 6), (sq_b, 4), (sq_b, 6)):
                steps.append(lambda fn=fn, m0=m0: fn(m0))
        else:
            for m0 in (0, 2, 4, 6):
                steps.append(lambda m0=m0: sq_step(m0))
        steps.append(rstd_step)
        for m in range(KC):
            steps.append(lambda m=m: nrm_step(m))
        return steps

    def norm_block(*a, **k):
        for st in norm_steps(*a, **k):
            st()

    ones_mat = A.alloc([128], BF16)
    persist_off = A.off
    P.op("dve", lambda e: e.memset(ones_mat, 1.0), writes=[R_const])

    def make_precast(si):
        fi = si + 1
        nl, nw = fi // 2, fi % 2
        dst_gu = wgu_bf[si].rearrange("p (kc n) -> p kc n", kc=KC)
        dst_dn = wdn_bf[si].rearrange("p (j n) -> p j n", j=JC)
        nsrc_dn = w_dn[nl, nw].rearrange("(j p) n -> p j n", p=128)
        parts = []
        for kc in range(KC):
            parts.append(lambda kc=kc: P.dma("pool", dst_gu[:, kc, :], w_gu[nl, nw, kc * 128:(kc + 1) * 128, :],
                                             writes=[R_wgu_bf[si][kc]], key=f"pcgu{si}"))
        for h in range(2):
            parts.append(lambda h=h: P.dma("pool", dst_dn[:, 11 * h:11 * (h + 1), :], nsrc_dn[:, 11 * h:11 * (h + 1), :],
                                           writes=[R_wdn_bf[si][h]], key=f"pcdn{si}_{h}"))
        return parts

    def ffn_phase(l, which):
        cur["l"] = l
        A.off = persist_off
        first = (l == 0 and which == 0)
        is_c = which == 2
        last = (l == DEPTH - 1) and is_c
        wsel = 0 if which == 0 else 1
        wgu = A.alloc([KC, 2 * FF], BF16)
        wdn = A.alloc([JC, D], BF16)
        idx = 2 * l + (1 if is_c else 0)
        R_wgu_g = [[Res(f"wgu_g{g}_{u}") for u in range(2)] for g in range(len(JG))]
        R_wdn = [Res(f"wdn{h}") for h in range(2)]
        jgroup = {}
        for g, (j0, j1) in enumerate(JG):
            for j in range(j0, j1):
                jgroup[j] = g
        use_scratch = idx >= 1
        if not use_scratch:
            src_gu = w_gu[l, wsel].rearrange("(kc p) n -> p kc n", p=128)
            src_dn = w_dn[l, wsel].rearrange("(j p) n -> p j n", p=128)
            wqs = ("pool", "pool")
            rd_gu, rd_dn = [], [[], []]
        else:
            src_gu = wgu_bf[idx - 1].rearrange("p (kc n) -> p kc n", kc=KC)
            src_dn = wdn_bf[idx - 1].rearrange("p (j n) -> p j n", j=JC)
            wqs = ("sp", "act")
            rd_gu, rd_dn = R_wgu_bf[idx - 1], [[R_wdn_bf[idx - 1][0]], [R_wdn_bf[idx - 1][1]]]

        def load_weights():
            for g, (j0, j1) in enumerate(JG):
                for u in range(2):
                    c0, c1 = u * FF + j0 * 128, u * FF + j1 * 128
                    P.dma(wqs[u], wgu[:, :, c0:c1], src_gu[:, :, c0:c1], reads=rd_gu, writes=[R_wgu_g[g][u]],
                          key=f"wgu{g}_{u}")
            for h in range(2):
                P.dma(wqs[h], wdn[:, 11 * h:11 * (h + 1), :], src_dn[:, 11 * h:11 * (h + 1), :], reads=rd_dn[h],
                      writes=[R_wdn[h]], key=f"wdn{h}")
        if is_c:
            wo = A.alloc([KC, D], BF16)
            R_wo = [Res(f"wo{kc}") for kc in range(KC)]
            for kc in range(KC):
                P.dma("pool", wo[:, kc, :], w_out[l, kc * 128:(kc + 1) * 128, :], writes=[R_wo[kc]], key="wo")
            ybuf = [A.alloc([KC, T], BF16) for _ in range(2)]
            R_yb = [Res(f"ybuf{i}") for i in range(2)]
        NH = 2 if is_c else 3
        hbuf = [A.alloc([KC, T], F32) for _ in range(NH)]
        R_hbuf = [[Res(f"hbuf{i}_{m}") for m in range(KC)] for i in range(NH)]
        nT = [A.alloc([KC, T], BF16) for _ in range(2)]
        R_nT = [[Res(f"nT{i}_{m}") for m in range(KC)] for i in range(2)]
        hid = A.alloc([JC, T], BF16)
        R_hid = [Res(f"hid{j}") for j in range(JC)]
        sq = [A.alloc([T], BF16) for _ in range(8)]
        R_sq = [Res(f"sq{i}") for i in range(8)]
        tmp = [A.alloc([T], F32) for _ in range(2)]
        R_tmp = [Res("tmp0"), Res("tmp1")]
        rstd = A.alloc([T], F32)
        R_rstd = Res("rstd")
        sg = [A.alloc([T], BF16) for _ in range(3)]
        R_sg = [Res(f"sg{i}") for i in range(3)]
        blocks = list(range(NB))
        if last:
            blocks = list(range(1, NB))
        last = False

        def load(b, slot):
            load1(b, slot)
            load2(b, slot)

        def load1(b, slot):
            rs = R_hbuf[slot]
            if first:
                src = ctxTb if b == 0 else xTb[b - 1]
                P.dma("sp", hbuf[slot].rearrange("p a b -> p (a b)"), src, writes=rs, key=f"hload{slot}")

        def pos_steps(b, slot):
            rs = R_hbuf[slot]
            steps = []
            if first and b > 0:
                def f1():
                    r0 = (b - 1) * 4
                    pr = S("pr").rearrange("p (m r) -> p m r", m=4)
                    pc = S("pc").rearrange("p (m r) -> p m r", m=4)
                    for m in range(4):
                        hv = hbuf[slot][:, m, :].rearrange("p (r c) -> p r c", c=64)
                        P.op("dve", lambda e, hv=hv, m=m, r0=r0: e.tensor_tensor(
                            hv, hv, pr[:, m, r0:r0 + 4].unsqueeze(2).to_broadcast([128, 4, 64]), ALU.add),
                            reads=[rs[m], R_sm], writes=[rs[m]])
                    for m in range(4, 8):
                        hv = hbuf[slot][:, m, :].rearrange("p (r c) -> p r c", c=64)
                        P.op("dve", lambda e, hv=hv, m=m: e.tensor_tensor(
                            hv, hv, pc[:, m - 4, :].unsqueeze(1).to_broadcast([128, 4, 64]), ALU.add),
                            reads=[rs[m], R_sm], writes=[rs[m]])
                steps.append(f1)
            return steps

        def load2(b, slot):
            rs = R_hbuf[slot]
            if not first:
                P.dma("sp", hbuf[slot].rearrange("p a b -> p (a b)"), hTb[b], reads=[R_h[b]], writes=rs,
                      key=f"hload{slot}")
            if is_c:
                P.dma("sp", ybuf[slot], yT[:, :, b * T:(b + 1) * T].rearrange("m p t -> p m t"), reads=R_y[b], writes=[R_yb[slot]],
                      key=f"yload{slot}")

        def wout_m(b, slot, m):
            s = 0 if b > 0 else 1
            if True:
                bank = 6 + (m % 2)
                ps = PS[bank]
                def emit(e, m=m, ps=ps):
                    for kc in range(KC):
                        ins = e.matmul(ps[:, 0:T], wo[:, kc, m * 128:(m + 1) * 128], ybuf[slot][:, kc, :],
                                       start=(kc == 0), stop=(kc == KC - 1))
                    return ins
                P.op("pe", emit, reads=R_wo + [R_yb[slot]], writes=[R_ps[bank]])
                P.op("dve", lambda e, m=m, ps=ps, s=s: e.scalar_tensor_tensor(
                    hbuf[slot][:, m, :], ps[:, 0:T], GV[:, l, s, 1, m:m + 1], hbuf[slot][:, m, :], ALU.mult, ALU.add),
                    reads=[R_ps[bank], RM(), R_hbuf[slot][m]], writes=[R_hbuf[slot][m]])

        def pre_steps(b, slot, ns):
            s = 0 if b > 0 else 1
            steps = pos_steps(b, slot)
            if is_c:
                for m in range(KC):
                    steps.append(lambda m=m: wout_m(b, slot, m))
            steps += norm_steps(hbuf[slot], R_hbuf[slot], nT[ns], R_nT[ns], sq, R_sq, rstd, R_rstd, tmp, R_tmp, 0,
                                lambda m: AV[:, l, s, which, m:m + 1], lambda m: SHIFT(l, s, which, m))
            return steps

        def gu(b, slot, j, cnt):
            bank = 1 + (cnt % 3)
            ps = PS[bank]
            def emit(e, j=j, ps=ps):
                for kc in range(KC):
                    e.matmul(ps[:, 0:T], wgu[:, kc, j * 128:(j + 1) * 128], nT[slot][:, kc, :],
                             start=(kc == 0), stop=(kc == KC - 1))
                for kc in range(KC):
                    ins = e.matmul(ps[:, T:2 * T], wgu[:, kc, FF + j * 128:FF + (j + 1) * 128], nT[slot][:, kc, :],
                                   start=(kc == 0), stop=(kc == KC - 1))
                return ins
            P.op("pe", emit, reads=R_wgu_g[jgroup[j]] + R_nT[slot], writes=[R_ps[bank]])
            si = cnt % 3
            P.op("act", lambda e, ps=ps, si=si: e.activation(sg[si], ps[:, 0:T], AF.Silu),
                 reads=[R_ps[bank]], writes=[R_sg[si]])
            P.op("dve", lambda e, ps=ps, si=si, j=j: e.tensor_tensor(hid[:, j, :], sg[si], ps[:, T:2 * T], ALU.mult),
                 reads=[R_sg[si], R_ps[bank]], writes=[R_hid[j]])

        def down(b, slot, m):
            s = 0 if b > 0 else 1
            bank = 4 + (m % 2)
            ps = PS[bank]
            def emit(e, m=m, ps=ps):
                for j in range(JC):
                    ins = e.matmul(ps[:, 0:T], wdn[:, j, m * 128:(m + 1) * 128], hid[:, j, :],
                                   start=(j == 0), stop=(j == JC - 1))
                return ins
            P.op("pe", emit, reads=R_wdn + R_hid, writes=[R_ps[bank]])
            P.op("dve", lambda e, m=m, ps=ps, s=s: e.scalar_tensor_tensor(
                hbuf[slot][:, m, :], ps[:, 0:T], GV[:, l, s, which, m:m + 1], hbuf[slot][:, m, :], ALU.mult, ALU.add),
                reads=[R_ps[bank], RM(), R_hbuf[slot][m]], writes=[R_hbuf[slot][m]])

        def store(b, slot):
            if last:
                fng = S("fng")
                norm_block(hbuf[slot], R_hbuf[slot], hbuf[slot], R_hbuf[slot], sq, R_sq, rstd, R_rstd, tmp,
                           R_tmp, 0, lambda m: fng[:, m:m + 1], None)
                o = P.dma("sp", outTb[b - 1], hbuf[slot].rearrange("p a b -> p (a b)"), reads=R_hbuf[slot],
                          key=f"hstore{slot}")
                out_stores.append(o)
            else:
                P.dma("sp", hTb[b], hbuf[slot].rearrange("p a b -> p (a b)"), reads=R_hbuf[slot], writes=[R_h[b]],
                      key=f"hstore{slot}")

        nblk = len(blocks)
        for i in range(min(NH, nblk)):
            load(blocks[i], i)
        load_weights()
        pc_parts = []
        if not is_c:
            pc_parts = make_precast(idx)
            if idx == 0:
                pc_parts = pc_parts + make_precast(1)
        for st in pre_steps(blocks[0], 0, 0):
            st()
        cnt = 0
        for i, b in enumerate(blocks):
            slot = i % NH
            ns = i % 2
            if i >= 1:
                for _ in range(2 if len(pc_parts) > (nblk - 1 - i) else 1):
                    if pc_parts:
                        pc_parts.pop(0)()
            pend = pre_steps(blocks[i + 1], (i + 1) % NH, 1 - ns) if i + 1 < nblk else []
            j0 = 8
            per_j = 2 if is_c else 1
            for j in range(JC):
                gu(b, ns, j, cnt)
                cnt += 1
                if j >= j0:
                    for _ in range(per_j):
                        if pend:
                            pend.pop(0)()
            while pend:
                pend.pop(0)()
            for m in range(KC):
                down(b, slot, m)
            store(b, slot)
            if i + NH < nblk:
                load(blocks[i + NH], slot)

    def mixin_phase(l):
        cur["l"] = l
        A.off = persist_off
        win = A.alloc([KC, 1792], BF16)
        R_win = [Res(f"win{kc}") for kc in range(KC)]
        for kc in range(KC):
            P.dma("pool", win[:, kc, :], w_in[l, kc * 128:(kc + 1) * 128, :], writes=[R_win[kc]], key="win")
        hbuf = [A.alloc([KC, T], F32) for _ in range(3)]
        R_hbuf = [[Res(f"a2h{i}_{m}") for m in range(KC)] for i in range(3)]
        nT = [A.alloc([KC, T], BF16) for _ in range(3)]
        R_nT = [[Res(f"a2n{i}_{m}") for m in range(KC)] for i in range(3)]
        sq = [A.alloc([T], BF16) for _ in range(8)]
        R_sq = [Res(f"a2sq{i}") for i in range(8)]
        tmp = [A.alloc([T], F32) for _ in range(2)]
        R_tmp = [Res("a2t0"), Res("a2t1")]
        rstd = A.alloc([T], F32)
        R_rstd = Res("a2rstd")
        GB = 4
        zst = [A.alloc([12, GB * T], BF16) for _ in range(2)]
        R_zst = [[Res(f"zst{i}_{c}") for c in range(12)] for i in range(2)]
        zvst = [A.alloc([2 * GB, 256], BF16) for _ in range(2)]
        R_zvst = [[Res(f"zvst{i}_{c}") for c in range(2 * GB)] for i in range(2)]
        fm_cols = [0, 1, 2, 3, 4, 5, 6, 7, 8, 9, 12, 13]
        cnt = 0

        def load(b, slot):
            P.dma("sp", hbuf[slot].rearrange("p a b -> p (a b)"), hTb[b], reads=[R_h[b]], writes=R_hbuf[slot],
                  key=f"a2load{slot}")

        def norm(b, slot):
            s = 0 if b > 0 else 1
            return norm_steps(hbuf[slot], R_hbuf[slot], nT[slot], R_nT[slot], sq, R_sq, rstd, R_rstd, tmp, R_tmp, 0,
                              lambda m: AV[:, l, s, 1, m:m + 1], lambda m: SHIFT(l, s, 1, m), mode="split")

        load(0, 0)
        load(1, 1)
        load(2, 2)
        for st in norm(0, 0):
            st()
        for st in norm(1, 1):
            st()
        cntbox = [0]

        mod_next = mod_layer_steps(l + 1, 7) if l + 1 < DEPTH else []

        def do_block(b, slot, zs):
            if b >= 1 and mod_next:
                mod_next.pop(0)()
            gi = b % GB
            g0 = (b // GB) * GB
            gn = min(GB, NB - g0)
            co = gi * T
            pend = norm(b + 2, (b + 2) % 3) if b + 2 < NB else []
            for ci, c in enumerate(fm_cols):
                bank = 1 + (cntbox[0] % 4)
                cntbox[0] += 1
                ps = PS[bank]
                def emit(e, c=c, ps=ps):
                    for kc in range(KC):
                        ins = e.matmul(ps[:, 0:T], win[:, kc, c * 128:(c + 1) * 128], nT[slot][:, kc, :],
                                       start=(kc == 0), stop=(kc == KC - 1))
                    return ins
                P.op("pe", emit, reads=R_win + R_nT[slot], writes=[R_ps[bank]])
                if 4 <= ci < 10:
                    P.op("act", lambda e, ps=ps, ci=ci: e.activation(zst[zs][:, ci, co:co + T], ps[:, 0:T], AF.Gelu_apprx_tanh),
                         reads=[R_ps[bank]], writes=[R_zst[zs][ci]])
                else:
                    P.op("dve", lambda e, ps=ps, ci=ci: e.tensor_copy(zst[zs][:, ci, co:co + T], ps[:, 0:T]),
                         reads=[R_ps[bank]], writes=[R_zst[zs][ci]])
                for _ in range(2 if ci >= 7 else 1):
                    if pend:
                        pend.pop(0)()
            for hf in range(2):
                bank = 5 + hf
                ps = PS[bank]
                def emit(e, hf=hf, ps=ps):
                    for kc in range(KC):
                        ins = e.matmul(ps[:, 0:256], nT[slot][:, kc, hf * 128:(hf + 1) * 128], win[:, kc, 1280:1536],
                                       start=(kc == 0), stop=(kc == KC - 1))
                    return ins
                P.op("pe", emit, reads=R_win + R_nT[slot], writes=[R_ps[bank]])
                P.op("act", lambda e, ps=ps, hf=hf: e.activation(zvst[zs][:, 2 * gi + hf, :], ps[:, 0:256], AF.Gelu_apprx_tanh),
                     reads=[R_ps[bank]], writes=[R_zvst[zs][2 * gi + hf]])
            while pend:
                pend.pop(0)()
            if gi == gn - 1:
                t0 = g0 * T
                nn = gn * T
                P.dma("sp", zxa[:, :, t0:t0 + nn].rearrange("c p t -> p c t"), zst[zs][:, 0:4, 0:nn],
                      reads=R_zst[zs][0:4], writes=R_zxa, key=f"zs0_{zs}")
                P.dma("sp", zga[:, :, t0:t0 + nn].rearrange("c p t -> p c t"), zst[zs][:, 4:8, 0:nn],
                      reads=R_zst[zs][4:8], writes=R_zga, key=f"zs1_{zs}")
                P.dma("sp", zu[:, :, t0:t0 + nn].rearrange("c p t -> p c t"), zst[zs][:, 8:10, 0:nn],
                      reads=R_zst[zs][8:10], writes=R_zu, key=f"zs2_{zs}")
                P.dma("sp", zf[:, :, t0:t0 + nn].rearrange("c p t -> p c t"), zst[zs][:, 10:12, 0:nn],
                      reads=R_zst[zs][10:12], writes=R_zf, key=f"zs3_{zs}")
                P.dma("sp", zv[:, 2 * g0:2 * g0 + 2 * gn, :], zvst[zs][:, 0:2 * gn, :],
                      reads=R_zvst[zs][0:2 * gn], writes=[R_zv], key=f"zs4_{zs}")
            if b + 3 < NB:
                load(b + 3, slot)

        for b in range(NB):
            do_block(b, b % 3, (b // GB) % 2)
        while mod_next:
            mod_next.pop(0)()

    def lru_phase(l):
        A.off = persist_off
        last = l == DEPTH - 1
        wg = [A.alloc([2, 2, 128], BF16) for _ in range(2)]
        wgf = A.alloc([2, 2, 128], F32)
        R_wg = [Res("wg0"), Res("wg1")]
        R_wgf = [Res(f"wgf{i}") for i in range(8)]
        ident = A.alloc([128], BF16)
        R_ident = Res("ident")
        P.dma("sp", ident, DG_d[:, 0:128], writes=[R_ident])
        dk = [A.alloc([4, 128], BF16) for _ in range(2)]
        R_dk = [Res("dk0"), Res("dk1")]
        padc = [A.alloc([2 + 256 + 1], BF16) for _ in range(2)]
        padl = [A.alloc([2 + L + 1], BF16) for _ in range(2)]
        R_padc = [Res("padc0"), Res("padc1")]
        R_padl = [Res("padl0"), Res("padl1")]
        xc = A.alloc([NTOK], F32)
        xcb = A.alloc([NTOK], BF16)
        Ab2 = [A.alloc([NTOK], F32) for _ in range(2)]
        Sb2 = [A.alloc([NTOK], F32) for _ in range(2)]
        Bb2 = [A.alloc([NTOK], F32) for _ in range(2)]
        Hf = A.alloc([NTOK], F32)
        Hb = Sb2[1]
        gga = A.alloc([NTOK], BF16)
        yst = A.alloc([NTOK], BF16)
        thr = [A.alloc([512], F32) for _ in range(2)]
        thi = [A.alloc([512], F32) for _ in range(2)]
        R_thr = [Res("thr0"), Res("thr1")]
        R_thi = [Res("thi0"), Res("thi1")]
        segs = [(0, 256)] + [(256 + 512 * k, 512) for k in range(8)]
        R_xc = [Res(f"xc{i}") for i in range(9)]
        R_xcb = [Res(f"xcb{i}") for i in range(9)]
        R_A2 = [[Res(f"A{q}_{i}") for i in range(9)] for q in range(2)]
        R_S2 = [[Res(f"S{q}_{i}") for i in range(9)] for q in range(2)]
        R_B2 = [[Res(f"B{q}_{i}") for i in range(9)] for q in range(2)]
        R_Hf = Res("Hf")
        R_gga = Res("gga")
        R_yst = Res("yst")
        cw = S("convw").rearrange("p (l c k) -> p l c k", l=2, c=4)
        cb = S("convb").rearrange("p (l c) -> p l c", l=2)
        for q in range(2):
            P.op("pool", lambda e, q=q: e.memset(padc[q], 0.0), writes=[R_padc[q]])
            P.op("pool", lambda e, q=q: e.memset(padl[q], 0.0), writes=[R_padl[q]])
        P.op("pool", lambda e: e.memset(wgf.rearrange("p a b c -> p (a b c)"), 0.0), writes=R_wgf)
        cntbox = [0]

        def prefetch(cc):
            q = cc % 2
            for d in range(2):
                for g in range(2):
                    for hh in range(2):
                        P.dma("sp", wgf[hh * 64:(hh + 1) * 64, d, g, hh * 64:(hh + 1) * 64], w_gates[l, d, g, 2 * cc + hh],
                              writes=[R_wgf[d * 4 + g * 2 + hh]], key="wgf")
            P.op("pool", lambda e: e.tensor_copy(wg[q].rearrange("p a b c -> p (a b c)"),
                                                 wgf.rearrange("p a b c -> p (a b c)")),
                 reads=R_wgf, writes=[R_wg[q]])
            P.dma("sp", padc[q][:, 2:258], zxa[cc, :, 0:256], reads=[R_zxa[cc]], writes=[R_padc[q]], key=f"padc{q}")
            P.dma("sp", padl[q][:, 2:2 + L], zxa[cc, :, 256:NTOK], reads=[R_zxa[cc]], writes=[R_padl[q]], key=f"padl{q}")
            for k in range(4):
                P.op("pool", lambda e, k=k: e.tensor_scalar(dk[q][:, k, :], ident, cw[:, l, cc, k:k + 1], None, ALU.mult),
                     reads=[R_ident, R_sm], writes=[R_dk[q]])

        def do_cc(cc):
            q = cc % 2
            if not last:
                P.dma("sp", gga, zga[cc], reads=[R_zga[cc]], writes=[R_gga], key="gga")
            else:
                P.dma("sp", gga[:, 256:NTOK], zga[cc, :, 256:NTOK], reads=[R_zga[cc]], writes=[R_gga], key="gga")
            for si, (t0, n) in enumerate(segs):
                if si == 0:
                    src = lambda k, t0=t0, n=n: padc[q][:, k:k + n]
                else:
                    src = lambda k, t0=t0, n=n: padl[q][:, t0 - 256 + k:t0 - 256 + k + n]
                bank = 6 + (si % 2)
                ps = PS[bank]
                def emit(e, src=src, n=n, ps=ps):
                    for k in range(4):
                        ins = e.matmul(ps[:, 0:n], dk[q][:, k, :], src(k), start=(k == 0), stop=(k == 3))
                    return ins
                P.op("pe", emit, reads=[R_dk[q], R_padc[q] if si == 0 else R_padl[q]], writes=[R_ps[bank]])
                P.op("dve", lambda e, t0=t0, n=n, ps=ps: e.tensor_scalar(xc[:, t0:t0 + n], ps[:, 0:n], cb[:, l, cc:cc + 1], None,
                                                                        ALU.add),
                     reads=[R_ps[bank], R_sm], writes=[R_xc[si]])
                P.op("pool", lambda e, t0=t0, n=n: e.tensor_copy(xcb[:, t0:t0 + n], xc[:, t0:t0 + n]),
                     reads=[R_xc[si]], writes=[R_xcb[si]])

            if cc + 1 < 4:
                prefetch(cc + 1)

            def do_dir(d):
                Ab, Sb, Bb = Ab2[d], Sb2[d], Bb2[d]
                R_A, R_S, R_B = R_A2[d], R_S2[d], R_B2[d]
                for si in range(9):
                    t0, n = segs[si]
                    bank = 2 * (cntbox[0] % 3)
                    b2 = cntbox[0] % 2
                    cntbox[0] += 1
                    psr, psi = PS[bank], PS[bank + 1]
                    P.op("pe", lambda e, psr=psr, t0=t0, n=n: e.matmul(psr[:, 0:n], wg[q][:, d, 0, :], xcb[:, t0:t0 + n],
                                                                      start=True, stop=True),
                         reads=[R_wg[q], R_xcb[si]], writes=[R_ps[bank]])
                    P.op("pe", lambda e, psi=psi, t0=t0, n=n: e.matmul(psi[:, 0:n], wg[q][:, d, 1, :], xcb[:, t0:t0 + n],
                                                                      start=True, stop=True),
                         reads=[R_wg[q], R_xcb[si]], writes=[R_ps[bank + 1]])
                    P.op("act", lambda e, psr=psr, n=n, b2=b2: e.activation(
                        thr[b2][:, 0:n], psr[:, 0:n], AF.Tanh, bias=HB[:, l, d, 0, cc:cc + 1], scale=0.5),
                        reads=[R_ps[bank], R_const], writes=[R_thr[b2]])
                    P.op("act", lambda e, psi=psi, n=n, b2=b2: e.activation(
                        thi[b2][:, 0:n], psi[:, 0:n], AF.Tanh, bias=HB[:, l, d, 1, cc:cc + 1], scale=0.5),
                        reads=[R_ps[bank + 1], R_const], writes=[R_thi[b2]])
                    P.op("act", lambda e, t0=t0, n=n, b2=b2: e.activation(
                        Ab[:, t0:t0 + n], thr[b2][:, 0:n], AF.Exp, bias=CH2[:, l, d, cc:cc + 1], scale=CH2[:, l, d, cc:cc + 1]),
                        reads=[R_thr[b2], R_const], writes=[R_A[si]])
                    P.op("dve", lambda e, t0=t0, n=n: e.scalar_tensor_tensor(
                        Sb[:, t0:t0 + n], Ab[:, t0:t0 + n], -1.0, Ab[:, t0:t0 + n], ALU.mult, ALU.mult),
                        reads=[R_A[si]], writes=[R_S[si]])
                    P.op("dve", lambda e, t0=t0, n=n, b2=b2: e.scalar_tensor_tensor(
                        Bb[:, t0:t0 + n], thi[b2][:, 0:n], 1.0, xc[:, t0:t0 + n], ALU.add, ALU.mult),
                        reads=[R_thi[b2], R_xc[si]], writes=[R_B[si]])
                for (a0, a1, rs) in [(0, 2304, R_S[0:5]), (2304, NTOK, R_S[5:9])]:
                    P.op("act", lambda e, a0=a0, a1=a1: e.activation(Sb[:, a0:a1], Sb[:, a0:a1], AF.Sqrt, bias=0.25, scale=0.25),
                         reads=rs, writes=rs)
                for si in range(9):
                    t0, n = segs[si]
                    P.op("dve", lambda e, t0=t0, n=n: e.tensor_tensor(Bb[:, t0:t0 + n], Bb[:, t0:t0 + n], Sb[:, t0:t0 + n], ALU.mult),
                         reads=[R_S[si], R_B[si]], writes=[R_B[si]])
                if d == 0:
                    P.op("dve", lambda e: e.tensor_tensor_scan(Hf[:, 0:256], Ab[:, 0:256], Bb[:, 0:256], 0.0, ALU.mult, ALU.add),
                         reads=[R_A[0], R_B[0]], writes=[R_Hf])
                    P.op("dve", lambda e: e.tensor_tensor_scan(Hf[:, 256:NTOK], Ab[:, 256:NTOK], Bb[:, 256:NTOK], Hf[:, 255:256],
                                                              ALU.mult, ALU.add),
                         reads=R_A[1:] + R_B[1:] + [R_Hf], writes=[R_Hf])
                else:
                    P.op("dve", lambda e: e.tensor_tensor_scan(Hb[:, 0:256][:, ::-1], Ab[:, 0:256][:, ::-1],
                                                              Bb[:, 0:256][:, ::-1], 0.0, ALU.mult, ALU.add),
                         reads=[R_A[0], R_B[0]], writes=[R_S[0]])
                    P.op("dve", lambda e: e.tensor_tensor_scan(Hb[:, 256:NTOK][:, ::-1], Ab[:, 256:NTOK][:, ::-1],
                                                              Bb[:, 256:NTOK][:, ::-1], Hb[:, 0:1], ALU.mult, ALU.add),
                         reads=R_A[1:] + R_B[1:] + [R_S[0]], writes=R_S[1:])

            for d in range(2):
                do_dir(d)
            c0 = 256 if last else 0
            for (a0, a1) in [(c0, 2304), (2304, NTOK)]:
                P.op("dve", lambda e, a0=a0, a1=a1: e.tensor_tensor(Hf[:, a0:a1], Hf[:, a0:a1], Hb[:, a0:a1], ALU.add),
                     reads=[R_Hf] + R_S2[1], writes=[R_Hf])
                P.op("pool", lambda e, a0=a0, a1=a1: e.tensor_tensor(yst[:, a0:a1], Hf[:, a0:a1], gga[:, a0:a1], ALU.mult),
                     reads=[R_Hf, R_gga], writes=[R_yst])
            b0 = 1 if last else 0
            P.dma("sp", yT[cc, :, b0 * 256:NTOK],
                  yst[:, b0 * 256:NTOK], reads=[R_yst],
                  writes=[R_y[b][cc] for b in range(b0, NB)], key="ysta")

        prefetch(0)
        for cc in range(4):
            do_cc(cc)

    def gmlp_phase(l, reset=True):
        if reset:
            A.off = persist_off
        last = l == DEPTH - 1
        wsT = A.alloc([4, 128], BF16)
        R_wsl = [Res(f"wsT{g}") for g in range(4)]
        for g in range(4):
            P.dma("pool", wsT[:, g, :], wsT_d[l, g], writes=[R_wsl[g]], key="wsT")
        gu_ = A.alloc([2, NTOK], BF16)
        R_gul = [Res("gmlp_u0"), Res("gmlp_u1")]
        for j in range(2):
            P.dma("sp", gu_[:, j, :], zu[j], reads=[R_zu[j]], writes=[R_gul[j]], key="gmu")
        vb = [A.alloc([4, 256], BF16) for _ in range(2)]
        R_vb = [Res("vb0"), Res("vb1")]
        tm = [A.alloc([512], F32) for _ in range(2)]
        R_tm = [Res("gtm0"), Res("gtm1")]
        yb = [A.alloc([2, 512], BF16) for _ in range(2)]
        R_yb = [[Res(f"gyb{i}_{j}") for j in range(2)] for i in range(2)]
        bsbc = S("bsbc").rearrange("p (l j n) -> p l j n", l=2, j=2)
        batches = [] if last else [(0, 2)]
        batches += [(2 + 4 * k, 4) for k in range(8)]
        for bi, (c0, nch) in enumerate(batches):
            sl = bi % 2
            n = nch * 128
            t0 = c0 * 128
            P.dma("sp", vb[sl][:, 0:nch, :], zv[:, c0:c0 + nch, :], reads=[R_zv], writes=[R_vb[sl]],
                  key=f"vb{sl}")
            for j in range(2):
                bank = (2 * bi + j) % 4
                ps = PS[bank]
                def emit(e, j=j, ps=ps, nch=nch, sl=sl):
                    for c4 in range(nch):
                        for gg in range(2):
                            g = 2 * j + gg
                            ins = e.matmul(ps[gg * 64:(gg + 1) * 64, c4 * 128:(c4 + 1) * 128],
                                           vb[sl][:, c4, g * 64:(g + 1) * 64], wsT[:, g, :], start=True, stop=True)
                    return ins
                P.op("pe", emit, reads=[R_vb[sl]] + R_wsl, writes=[R_ps[bank]])
                tj = (2 * bi + j) % 2
                P.op("dve", lambda e, ps=ps, j=j, n=n, nch=nch, tj=tj: e.tensor_tensor(
                    tm[tj][:, 0:n].rearrange("p (c q) -> p c q", q=128), ps[:, 0:n].rearrange("p (c q) -> p c q", q=128),
                    bsbc[:, l, j, :].unsqueeze(1).to_broadcast([128, nch, 128]), ALU.add),
                    reads=[R_ps[bank], R_sm], writes=[R_tm[tj]])
                P.op("dve", lambda e, j=j, n=n, t0=t0, tj=tj, sl=sl: e.tensor_tensor(
                    yb[sl][:, j, 0:n], tm[tj][:, 0:n], gu_[:, j, t0:t0 + n], ALU.mult),
                    reads=[R_tm[tj], R_gul[j]], writes=[R_yb[sl][j]])
            bb0 = t0 // T
            nb_ = n // T
            for j in range(2):
                P.dma("sp", yT[4 + j, :, t0:t0 + n],
                      yb[sl][:, j, 0:n], reads=[R_yb[sl][j]],
                      writes=[R_y[b][4 + j] for b in range(bb0, bb0 + nb_)], key=f"gyst{sl}{j}")

    def fourier_phase(l, mid=None):
        A.off = persist_off
        last = l == DEPTH - 1
        cs64 = A.alloc([384], BF16)
        dg = A.alloc([4, 128], BF16)
        R_cs = Res("cs64")
        R_dg = Res("dg")
        P.dma("sp", cs64, CS_d, writes=[R_cs])
        P.dma("sp", dg.rearrange("p a b -> p (a b)"), DG_d, writes=[R_dg])
        c0 = A.alloc([2, 32, 512], BF16)
        R_c0 = [[Res(f"c0_{cs}_{h}") for h in range(2)] for cs in range(2)]
        for cs in range(2):
            for h in range(2):
                P.dma("sp", c0[:, cs, 16 * h:16 * (h + 1), :].rearrange("p a b -> p (a b)"),
                      C0_d[cs, :, 16 * h * 512:16 * (h + 1) * 512], writes=[R_c0[cs][h]], key="c0")
        fT = A.alloc([2, NTOK], BF16)
        R_fTl = [Res("fT0"), Res("fT1")]
        for j in range(2):
            P.dma("sp", fT[:, j, :], zf[j], reads=[R_zf[j]], writes=[R_fTl[j]], key="fT")
        FCS = A.alloc([32, 2, 3, 128], BF16)
        R_F = [[Res(f"FCS{c}_{j}") for j in range(2)] for c in range(32)]
        UV = A.alloc([8, 2, 2, 512], BF16)
        R_UV = [[[Res(f"UV{r}_{j}_{u}") for u in range(2)] for j in range(2)] for r in range(8)]
        ost = [A.alloc([512], BF16) for _ in range(2)]
        R_ost = [Res("fo0"), Res("fo1")]
        ev = [0]

        def evac(dst, src, reads, writes):
            if ev[0] % 2 == 0:
                P.op("act", lambda e: e.activation(dst, src, AF.Copy), reads=reads, writes=writes)
            else:
                P.op("dve", lambda e: e.tensor_copy(dst, src), reads=reads, writes=writes)
            ev[0] += 1

        oc = [0]
        if not last:
            FCSc = A.alloc([2, 2, 3, 128], BF16)
            R_Fc = Res("FCSc")
            clc = A.alloc([2, 2, 256], BF16)
            R_clc = Res("clc")
            P.dma("sp", clc.rearrange("p a b c -> p (a b c)"), CLc_d, writes=[R_clc])
            for i in range(2):
                for j in range(2):
                    bank = (2 * i + j) % 2
                    ps = PS[bank]
                    P.op("pe", lambda e, i=i, j=j, ps=ps: e.matmul(ps[:, 0:384], fT[:, j, i * 128:(i + 1) * 128], cs64,
                                                                 start=True, stop=True),
                         reads=R_fTl + [R_cs], writes=[R_ps[bank]])
                    evac(FCSc[:, i, j].rearrange("p a b -> p (a b)"), ps[:, 0:384], [R_ps[bank]], [R_Fc])
            for j in range(2):
                bank = 2 + j
                ps = PS[bank]
                def emit(e, j=j, ps=ps):
                    k = 0
                    for cs in range(2):
                        for i in range(2):
                            ins = e.matmul(ps[:, 0:256], FCSc[:, i, j, cs, :], clc[:, cs, i, :], start=(k == 0), stop=(k == 3))
                            k += 1
                    return ins
                P.op("pe", emit, reads=[R_Fc, R_clc], writes=[R_ps[bank]])
                osl = oc[0] % 2
                oc[0] += 1
                evac(ost[osl][:, 0:256], ps[:, 0:256], [R_ps[bank]], [R_ost[osl]])
                P.dma("sp", yT[6 + j, :, 0:256], ost[osl][:, 0:256], reads=[R_ost[osl]],
                      writes=[R_y[0][6 + j]], key=f"fost{osl}")
        cnt = 0
        for c in range(32):
            r, qc = c // 4, c % 4
            tb = 256 + r + 1024 * qc
            for j in range(2):
                bank = cnt % 2
                cnt += 1
                ps = PS[bank]
                P.op("pe", lambda e, j=j, ps=ps, tb=tb: e.matmul(ps[:, 0:384], fT[:, j, tb:tb + 1017:8], cs64, start=True, stop=True),
                     reads=R_fTl + [R_cs], writes=[R_ps[bank]])
                evac(FCS[:, c, j].rearrange("p a b -> p (a b)"), ps[:, 0:384], [R_ps[bank]], [R_F[c][j]])
        if mid is not None:
            mid()
        cnt = 0
        for r in range(8):
            for j in range(2):
                for u in range(2):
                    bank = 2 + cnt % 4
                    cnt += 1
                    ps = PS[bank]
                    def emit(e, r=r, j=j, u=u, ps=ps):
                        k = 0
                        for qc in range(4):
                            c = 4 * r + qc
                            a0, a1 = (0, 1) if u == 0 else (1, 2)
                            e.matmul(ps[:, 0:512], FCS[:, c, j, a0, :], c0[:, 0, c, :], start=(k == 0), stop=False)
                            ins = e.matmul(ps[:, 0:512], FCS[:, c, j, a1, :], c0[:, 1, c, :], start=False, stop=(k == 3))
                            k += 1
                        return ins
                    P.op("pe", emit, reads=[R_F[4 * r + qc][j] for qc in range(4)] + R_c0[0] + R_c0[1],
                         writes=[R_ps[bank]])
                    evac(UV[:, r, j, u, :], ps[:, 0:512], [R_ps[bank]], [R_UV[r][j][u]])
        def cidx(v):
            if abs(v) < 1e-9:
                return None
            if abs(v - 1.0) < 1e-6:
                return 0
            if abs(v + 1.0) < 1e-6:
                return 1
            return 2 if v > 0 else 3
        cnt = 0
        for kb in range(8):
            for j in range(2):
                terms = []
                for r in range(8):
                    ang = 2.0 * np.pi * ((r * kb) % 8) / 8.0
                    iu = cidx(np.cos(ang))
                    iv = cidx(-np.sin(ang))
                    if iu is not None:
                        terms.append((iu, r, 0))
                    if iv is not None:
                        terms.append((iv, r, 1))
                bank = 6 + cnt % 2
                cnt += 1
                ps = PS[bank]
                def emit(e, j=j, ps=ps, terms=terms):
                    nt = len(terms)
                    for k, (ix, r, u) in enumerate(terms):
                        ins = e.matmul(ps[:, 0:512], dg[:, ix, :], UV[:, r, j, u, :], start=(k == 0), stop=(k == nt - 1))
                    return ins
                P.op("pe", emit, reads=[R_dg] + [R_UV[r][j][u] for (_, r, u) in terms], writes=[R_ps[bank]])
                osl = oc[0] % 2
                oc[0] += 1
                evac(ost[osl], ps[:, 0:512], [R_ps[bank]], [R_ost[osl]])
                bb0 = 1 + 2 * kb
                P.dma("sp", yT[6 + j, :, 256 + kb * 512:256 + (kb + 1) * 512],
                      ost[osl], reads=[R_ost[osl]],
                      writes=[R_y[bb0][6 + j], R_y[bb0 + 1][6 + j]], key=f"fost{osl}")

    def final_phase():
        A.off = persist_off
        hbuf = [A.alloc([KC, T], F32) for _ in range(3)]
        R_hbuf = [[Res(f"fh{i}_{m}") for m in range(KC)] for i in range(3)]
        obuf = [A.alloc([KC, T], F32) for _ in range(3)]
        R_ob = [[Res(f"fo{i}_{m}") for m in range(KC)] for i in range(3)]
        sq = [A.alloc([T], BF16) for _ in range(2)]
        R_sq = [Res("fsq0"), Res("fsq1")]
        rstd = [A.alloc([T], F32) for _ in range(2)]
        R_rstd = [Res("frs0"), Res("frs1")]
        fng = S("fng")
        for i in range(min(3, NB - 1)):
            P.dma("sp", hbuf[i].rearrange("p a b -> p (a b)"), hTb[1 + i], reads=[R_h[1 + i]], writes=R_hbuf[i], key=f"fl{i}")
        for i in range(NB - 1):
            b = 1 + i
            sl = i % 3
            for st in norm_steps(hbuf[sl], R_hbuf[sl], obuf[sl], R_ob[sl], sq, R_sq, rstd[i % 2], R_rstd[i % 2], None, None,
                                 i % 2, lambda m: fng[:, m:m + 1], None, mode="mixed"):
                st()
            o = P.dma("sp", outTb[b - 1], obuf[sl].rearrange("p a b -> p (a b)"), reads=R_ob[sl], key=f"fs{sl}")
            out_stores.append(o)
            if i + 3 < NB - 1:
                P.dma("sp", hbuf[sl].rearrange("p a b -> p (a b)"), hTb[b + 3], reads=[R_h[b + 3]], writes=R_hbuf[sl],
                      key=f"fl{sl}")

    phases = []
    phases.append(("prologue", prologue))
    for l in range(DEPTH):
        phases.append((f"ffn1_{l}", lambda l=l: ffn_phase(l, 0)))
        phases.append((f"mixin_{l}", lambda l=l: mixin_phase(l)))
        phases.append((f"lru_{l}", lambda l=l: lru_phase(l)))
        phases.append((f"fourier_{l}", lambda l=l: fourier_phase(l, mid=lambda: gmlp_phase(l, reset=False))))
        phases.append((f"ffn2_{l}", lambda l=l: ffn_phase(l, 2)))
    phases.append(("final", final_phase))
    for name, fn in phases:
        P.fence()
        fn()
        if stop_after is not None and name == stop_after:
            break
    finals = list(out_stores)
    for r in R_h + R_zxa + R_zga + R_zu + R_zf + [R_zv] + [x for row in R_y for x in row]:
        if r.w is not None and r.w.is_dma:
            finals.append(r.w)
    P.wait_all("sp", finals)
    P.emit()
    return nc, P


_CONST_CACHE = {}


def prepare_inputs(inputs):
    if "c" not in _CONST_CACHE:
        _CONST_CACHE["c"] = build_consts()
    C0, DG, CLc, CS = _CONST_CACHE["c"]
    f32 = lambda a: np.ascontiguousarray(np.asarray(a, np.float32))
    x = f32(inputs["x"])
    ctx = f32(inputs["ctx"])
    shared = {
        "w_mod": f32(inputs["w_mod"]),
        "ffn_w_gu": f32(inputs["ffn_w_gu"]),
        "ffn_w_down": f32(inputs["ffn_w_down"]),
        "w_in": f32(inputs["w_in"]),
        "w_out": f32(inputs["w_out"]),
        "lru_w_gates": f32(inputs["lru_w_gates"]),
        "gmlp_wsT": np.ascontiguousarray(np.transpose(f32(inputs["gmlp_ws"]), (0, 1, 3, 2))),
        "C0": C0, "DG": DG, "CLc": CLc, "CS": CS,
    }
    in_maps = []
    for b in range(8):
        xb = x[b].reshape(16, T, KC, 128).transpose(0, 3, 2, 1).reshape(16, 128, KC * T)
        cb = ctx[b].reshape(T, KC, 128).transpose(2, 1, 0).reshape(128, KC * T)
        m = dict(shared)
        m["xTb"] = np.ascontiguousarray(xb)
        m["ctxTb"] = np.ascontiguousarray(cb)
        m["smalls"] = build_smalls(b, inputs)
        in_maps.append(m)
    return in_maps


def kernel(**inputs):
    in_maps = prepare_inputs(inputs)
    nc, _ = build()
    res = run_bass_kernel_spmd(nc, in_maps, core_ids=list(range(8)))
    out = np.empty((8, L, D), np.float32)
    for b in range(8):
        o = np.asarray(res.results[b]["outTb"], np.float32).reshape(16, 128, KC, T)
        out[b] = o.transpose(0, 3, 2, 1).reshape(L, D)
    return out
```

```python
import contextlib
import numpy as np
import ml_dtypes
import concourse.bass as bass
import concourse.mybir as mybir
from concourse.bass_utils import run_bass_kernel_spmd

F32 = mybir.dt.float32
BF16 = mybir.dt.bfloat16
ALU = mybir.AluOpType
AF = mybir.ActivationFunctionType

D = 1024
KC = 8
FF = 2816
JC = 22
T = 256
NB = 17
NTOK = 4352
L = 4096
DEPTH = 2
EPS = 1e-6
SB_BASE = 16512
SB_END = 229344
ARENA_BYTES = SB_END - SB_BASE

SAME_ENGINE_SYNC = True


class Res:
    __slots__ = ("name", "w", "rs", "rd")

    def __init__(self, name):
        self.name = name
        self.w = None
        self.rs = {}
        self.rd = []


class Op:
    __slots__ = ("eng", "emit", "deps", "is_dma", "dsem", "needs_inc", "tok", "waits")

    def __init__(self, eng, emit, is_dma=False, dsem=None):
        self.eng = eng
        self.emit = emit
        self.deps = []
        self.is_dma = is_dma
        self.dsem = dsem
        self.needs_inc = False
        self.tok = None
        self.waits = []


class Prog:
    ENGS = ("pe", "act", "dve", "pool", "sp")

    def __init__(self, nc):
        self.nc = nc
        self.stack = contextlib.ExitStack()
        self.ops = []
        self.nsem = 0
        self.dsems = {}
        self.ntens = 0
        self.pending = {}
        self.last_c = {}
        self.dma_since = []

    def fence(self):
        deps = list(self.last_c.values()) + list(self.dma_since)
        self.dma_since = []
        for e in self.ENGS:
            self.pending[e] = list(self.pending.get(e, [])) + deps

    def sem(self, name):
        self.nsem += 1
        return self.stack.enter_context(self.nc.semaphore(f"{name}_{self.nsem}"))

    def dsem(self, key):
        if key not in self.dsems:
            self.dsems[key] = [self.sem("d"), 0]
        return self.dsems[key]

    def psum(self, shape, dtype=F32):
        self.ntens += 1
        return self.stack.enter_context(self.nc.psum_tensor(f"ps{self.ntens}", list(shape), dtype))

    def _track(self, op, reads, writes):
        deps = []
        for r in reads:
            if r.w is not None:
                deps.append(r.w)
        for w in writes:
            if w.w is not None and not (w.rs or w.rd):
                deps.append(w.w)
            deps.extend(w.rs.values())
            deps.extend(w.rd)
        if op.eng in self.pending:
            deps.extend(self.pending.pop(op.eng))
        seen = set()
        for d in deps:
            if id(d) in seen or d is op:
                continue
            seen.add(id(d))
            op.deps.append(d)
        if op.is_dma:
            self.dma_since.append(op)
        else:
            self.last_c[op.eng] = op
        for r in reads:
            if op.is_dma:
                r.rd.append(op)
            else:
                r.rs[op.eng] = op
        for w in writes:
            w.w = op
            w.rs = {}
            w.rd = []
        self.ops.append(op)
        return op

    def op(self, eng, emit, reads=(), writes=()):
        return self._track(Op(eng, emit), reads, writes)

    def dma(self, queue, out, in_, reads=(), writes=(), key=None):
        base = key if key is not None else (writes[0].name if writes else reads[0].name)
        ds = self.dsem(f"{base}@{queue}")
        o = Op(queue, lambda e, out=out, in_=in_: e.dma_start(out=out, in_=in_), is_dma=True, dsem=ds)
        return self._track(o, reads, writes)

    def wait_all(self, eng, ops):
        o = Op(eng, None)
        o.deps = list(ops)
        self.ops.append(o)

    def emit(self):
        nc = self.nc
        csem = {e: self.sem("c" + e) for e in self.ENGS}

        def skip(d, o):
            if d.is_dma:
                return False
            if d.eng == "pe" and o.eng == "pe":
                return True
            if (not SAME_ENGINE_SYNC) and d.eng == o.eng and not o.is_dma:
                return True
            return False

        for o in self.ops:
            for d in o.deps:
                if not skip(d, o) and not d.is_dma:
                    d.needs_inc = True
        cnt = {e: 0 for e in self.ENGS}
        for o in self.ops:
            if o.is_dma:
                o.dsem[1] += 16
                o.tok = (o.dsem[0], o.dsem[1])
            elif o.needs_inc:
                cnt[o.eng] += 1
                o.tok = (csem[o.eng], cnt[o.eng])
        seen = {e: {} for e in self.ENGS}
        issued = {}
        for o in self.ops:
            s = seen[o.eng]
            need = {}
            for d in o.deps:
                if d.tok is None or skip(d, o):
                    continue
                sem, val = d.tok
                k = id(sem)
                if d.is_dma and not o.is_dma:
                    val = max(val, issued.get(k, 0))
                if k not in need or need[k][1] < val:
                    need[k] = (sem, val)
            if o.is_dma:
                issued[id(o.tok[0])] = o.tok[1]
            for k, (sem, val) in need.items():
                if s.get(k, 0) >= val:
                    continue
                s[k] = val
                o.waits.append((sem, val))
        per = {e: [o for o in self.ops if o.eng == e] for e in self.ENGS}
        self.counts = {e: len(per[e]) for e in self.ENGS}
        self.sem_counts = cnt

        def run(engobj, lst):
            for o in lst:
                for sem, val in o.waits:
                    engobj.wait_ge(sem, val)
                if o.emit is None:
                    continue
                ins = o.emit(engobj)
                if o.is_dma:
                    ins.then_inc(o.dsem[0], 16)
                elif o.needs_inc:
                    ins.then_inc(o.tok[0], 1)

        with nc.Block() as block:
            @block.tensor
            def _(e):
                run(e, per["pe"])

            @block.scalar
            def _(e):
                run(e, per["act"])

            @block.vector
            def _(e):
                run(e, per["dve"])

            @block.gpsimd
            def _(e):
                run(e, per["pool"])

            @block.sync
            def _(e):
                run(e, per["sp"])
        self.stack.close()


class Arena:
    def __init__(self, nc):
        self.t = nc.alloc_sbuf_tensor_at("arena", [128, ARENA_BYTES // 4], F32, offset=SB_BASE)
        self.off = 0

    def alloc(self, shape, dtype, parts=128):
        esz = 4 if dtype == F32 else 2
        n = int(np.prod(shape))
        nb = (n * esz + 31) // 32 * 32
        o4 = self.off // 4
        v = self.t[0:parts, o4:o4 + nb // 4]
        if dtype != F32:
            v = v.bitcast(dtype)
        v = v[:, 0:n]
        if len(shape) == 2:
            v = v.rearrange("p (a b) -> p a b", b=shape[1])
        elif len(shape) == 3:
            v = v.rearrange("p (a b c) -> p a b c", b=shape[1], c=shape[2])
        elif len(shape) == 4:
            v = v.rearrange("p (a b c d) -> p a b c d", b=shape[1], c=shape[2], d=shape[3])
        self.off += nb
        assert self.off <= ARENA_BYTES, f"SBUF arena overflow {self.off} > {ARENA_BYTES}"
        return v


SM = {}
_o = 0
for _n, _w in [("c", 8), ("cctx", 8), ("bmod", 144), ("ng", 48), ("fng", 8), ("convw", 32), ("convb", 8),
               ("lrub", 32), ("lam", 16), ("pr", 256), ("pc", 256), ("bsbc", 512)]:
    SM[_n] = (_o, _w)
    _o += _w
NS = _o


def _pm(v):
    v = np.asarray(v, np.float32)
    n = v.shape[-1] // 128
    v = v.reshape(v.shape[:-1] + (n, 128))
    return np.moveaxis(v, -1, 0)


def build_smalls(b, inp):
    s = np.zeros((128, NS), np.float32)

    def put(name, arr):
        o, w = SM[name]
        s[:, o:o + w] = np.asarray(arr, np.float32).reshape(128, w)

    put("c", _pm(inp["c"][b]))
    put("cctx", _pm(inp["c_ctx"]))
    put("bmod", _pm(inp["b_mod"]))
    put("ng", _pm(inp["norm_g"]))
    put("fng", _pm(inp["final_norm_g"]))
    cw = _pm(inp["conv_w"])
    put("convw", np.transpose(cw, (0, 1, 3, 2)))
    put("convb", _pm(inp["conv_b"]))
    put("lrub", _pm(inp["lru_b_gates"]))
    put("lam", _pm(inp["lru_lambda"]))
    q = D // 4
    freqs = (1.0 / (10000.0 ** (np.arange(q, dtype=np.float32) / np.float32(q)))).astype(np.float32)
    pos = np.arange(64, dtype=np.float32)
    e = pos[:, None] * freqs[None, :]
    tab = np.concatenate([np.sin(e), np.cos(e)], axis=1).astype(np.float32)
    tabT = tab.T.reshape(4, 128, 64).transpose(1, 0, 2)
    put("pr", tabT)
    put("pc", tabT)
    bs = np.asarray(inp["gmlp_bs"], np.float32)
    bsbc = np.zeros((128, 2, 2, 128), np.float32)
    for l in range(2):
        for j in range(2):
            bsbc[0:64, l, j, :] = bs[l, 2 * j][None, :]
            bsbc[64:128, l, j, :] = bs[l, 2 * j + 1][None, :]
    put("bsbc", bsbc)
    return s


def build_consts():
    bf = ml_dtypes.bfloat16
    p = np.arange(128, dtype=np.int64)
    kp = np.arange(512, dtype=np.int64)
    C0 = np.empty((2, 128, 32, 512), np.float32)
    for r in range(8):
        for qc in range(4):
            n = 8 * (128 * qc + p) + r
            ang = (2.0 * np.pi / L) * ((n[:, None] * kp[None, :]) % L).astype(np.float64)
            C0[0, :, 4 * r + qc, :] = np.cos(ang) / 64.0
            C0[1, :, 4 * r + qc, :] = -np.sin(ang) / 64.0
    C0 = C0.reshape(2, 128, 32 * 512).astype(bf)
    DG = np.zeros((128, 4, 128), np.float32)
    eye = np.eye(128, dtype=np.float32)
    for i, v in enumerate((1.0, -1.0, np.sqrt(0.5), -np.sqrt(0.5))):
        DG[:, i, :] = eye * np.float32(v)
    DG = DG.reshape(128, 512).astype(bf)
    n2 = np.arange(256, dtype=np.int64)
    ang2 = (2.0 * np.pi / 256) * ((n2[:, None] * n2[None, :]) % 256).astype(np.float64)
    C2 = (np.cos(ang2) / 16.0).astype(np.float32).reshape(2, 128, 256)
    S2 = (-np.sin(ang2) / 16.0).astype(np.float32).reshape(2, 128, 256)
    CLc = np.stack([C2, S2], 0).transpose(2, 0, 1, 3).reshape(128, 2 * 2 * 256).astype(bf)
    m = np.arange(64, dtype=np.int64)
    a64 = (2.0 * np.pi / 64) * ((m[:, None] * m[None, :]) % 64).astype(np.float64)
    c64 = np.cos(a64) / 8.0
    s64 = np.sin(a64) / 8.0
    CS = np.zeros((128, 3, 128), np.float32)
    for g in range(2):
        CS[g * 64:(g + 1) * 64, 0, g * 64:(g + 1) * 64] = c64
        CS[g * 64:(g + 1) * 64, 1, g * 64:(g + 1) * 64] = s64
        CS[g * 64:(g + 1) * 64, 2, g * 64:(g + 1) * 64] = -c64
    CS = CS.reshape(128, 384).astype(bf)
    return C0, DG, CLc, CS


def build(stop_after=None, debug=False):
    nc = bass.Bass("TRN2", target_bir_lowering=False)
    P = Prog(nc)
    A = Arena(nc)

    def din(name, shape, dt=F32):
        return nc.dram_tensor(name, list(shape), dt, kind="ExternalInput").ap()

    skind = "ExternalOutput" if debug else "Internal"

    def dscratch(name, shape, dt):
        return nc.dram_tensor(name, list(shape), dt, kind=skind).ap()

    xTb = din("xTb", [16, 128, KC * T])
    ctxTb = din("ctxTb", [128, KC * T])
    smalls_d = din("smalls", [128, NS])
    w_mod = din("w_mod", [2, D, 9 * D])
    w_gu = din("ffn_w_gu", [2, 2, D, 2 * FF])
    w_dn = din("ffn_w_down", [2, 2, FF, D])
    w_in = din("w_in", [2, D, 1792])
    w_out = din("w_out", [2, D, D])
    w_gates = din("lru_w_gates", [2, 2, 2, 8, 64, 64])
    wsT_d = din("gmlp_wsT", [2, 4, 128, 128])
    C0_d = din("C0", [2, 128, 32 * 512], BF16)
    DG_d = din("DG", [128, 512], BF16)
    CLc_d = din("CLc", [128, 1024], BF16)
    CS_d = din("CS", [128, 384], BF16)
    outTb = nc.dram_tensor("outTb", [16, 128, KC * T], F32, kind="ExternalOutput").ap()

    hTb = dscratch("hTb", [NB, 128, KC * T], F32)
    zxa = dscratch("zxa", [4, 128, NTOK], BF16)
    zga = dscratch("zga", [4, 128, NTOK], BF16)
    zu = dscratch("zu", [2, 128, NTOK], BF16)
    zf = dscratch("zf", [2, 128, NTOK], BF16)
    zv = dscratch("zv", [128, 34, 256], BF16)
    yT = dscratch("yT", [KC, 128, NTOK], BF16)

    wgu_bf = [nc.dram_tensor(f"wgu_bf{i}", [128, KC * 2 * FF], BF16, kind="Internal").ap() for i in range(3)]
    wdn_bf = [nc.dram_tensor(f"wdn_bf{i}", [128, JC * D], BF16, kind="Internal").ap() for i in range(3)]
    R_wgu_bf = [[Res(f"wgubf{i}_{kc}") for kc in range(KC)] for i in range(3)]
    R_wdn_bf = [[Res(f"wdnbf{i}_{h}") for h in range(2)] for i in range(3)]
    JG = [(0, 6), (6, 12), (12, 17), (17, 22)]

    R_h = [Res(f"hTb{b}") for b in range(NB)]
    R_y = [[Res(f"yTb{b}_{m}") for m in range(KC)] for b in range(NB)]
    R_zxa = [Res(f"zxa{c}") for c in range(4)]
    R_zga = [Res(f"zga{c}") for c in range(4)]
    R_zu = [Res(f"zu{c}") for c in range(2)]
    R_zf = [Res(f"zf{c}") for c in range(2)]
    R_zv = Res("zv")
    out_stores = []

    sm = A.alloc([NS], F32)
    R_sm = Res("sm")

    def S(name):
        o, w = SM[name]
        return sm[:, o:o + w]

    MOD = A.alloc([2, 72, 2], F32)
    AV = A.alloc([2, 2, 3, 8], F32)
    GV = A.alloc([2, 2, 3, 8], F32)
    CH2 = A.alloc([2, 2, 4], F32)
    HB = A.alloc([2, 2, 2, 4], F32)
    SC = A.alloc([8, 2], BF16)
    sctmp = A.alloc([8, 2], F32)
    eps_t = A.alloc([1], F32)
    R_const = Res("const")
    R_modl = [Res("mod0"), Res("mod1")]
    R_sc = Res("silu_c")
    cur = {"l": 0}

    def RM():
        return R_modl[cur["l"]]
    persist_off = A.off

    PS = [P.psum([128, 512]) for _ in range(8)]
    R_ps = [Res(f"psbank{i}") for i in range(8)]

    P.dma("sp", sm, smalls_d, writes=[R_sm])
    P.op("dve", lambda e: e.memset(eps_t, EPS), writes=[R_const])

    def SHIFT(l, s, which, m):
        j = (3 * which) * 8 + m
        return MOD[:, l, j, s:s + 1]

    def prologue():
        P.op("act", lambda e: e.activation(sctmp[:, :, 0], S("c"), AF.Silu), reads=[R_sm], writes=[R_sc])
        P.op("act", lambda e: e.activation(sctmp[:, :, 1], S("cctx"), AF.Silu), reads=[R_sm], writes=[R_sc])
        P.op("dve", lambda e: e.tensor_copy(SC, sctmp), reads=[R_sc], writes=[R_sc])
        lamv = S("lam")
        ch2f = CH2.rearrange("p a b c -> p (a b c)")
        sp_x = A.alloc([16], F32)
        sp_z = A.alloc([16], F32)
        sp_z2 = A.alloc([16], F32)
        sp_p = A.alloc([16], F32)
        rc = [R_sm, R_const]
        P.op("act", lambda e: e.activation(sp_x, lamv, AF.Abs), reads=rc, writes=[R_const])
        P.op("act", lambda e: e.activation(sp_x, sp_x, AF.Exp, scale=-1.0), reads=rc, writes=[R_const])
        P.op("dve", lambda e: e.tensor_scalar(sp_z, sp_x, 2.0, None, ALU.add), reads=rc, writes=[R_const])
        P.op("dve", lambda e: e.reciprocal(sp_z, sp_z), reads=rc, writes=[R_const])
        P.op("dve", lambda e: e.tensor_tensor(sp_z, sp_z, sp_x, ALU.mult), reads=rc, writes=[R_const])
        P.op("dve", lambda e: e.tensor_tensor(sp_z2, sp_z, sp_z, ALU.mult), reads=rc, writes=[R_const])
        P.op("dve", lambda e: e.memset(sp_p, 1.0 / 15.0), reads=rc, writes=[R_const])
        for cf in (1.0 / 13, 1.0 / 11, 1.0 / 9, 1.0 / 7, 1.0 / 5, 1.0 / 3, 1.0):
            P.op("dve", lambda e: e.tensor_tensor(sp_p, sp_p, sp_z2, ALU.mult), reads=rc, writes=[R_const])
            P.op("dve", lambda e, cf=cf: e.tensor_scalar(sp_p, sp_p, float(cf), None, ALU.add), reads=rc, writes=[R_const])
        P.op("dve", lambda e: e.tensor_tensor(sp_p, sp_p, sp_z, ALU.mult), reads=rc, writes=[R_const])
        P.op("dve", lambda e: e.tensor_scalar(sp_x, lamv, -1.0, 0.0, ALU.mult, ALU.max), reads=rc, writes=[R_const])
        P.op("dve", lambda e: e.scalar_tensor_tensor(sp_p, sp_p, 2.0, sp_x, ALU.mult, ALU.add), reads=rc, writes=[R_const])
        P.op("dve", lambda e: e.tensor_scalar(ch2f, sp_p, -4.0, None, ALU.mult), reads=rc, writes=[R_const])
        hbf = HB.rearrange("p a b c d -> p (a b c d)")
        P.op("dve", lambda e: e.tensor_scalar(hbf, S("lrub"), 0.5, None, ALU.mult), reads=[R_sm], writes=[R_const])
        mark = A.off
        for st in mod_layer_steps(0, 0):
            st()
        A.off = mark

    def mod_layer_steps(l, bank, cw_=1152):
        NCH = 9216 // cw_
        JJ = cw_ // 128
        wbuf = [A.alloc([KC, cw_], BF16) for _ in range(2)]
        R_wb = [[Res(f"wmod{l}_{i}_{kc}") for kc in range(KC)] for i in range(2)]
        bm = S("bmod").rearrange("p (l j) -> p l j", l=2)
        ps = PS[bank]
        ng = S("ng").rearrange("p (l w m) -> p l w m", l=2, w=3)

        def chunk(ch):
            bi = ch % 2
            for kc in range(KC):
                P.dma("pool", wbuf[bi][:, kc, :], w_mod[l, kc * 128:(kc + 1) * 128, ch * cw_:(ch + 1) * cw_],
                      writes=[R_wb[bi][kc]], key=f"wmod{l}_{bi}")
            for jj in range(JJ):
                j = ch * JJ + jj
                def emit(e, jj=jj, j=j):
                    for kc in range(KC):
                        ins = e.matmul(ps[:, 2 * j:2 * j + 2], wbuf[bi][:, kc, jj * 128:(jj + 1) * 128], SC[:, kc, :],
                                       start=(kc == 0), stop=(kc == KC - 1))
                    return ins
                P.op("pe", emit, reads=R_wb[bi] + [R_sc], writes=[R_ps[bank]])

        def epilogue():
            P.op("dve", lambda e: e.tensor_tensor(
                MOD[:, l], ps[:, 0:144].rearrange("p (j s) -> p j s", s=2),
                bm[:, l, :].unsqueeze(2).to_broadcast([128, 72, 2]), ALU.add),
                reads=[R_ps[bank], R_sm], writes=[R_modl[l]])
            for s_ in range(2):
                for w in range(3):
                    j0 = (3 * w + 1) * 8
                    P.op("dve", lambda e, s_=s_, w=w, j0=j0: e.scalar_tensor_tensor(
                        AV[:, l, s_, w, :], MOD[:, l, j0:j0 + 8, s_], 1.0, ng[:, l, w, :], ALU.add, ALU.mult),
                        reads=[R_modl[l], R_sm], writes=[R_modl[l]])
                    j2 = (3 * w + 2) * 8
                    P.op("dve", lambda e, s_=s_, w=w, j2=j2: e.tensor_scalar(
                        GV[:, l, s_, w, :], MOD[:, l, j2:j2 + 8, s_], (1.0 if w == 1 else 0.5), None, ALU.mult),
                        reads=[R_modl[l]], writes=[R_modl[l]])

        return [(lambda ch=ch: chunk(ch)) for ch in range(NCH)] + [epilogue]

    def norm_steps(hb, R_hb, nT, R_nT, sq, R_sq, rstd, R_rstd, tmp, R_tmp, stat_bank, scale_ap, shift_ap, n=T,
                   mode="pool"):
        ps = PS[stat_bank]
        steps = []

        nsq = len(sq)

        def sq_a(m0):
            for m in (m0, m0 + 1):
                if mode in ("pool", "split"):
                    P.op("pool", lambda e, m=m: e.tensor_tensor(sq[m % nsq][:, 0:n], hb[:, m, 0:n], hb[:, m, 0:n], ALU.mult),
                         reads=[R_hb[m]], writes=[R_sq[m % nsq]])
                else:
                    P.op("act", lambda e, m=m: e.activation(sq[m % nsq][:, 0:n], hb[:, m, 0:n], AF.Square),
                         reads=[R_hb[m]], writes=[R_sq[m % nsq]])

        def sq_b(m0):
            for m in (m0, m0 + 1):
                P.op("pe", lambda e, m=m: e.matmul(ps[:, 0:n], ones_mat, sq[m % nsq][:, 0:n],
                                                  start=(m == 0), stop=(m == KC - 1)),
                     reads=[R_sq[m % nsq], R_const], writes=[R_ps[stat_bank]])

        def sq_step(m0):
            sq_a(m0)
            sq_b(m0)

        def rstd_step():
            P.op("act", lambda e: e.activation(rstd[:, 0:n], ps[:, 0:n], AF.Ln, bias=eps_t[:, 0:1], scale=1.0 / D),
                 reads=[R_ps[stat_bank], R_const], writes=[R_rstd])
            P.op("act", lambda e: e.activation(rstd[:, 0:n], rstd[:, 0:n], AF.Exp, scale=-0.5),
                 reads=[R_rstd], writes=[R_rstd])

        def nrm_step(m):
            if mode == "split":
                P.op("dve", lambda e: e.tensor_tensor(tmp[m % 2][:, 0:n], hb[:, m, 0:n], rstd[:, 0:n], ALU.mult),
                     reads=[R_hb[m], R_rstd], writes=[R_tmp[m % 2]])
                P.op("pool", lambda e: e.tensor_scalar(nT[:, m, 0:n], tmp[m % 2][:, 0:n], scale_ap(m), shift_ap(m),
                                                       ALU.mult, ALU.add),
                     reads=[R_tmp[m % 2], RM()], writes=[R_nT[m]])
                return
            if mode != "pool":
                if shift_ap is None:
                    P.op("dve", lambda e: e.scalar_tensor_tensor(nT[:, m, 0:n], hb[:, m, 0:n], scale_ap(m), rstd[:, 0:n],
                                                                 ALU.mult, ALU.mult),
                         reads=[R_hb[m], R_rstd, RM()], writes=[R_nT[m]])
                    return
                P.op("dve", lambda e: e.tensor_tensor(tmp[m % 2][:, 0:n], hb[:, m, 0:n], rstd[:, 0:n], ALU.mult),
                     reads=[R_hb[m], R_rstd], writes=[R_tmp[m % 2]])
                P.op("act", lambda e: e.activation(nT[:, m, 0:n], tmp[m % 2][:, 0:n], AF.Identity,
                                                   bias=shift_ap(m), scale=scale_ap(m)),
                     reads=[R_tmp[m % 2], RM()], writes=[R_nT[m]])
                return
            P.op("pool", lambda e: e.tensor_tensor(tmp[m % 2][:, 0:n], hb[:, m, 0:n], rstd[:, 0:n], ALU.mult),
                 reads=[R_hb[m], R_rstd], writes=[R_tmp[m % 2]])
            if shift_ap is not None:
                P.op("pool", lambda e: e.tensor_scalar(nT[:, m, 0:n], tmp[m % 2][:, 0:n], scale_ap(m), shift_ap(m),
                                                       ALU.mult, ALU.add),
                     reads=[R_tmp[m % 2], RM()], writes=[R_nT[m]])
            else:
                P.op("pool", lambda e: e.tensor_scalar(nT[:, m, 0:n], tmp[m % 2][:, 0:n], scale_ap(m), None, ALU.mult),
                     reads=[R_tmp[m % 2], RM()], writes=[R_nT[m]])

        if nsq >= 8:
            for fn, m0 in ((sq_a, 0), (sq_a, 2), (sq_b, 0), (sq_a, 4), (sq_b, 2), (sq_a, 6), (sq_b, 4), (sq_b, 6)):
                steps.append(lambda fn=fn, m0=m0: fn(m0))
        else:
            for m0 in (0, 2, 4, 6):
                steps.append(lambda m0=m0: sq_step(m0))
        steps.append(rstd_step)
        for m in range(KC):
            steps.append(lambda m=m: nrm_step(m))
        return steps

    def norm_block(*a, **k):
        for st in norm_steps(*a, **k):
            st()

    ones_mat = A.alloc([128], BF16)
    persist_off = A.off
    P.op("dve", lambda e: e.memset(ones_mat, 1.0), writes=[R_const])

    def make_precast(si):
        fi = si + 1
        nl, nw = fi // 2, fi % 2
        dst_gu = wgu_bf[si].rearrange("p (kc n) -> p kc n", kc=KC)
        dst_dn = wdn_bf[si].rearrange("p (j n) -> p j n", j=JC)
        nsrc_dn = w_dn[nl, nw].rearrange("(j p) n -> p j n", p=128)
        parts = []
        for kc in range(KC):
            parts.append(lambda kc=kc: P.dma("pool", dst_gu[:, kc, :], w_gu[nl, nw, kc * 128:(kc + 1) * 128, :],
                                             writes=[R_wgu_bf[si][kc]], key="pcgu"))
        for h in range(2):
            parts.append(lambda h=h: P.dma("pool", dst_dn[:, 11 * h:11 * (h + 1), :], nsrc_dn[:, 11 * h:11 * (h + 1), :],
                                           writes=[R_wdn_bf[si][h]], key=f"pcdn{h}"))
        return parts

    def ffn_phase(l, which):
        cur["l"] = l
        A.off = persist_off
        first = (l == 0 and which == 0)
        is_c = which == 2
        last = (l == DEPTH - 1) and is_c
        wsel = 0 if which == 0 else 1
        wgu = A.alloc([KC, 2 * FF], BF16)
        wdn = A.alloc([JC, D], BF16)
        idx = 2 * l + (1 if is_c else 0)
        R_wgu_g = [[Res(f"wgu_g{g}_{u}") for u in range(2)] for g in range(len(JG))]
        R_wdn = [Res(f"wdn{h}") for h in range(2)]
        jgroup = {}
        for g, (j0, j1) in enumerate(JG):
            for j in range(j0, j1):
                jgroup[j] = g
        use_scratch = is_c
        if not use_scratch:
            src_gu = w_gu[l, wsel].rearrange("(kc p) n -> p kc n", p=128)
            src_dn = w_dn[l, wsel].rearrange("(j p) n -> p j n", p=128)
            wqs = ("pool", "pool")
            rd_gu, rd_dn = [], [[], []]
        else:
            src_gu = wgu_bf[idx - 1].rearrange("p (kc n) -> p kc n", kc=KC)
            src_dn = wdn_bf[idx - 1].rearrange("p (j n) -> p j n", j=JC)
            wqs = ("sp", "act")
            rd_gu, rd_dn = R_wgu_bf[idx - 1], [[R_wdn_bf[idx - 1][0]], [R_wdn_bf[idx - 1][1]]]

        def load_weights():
            for g, (j0, j1) in enumerate(JG):
                for u in range(2):
                    c0, c1 = u * FF + j0 * 128, u * FF + j1 * 128
                    P.dma(wqs[u], wgu[:, :, c0:c1], src_gu[:, :, c0:c1], reads=rd_gu, writes=[R_wgu_g[g][u]],
                          key=f"wgu{g}_{u}")
            for h in range(2):
                P.dma(wqs[h], wdn[:, 11 * h:11 * (h + 1), :], src_dn[:, 11 * h:11 * (h + 1), :], reads=rd_dn[h],
                      writes=[R_wdn[h]], key=f"wdn{h}")
        if is_c:
            wo = A.alloc([KC, D], BF16)
            R_wo = [Res(f"wo{kc}") for kc in range(KC)]
            for kc in range(KC):
                P.dma("pool", wo[:, kc, :], w_out[l, kc * 128:(kc + 1) * 128, :], writes=[R_wo[kc]], key="wo")
            ybuf = [A.alloc([KC, T], BF16) for _ in range(2)]
            R_yb = [Res(f"ybuf{i}") for i in range(2)]
        NH = 2 if is_c else 3
        hbuf = [A.alloc([KC, T], F32) for _ in range(NH)]
        R_hbuf = [[Res(f"hbuf{i}_{m}") for m in range(KC)] for i in range(NH)]
        nT = [A.alloc([KC, T], BF16) for _ in range(2)]
        R_nT = [[Res(f"nT{i}_{m}") for m in range(KC)] for i in range(2)]
        hid = A.alloc([JC, T], BF16)
        R_hid = [Res(f"hid{j}") for j in range(JC)]
        sq = [A.alloc([T], BF16) for _ in range(8)]
        R_sq = [Res(f"sq{i}") for i in range(8)]
        tmp = [A.alloc([T], F32) for _ in range(2)]
        R_tmp = [Res("tmp0"), Res("tmp1")]
        rstd = A.alloc([T], F32)
        R_rstd = Res("rstd")
        sg = [A.alloc([T], BF16) for _ in range(3)]
        R_sg = [Res(f"sg{i}") for i in range(3)]
        blocks = list(range(NB))
        if last:
            blocks = list(range(1, NB))
        last = False

        def load(b, slot):
            load1(b, slot)
            load2(b, slot)

        def load1(b, slot):
            rs = R_hbuf[slot]
            if first:
                src = ctxTb if b == 0 else xTb[b - 1]
                P.dma("sp", hbuf[slot].rearrange("p a b -> p (a b)"), src, writes=rs, key=f"hload{slot}")

        def pos_steps(b, slot):
            rs = R_hbuf[slot]
            steps = []
            if first and b > 0:
                def f1():
                    r0 = (b - 1) * 4
                    pr = S("pr").rearrange("p (m r) -> p m r", m=4)
                    pc = S("pc").rearrange("p (m r) -> p m r", m=4)
                    for m in range(4):
                        hv = hbuf[slot][:, m, :].rearrange("p (r c) -> p r c", c=64)
                        P.op("dve", lambda e, hv=hv, m=m, r0=r0: e.tensor_tensor(
                            hv, hv, pr[:, m, r0:r0 + 4].unsqueeze(2).to_broadcast([128, 4, 64]), ALU.add),
                            reads=[rs[m], R_sm], writes=[rs[m]])
                    for m in range(4, 8):
                        hv = hbuf[slot][:, m, :].rearrange("p (r c) -> p r c", c=64)
                        P.op("dve", lambda e, hv=hv, m=m: e.tensor_tensor(
                            hv, hv, pc[:, m - 4, :].unsqueeze(1).to_broadcast([128, 4, 64]), ALU.add),
                            reads=[rs[m], R_sm], writes=[rs[m]])
                steps.append(f1)
            return steps

        def load2(b, slot):
            rs = R_hbuf[slot]
            if not first:
                P.dma("sp", hbuf[slot].rearrange("p a b -> p (a b)"), hTb[b], reads=[R_h[b]], writes=rs,
                      key=f"hload{slot}")
            if is_c:
                P.dma("sp", ybuf[slot], yT[:, :, b * T:(b + 1) * T].rearrange("m p t -> p m t"), reads=R_y[b], writes=[R_yb[slot]],
                      key=f"yload{slot}")

        def wout_m(b, slot, m):
            s = 0 if b > 0 else 1
            if True:
                bank = 6 + (m % 2)
                ps = PS[bank]
                def emit(e, m=m, ps=ps):
                    for kc in range(KC):
                        ins = e.matmul(ps[:, 0:T], wo[:, kc, m * 128:(m + 1) * 128], ybuf[slot][:, kc, :],
                                       start=(kc == 0), stop=(kc == KC - 1))
                    return ins
                P.op("pe", emit, reads=R_wo + [R_yb[slot]], writes=[R_ps[bank]])
                P.op("dve", lambda e, m=m, ps=ps, s=s: e.scalar_tensor_tensor(
                    hbuf[slot][:, m, :], ps[:, 0:T], GV[:, l, s, 1, m:m + 1], hbuf[slot][:, m, :], ALU.mult, ALU.add),
                    reads=[R_ps[bank], RM(), R_hbuf[slot][m]], writes=[R_hbuf[slot][m]])

        def pre_steps(b, slot, ns):
            s = 0 if b > 0 else 1
            steps = pos_steps(b, slot)
            if is_c:
                for m in range(KC):
                    steps.append(lambda m=m: wout_m(b, slot, m))
            steps += norm_steps(hbuf[slot], R_hbuf[slot], nT[ns], R_nT[ns], sq, R_sq, rstd, R_rstd, tmp, R_tmp, 0,
                                lambda m: AV[:, l, s, which, m:m + 1], lambda m: SHIFT(l, s, which, m))
            return steps

        def gu(b, slot, j, cnt):
            bank = 1 + (cnt % 3)
            ps = PS[bank]
            def emit(e, j=j, ps=ps):
                for kc in range(KC):
                    e.matmul(ps[:, 0:T], wgu[:, kc, j * 128:(j + 1) * 128], nT[slot][:, kc, :],
                             start=(kc == 0), stop=(kc == KC - 1))
                for kc in range(KC):
                    ins = e.matmul(ps[:, T:2 * T], wgu[:, kc, FF + j * 128:FF + (j + 1) * 128], nT[slot][:, kc, :],
                                   start=(kc == 0), stop=(kc == KC - 1))
                return ins
            P.op("pe", emit, reads=R_wgu_g[jgroup[j]] + R_nT[slot], writes=[R_ps[bank]])
            si = cnt % 3
            P.op("act", lambda e, ps=ps, si=si: e.activation(sg[si], ps[:, 0:T], AF.Silu),
                 reads=[R_ps[bank]], writes=[R_sg[si]])
            P.op("dve", lambda e, ps=ps, si=si, j=j: e.tensor_tensor(hid[:, j, :], sg[si], ps[:, T:2 * T], ALU.mult),
                 reads=[R_sg[si], R_ps[bank]], writes=[R_hid[j]])

        def down(b, slot, m):
            s = 0 if b > 0 else 1
            bank = 4 + (m % 2)
            ps = PS[bank]
            def emit(e, m=m, ps=ps):
                for j in range(JC):
                    ins = e.matmul(ps[:, 0:T], wdn[:, j, m * 128:(m + 1) * 128], hid[:, j, :],
                                   start=(j == 0), stop=(j == JC - 1))
                return ins
            P.op("pe", emit, reads=R_wdn + R_hid, writes=[R_ps[bank]])
            P.op("dve", lambda e, m=m, ps=ps, s=s: e.scalar_tensor_tensor(
                hbuf[slot][:, m, :], ps[:, 0:T], GV[:, l, s, which, m:m + 1], hbuf[slot][:, m, :], ALU.mult, ALU.add),
                reads=[R_ps[bank], RM(), R_hbuf[slot][m]], writes=[R_hbuf[slot][m]])

        def store(b, slot):
            if last:
                fng = S("fng")
                norm_block(hbuf[slot], R_hbuf[slot], hbuf[slot], R_hbuf[slot], sq, R_sq, rstd, R_rstd, tmp,
                           R_tmp, 0, lambda m: fng[:, m:m + 1], None)
                o = P.dma("sp", outTb[b - 1], hbuf[slot].rearrange("p a b -> p (a b)"), reads=R_hbuf[slot],
                          key=f"hstore{slot}")
                out_stores.append(o)
            else:
                P.dma("sp", hTb[b], hbuf[slot].rearrange("p a b -> p (a b)"), reads=R_hbuf[slot], writes=[R_h[b]],
                      key=f"hstore{slot}")

        nblk = len(blocks)
        for i in range(min(NH, nblk)):
            load(blocks[i], i)
        load_weights()
        pc_parts = make_precast(idx) if (not is_c) else []
        for st in pre_steps(blocks[0], 0, 0):
            st()
        cnt = 0
        for i, b in enumerate(blocks):
            slot = i % NH
            ns = i % 2
            if i >= 2 and pc_parts:
                pc_parts.pop(0)()
            pend = pre_steps(blocks[i + 1], (i + 1) % NH, 1 - ns) if i + 1 < nblk else []
            j0 = 8
            per_j = 2 if is_c else 1
            for j in range(JC):
                gu(b, ns, j, cnt)
                cnt += 1
                if j >= j0:
                    for _ in range(per_j):
                        if pend:
                            pend.pop(0)()
            while pend:
                pend.pop(0)()
            for m in range(KC):
                down(b, slot, m)
            store(b, slot)
            if i + NH < nblk:
                load(blocks[i + NH], slot)

    def mixin_phase(l):
        cur["l"] = l
        A.off = persist_off
        win = A.alloc([KC, 1792], BF16)
        R_win = [Res(f"win{kc}") for kc in range(KC)]
        for kc in range(KC):
            P.dma("pool", win[:, kc, :], w_in[l, kc * 128:(kc + 1) * 128, :], writes=[R_win[kc]], key="win")
        hbuf = [A.alloc([KC, T], F32) for _ in range(3)]
        R_hbuf = [[Res(f"a2h{i}_{m}") for m in range(KC)] for i in range(3)]
        nT = [A.alloc([KC, T], BF16) for _ in range(3)]
        R_nT = [[Res(f"a2n{i}_{m}") for m in range(KC)] for i in range(3)]
        sq = [A.alloc([T], BF16) for _ in range(8)]
        R_sq = [Res(f"a2sq{i}") for i in range(8)]
        tmp = [A.alloc([T], F32) for _ in range(2)]
        R_tmp = [Res("a2t0"), Res("a2t1")]
        rstd = A.alloc([T], F32)
        R_rstd = Res("a2rstd")
        GB = 4
        zst = [A.alloc([12, GB * T], BF16) for _ in range(2)]
        R_zst = [[Res(f"zst{i}_{c}") for c in range(12)] for i in range(2)]
        zvst = [A.alloc([2 * GB, 256], BF16) for _ in range(2)]
        R_zvst = [[Res(f"zvst{i}_{c}") for c in range(2 * GB)] for i in range(2)]
        fm_cols = [0, 1, 2, 3, 4, 5, 6, 7, 8, 9, 12, 13]
        cnt = 0

        def load(b, slot):
            P.dma("sp", hbuf[slot].rearrange("p a b -> p (a b)"), hTb[b], reads=[R_h[b]], writes=R_hbuf[slot],
                  key=f"a2load{slot}")

        def norm(b, slot):
            s = 0 if b > 0 else 1
            return norm_steps(hbuf[slot], R_hbuf[slot], nT[slot], R_nT[slot], sq, R_sq, rstd, R_rstd, tmp, R_tmp, 0,
                              lambda m: AV[:, l, s, 1, m:m + 1], lambda m: SHIFT(l, s, 1, m), mode="split")

        load(0, 0)
        load(1, 1)
        load(2, 2)
        for st in norm(0, 0):
            st()
        for st in norm(1, 1):
            st()
        cntbox = [0]

        mod_next = []

        def do_block(b, slot, zs):
            if b >= 1 and mod_next:
                mod_next.pop(0)()
            gi = b % GB
            g0 = (b // GB) * GB
            gn = min(GB, NB - g0)
            co = gi * T
            pend = norm(b + 2, (b + 2) % 3) if b + 2 < NB else []
            for ci, c in enumerate(fm_cols):
                bank = 1 + (cntbox[0] % 4)
                cntbox[0] += 1
                ps = PS[bank]
                def emit(e, c=c, ps=ps):
                    for kc in range(KC):
                        ins = e.matmul(ps[:, 0:T], win[:, kc, c * 128:(c + 1) * 128], nT[slot][:, kc, :],
                                       start=(kc == 0), stop=(kc == KC - 1))
                    return ins
                P.op("pe", emit, reads=R_win + R_nT[slot], writes=[R_ps[bank]])
                if 4 <= ci < 10:
                    P.op("act", lambda e, ps=ps, ci=ci: e.activation(zst[zs][:, ci, co:co + T], ps[:, 0:T], AF.Gelu_apprx_tanh),
                         reads=[R_ps[bank]], writes=[R_zst[zs][ci]])
                else:
                    P.op("dve", lambda e, ps=ps, ci=ci: e.tensor_copy(zst[zs][:, ci, co:co + T], ps[:, 0:T]),
                         reads=[R_ps[bank]], writes=[R_zst[zs][ci]])
                for _ in range(2 if ci >= 7 else 1):
                    if pend:
                        pend.pop(0)()
            for hf in range(2):
                bank = 5 + hf
                ps = PS[bank]
                def emit(e, hf=hf, ps=ps):
                    for kc in range(KC):
                        ins = e.matmul(ps[:, 0:256], nT[slot][:, kc, hf * 128:(hf + 1) * 128], win[:, kc, 1280:1536],
                                       start=(kc == 0), stop=(kc == KC - 1))
                    return ins
                P.op("pe", emit, reads=R_win + R_nT[slot], writes=[R_ps[bank]])
                P.op("act", lambda e, ps=ps, hf=hf: e.activation(zvst[zs][:, 2 * gi + hf, :], ps[:, 0:256], AF.Gelu_apprx_tanh),
                     reads=[R_ps[bank]], writes=[R_zvst[zs][2 * gi + hf]])
            while pend:
                pend.pop(0)()
            if gi == gn - 1:
                t0 = g0 * T
                nn = gn * T
                P.dma("sp", zxa[:, :, t0:t0 + nn].rearrange("c p t -> p c t"), zst[zs][:, 0:4, 0:nn],
                      reads=R_zst[zs][0:4], writes=R_zxa, key=f"zs0_{zs}")
                P.dma("sp", zga[:, :, t0:t0 + nn].rearrange("c p t -> p c t"), zst[zs][:, 4:8, 0:nn],
                      reads=R_zst[zs][4:8], writes=R_zga, key=f"zs1_{zs}")
                P.dma("sp", zu[:, :, t0:t0 + nn].rearrange("c p t -> p c t"), zst[zs][:, 8:10, 0:nn],
                      reads=R_zst[zs][8:10], writes=R_zu, key=f"zs2_{zs}")
                P.dma("sp", zf[:, :, t0:t0 + nn].rearrange("c p t -> p c t"), zst[zs][:, 10:12, 0:nn],
                      reads=R_zst[zs][10:12], writes=R_zf, key=f"zs3_{zs}")
                P.dma("sp", zv[:, 2 * g0:2 * g0 + 2 * gn, :], zvst[zs][:, 0:2 * gn, :],
                      reads=R_zvst[zs][0:2 * gn], writes=[R_zv], key=f"zs4_{zs}")
            if b + 3 < NB:
                load(b + 3, slot)

        for b in range(NB):
            do_block(b, b % 3, (b // GB) % 2)
        while mod_next:
            mod_next.pop(0)()

    def lru_phase(l):
        A.off = persist_off
        last = l == DEPTH - 1
        wg = [A.alloc([2, 2, 128], BF16) for _ in range(2)]
        wgf = A.alloc([2, 2, 128], F32)
        R_wg = [Res("wg0"), Res("wg1")]
        R_wgf = [Res(f"wgf{i}") for i in range(8)]
        ident = A.alloc([128], BF16)
        R_ident = Res("ident")
        P.dma("sp", ident, DG_d[:, 0:128], writes=[R_ident])
        dk = [A.alloc([4, 128], BF16) for _ in range(2)]
        R_dk = [Res("dk0"), Res("dk1")]
        padc = [A.alloc([2 + 256 + 1], BF16) for _ in range(2)]
        padl = [A.alloc([2 + L + 1], BF16) for _ in range(2)]
        R_padc = [Res("padc0"), Res("padc1")]
        R_padl = [Res("padl0"), Res("padl1")]
        xc = A.alloc([NTOK], F32)
        xcb = A.alloc([NTOK], BF16)
        Ab2 = [A.alloc([NTOK], F32) for _ in range(2)]
        Sb2 = [A.alloc([NTOK], F32) for _ in range(2)]
        Bb2 = [A.alloc([NTOK], F32) for _ in range(2)]
        Hf = A.alloc([NTOK], F32)
        Hb = Sb2[1]
        gga = A.alloc([NTOK], BF16)
        yst = Bb2[1][:, 0:NTOK // 2].bitcast(BF16)
        thr = [A.alloc([512], F32) for _ in range(2)]
        thi = [A.alloc([512], F32) for _ in range(2)]
        R_thr = [Res("thr0"), Res("thr1")]
        R_thi = [Res("thi0"), Res("thi1")]
        segs = [(0, 256)] + [(256 + 512 * k, 512) for k in range(8)]
        R_xc = [Res(f"xc{i}") for i in range(9)]
        R_xcb = [Res(f"xcb{i}") for i in range(9)]
        R_A2 = [[Res(f"A{q}_{i}") for i in range(9)] for q in range(2)]
        R_S2 = [[Res(f"S{q}_{i}") for i in range(9)] for q in range(2)]
        R_B2 = [[Res(f"B{q}_{i}") for i in range(9)] for q in range(2)]
        R_Hf = Res("Hf")
        R_gga = Res("gga")
        mod_next = mod_layer_steps(l + 1, 7, 384) if l + 1 < DEPTH else []
        cw = S("convw").rearrange("p (l c k) -> p l c k", l=2, c=4)
        cb = S("convb").rearrange("p (l c) -> p l c", l=2)
        for q in range(2):
            P.op("pool", lambda e, q=q: e.memset(padc[q], 0.0), writes=[R_padc[q]])
            P.op("pool", lambda e, q=q: e.memset(padl[q], 0.0), writes=[R_padl[q]])
        P.op("pool", lambda e: e.memset(wgf.rearrange("p a b c -> p (a b c)"), 0.0), writes=R_wgf)
        cntbox = [0]

        def prefetch(cc):
            q = cc % 2
            for d in range(2):
                for g in range(2):
                    for hh in range(2):
                        P.dma("sp", wgf[hh * 64:(hh + 1) * 64, d, g, hh * 64:(hh + 1) * 64], w_gates[l, d, g, 2 * cc + hh],
                              writes=[R_wgf[d * 4 + g * 2 + hh]], key="wgf")
            P.op("pool", lambda e: e.tensor_copy(wg[q].rearrange("p a b c -> p (a b c)"),
                                                 wgf.rearrange("p a b c -> p (a b c)")),
                 reads=R_wgf, writes=[R_wg[q]])
            P.dma("sp", padc[q][:, 2:258], zxa[cc, :, 0:256], reads=[R_zxa[cc]], writes=[R_padc[q]], key=f"padc{q}")
            P.dma("sp", padl[q][:, 2:2 + L], zxa[cc, :, 256:NTOK], reads=[R_zxa[cc]], writes=[R_padl[q]], key=f"padl{q}")
            for k in range(4):
                P.op("pool", lambda e, k=k: e.tensor_scalar(dk[q][:, k, :], ident, cw[:, l, cc, k:k + 1], None, ALU.mult),
                     reads=[R_ident, R_sm], writes=[R_dk[q]])

        def do_cc(cc):
            q = cc % 2
            if not last:
                P.dma("sp", gga, zga[cc], reads=[R_zga[cc]], writes=[R_gga], key="gga")
            else:
                P.dma("sp", gga[:, 256:NTOK], zga[cc, :, 256:NTOK], reads=[R_zga[cc]], writes=[R_gga], key="gga")
            for si, (t0, n) in enumerate(segs):
                if si == 0:
                    src = lambda k, t0=t0, n=n: padc[q][:, k:k + n]
                else:
                    src = lambda k, t0=t0, n=n: padl[q][:, t0 - 256 + k:t0 - 256 + k + n]
                bank = 6
                ps = PS[bank]
                def emit(e, src=src, n=n, ps=ps):
                    for k in range(4):
                        ins = e.matmul(ps[:, 0:n], dk[q][:, k, :], src(k), start=(k == 0), stop=(k == 3))
                    return ins
                P.op("pe", emit, reads=[R_dk[q], R_padc[q] if si == 0 else R_padl[q]], writes=[R_ps[bank]])
                P.op("dve", lambda e, t0=t0, n=n, ps=ps: e.tensor_scalar(xc[:, t0:t0 + n], ps[:, 0:n], cb[:, l, cc:cc + 1], None,
                                                                        ALU.add),
                     reads=[R_ps[bank], R_sm], writes=[R_xc[si]])
                P.op("dve", lambda e, t0=t0, n=n: e.tensor_copy(xcb[:, t0:t0 + n], xc[:, t0:t0 + n]),
                     reads=[R_xc[si]], writes=[R_xcb[si]])
                for _ in range(1):
                    if mod_next:
                        mod_next.pop(0)()

            if cc + 1 < 4:
                prefetch(cc + 1)

            def do_dir(d):
                Ab, Sb, Bb = Ab2[d], Sb2[d], Bb2[d]
                R_A, R_S, R_B = R_A2[d], R_S2[d], R_B2[d]
                for si in range(9):
                    t0, n = segs[si]
                    bank = 2 * (cntbox[0] % 3)
                    b2 = cntbox[0] % 2
                    cntbox[0] += 1
                    psr, psi = PS[bank], PS[bank + 1]
                    P.op("pe", lambda e, psr=psr, t0=t0, n=n: e.matmul(psr[:, 0:n], wg[q][:, d, 0, :], xcb[:, t0:t0 + n],
                                                                      start=True, stop=True),
                         reads=[R_wg[q], R_xcb[si]], writes=[R_ps[bank]])
                    P.op("pe", lambda e, psi=psi, t0=t0, n=n: e.matmul(psi[:, 0:n], wg[q][:, d, 1, :], xcb[:, t0:t0 + n],
                                                                      start=True, stop=True),
                         reads=[R_wg[q], R_xcb[si]], writes=[R_ps[bank + 1]])
                    P.op("act", lambda e, psr=psr, n=n, b2=b2: e.activation(
                        thr[b2][:, 0:n], psr[:, 0:n], AF.Tanh, bias=HB[:, l, d, 0, cc:cc + 1], scale=0.5),
                        reads=[R_ps[bank], R_const], writes=[R_thr[b2]])
                    P.op("act", lambda e, psi=psi, n=n, b2=b2: e.activation(
                        thi[b2][:, 0:n], psi[:, 0:n], AF.Tanh, bias=HB[:, l, d, 1, cc:cc + 1], scale=0.5),
                        reads=[R_ps[bank + 1], R_const], writes=[R_thi[b2]])
                    P.op("act", lambda e, t0=t0, n=n, b2=b2: e.activation(
                        Ab[:, t0:t0 + n], thr[b2][:, 0:n], AF.Exp, bias=CH2[:, l, d, cc:cc + 1], scale=CH2[:, l, d, cc:cc + 1]),
                        reads=[R_thr[b2], R_const], writes=[R_A[si]])
                    P.op("dve", lambda e, t0=t0, n=n: e.scalar_tensor_tensor(
                        Sb[:, t0:t0 + n], Ab[:, t0:t0 + n], -1.0, Ab[:, t0:t0 + n], ALU.mult, ALU.mult),
                        reads=[R_A[si]], writes=[R_S[si]])
                    P.op("dve", lambda e, t0=t0, n=n, b2=b2: e.scalar_tensor_tensor(
                        Bb[:, t0:t0 + n], thi[b2][:, 0:n], 1.0, xc[:, t0:t0 + n], ALU.add, ALU.mult),
                        reads=[R_thi[b2], R_xc[si]], writes=[R_B[si]])
                for (a0, a1, rs) in [(0, 2304, R_S[0:5]), (2304, NTOK, R_S[5:9])]:
                    P.op("act", lambda e, a0=a0, a1=a1: e.activation(Sb[:, a0:a1], Sb[:, a0:a1], AF.Sqrt, bias=0.25, scale=0.25),
                         reads=rs, writes=rs)
                for si in range(9):
                    t0, n = segs[si]
                    P.op("dve", lambda e, t0=t0, n=n: e.tensor_tensor(Bb[:, t0:t0 + n], Bb[:, t0:t0 + n], Sb[:, t0:t0 + n], ALU.mult),
                         reads=[R_S[si], R_B[si]], writes=[R_B[si]])
                if d == 0:
                    P.op("dve", lambda e: e.tensor_tensor_scan(Hf[:, 0:256], Ab[:, 0:256], Bb[:, 0:256], 0.0, ALU.mult, ALU.add),
                         reads=[R_A[0], R_B[0]], writes=[R_Hf])
                    P.op("dve", lambda e: e.tensor_tensor_scan(Hf[:, 256:NTOK], Ab[:, 256:NTOK], Bb[:, 256:NTOK], Hf[:, 255:256],
                                                              ALU.mult, ALU.add),
                         reads=R_A[1:] + R_B[1:] + [R_Hf], writes=[R_Hf])
                else:
                    P.op("dve", lambda e: e.tensor_tensor_scan(Hb[:, 0:256][:, ::-1], Ab[:, 0:256][:, ::-1],
                                                              Bb[:, 0:256][:, ::-1], 0.0, ALU.mult, ALU.add),
                         reads=[R_A[0], R_B[0]], writes=[R_S[0]])
                    P.op("dve", lambda e: e.tensor_tensor_scan(Hb[:, 256:NTOK][:, ::-1], Ab[:, 256:NTOK][:, ::-1],
                                                              Bb[:, 256:NTOK][:, ::-1], Hb[:, 0:1], ALU.mult, ALU.add),
                         reads=R_A[1:] + R_B[1:] + [R_S[0]], writes=R_S[1:])

            for d in range(2):
                do_dir(d)
            c0 = 256 if last else 0
            for (a0, a1) in [(c0, 2304), (2304, NTOK)]:
                P.op("dve", lambda e, a0=a0, a1=a1: e.tensor_tensor(Hf[:, a0:a1], Hf[:, a0:a1], Hb[:, a0:a1], ALU.add),
                     reads=[R_Hf] + R_S2[1], writes=[R_Hf])
                P.op("pool", lambda e, a0=a0, a1=a1: e.tensor_tensor(yst[:, a0:a1], Hf[:, a0:a1], gga[:, a0:a1], ALU.mult),
                     reads=[R_Hf, R_gga], writes=R_B2[1])
            b0 = 1 if last else 0
            P.dma("sp", yT[cc, :, b0 * 256:NTOK],
                  yst[:, b0 * 256:NTOK], reads=R_B2[1],
                  writes=[R_y[b][cc] for b in range(b0, NB)], key="ysta")

        prefetch(0)
        for cc in range(4):
            do_cc(cc)
        while mod_next:
            mod_next.pop(0)()

    def gmlp_phase(l, reset=True):
        if reset:
            A.off = persist_off
        last = l == DEPTH - 1
        wsT = A.alloc([4, 128], BF16)
        R_wsl = [Res(f"wsT{g}") for g in range(4)]
        for g in range(4):
            P.dma("pool", wsT[:, g, :], wsT_d[l, g], writes=[R_wsl[g]], key="wsT")
        gu_ = A.alloc([2, NTOK], BF16)
        R_gul = [Res("gmlp_u0"), Res("gmlp_u1")]
        for j in range(2):
            P.dma("sp", gu_[:, j, :], zu[j], reads=[R_zu[j]], writes=[R_gul[j]], key="gmu")
        vb = [A.alloc([4, 256], BF16) for _ in range(2)]
        R_vb = [Res("vb0"), Res("vb1")]
        tm = [A.alloc([512], F32) for _ in range(2)]
        R_tm = [Res("gtm0"), Res("gtm1")]
        yb = [A.alloc([2, 512], BF16) for _ in range(2)]
        R_yb = [[Res(f"gyb{i}_{j}") for j in range(2)] for i in range(2)]
        bsbc = S("bsbc").rearrange("p (l j n) -> p l j n", l=2, j=2)
        batches = [] if last else [(0, 2)]
        batches += [(2 + 4 * k, 4) for k in range(8)]
        for bi, (c0, nch) in enumerate(batches):
            sl = bi % 2
            n = nch * 128
            t0 = c0 * 128
            P.dma("sp", vb[sl][:, 0:nch, :], zv[:, c0:c0 + nch, :], reads=[R_zv], writes=[R_vb[sl]],
                  key=f"vb{sl}")
            for j in range(2):
                bank = (2 * bi + j) % 4
                ps = PS[bank]
                def emit(e, j=j, ps=ps, nch=nch, sl=sl):
                    for c4 in range(nch):
                        for gg in range(2):
                            g = 2 * j + gg
                            ins = e.matmul(ps[gg * 64:(gg + 1) * 64, c4 * 128:(c4 + 1) * 128],
                                           vb[sl][:, c4, g * 64:(g + 1) * 64], wsT[:, g, :], start=True, stop=True)
                    return ins
                P.op("pe", emit, reads=[R_vb[sl]] + R_wsl, writes=[R_ps[bank]])
                tj = (2 * bi + j) % 2
                P.op("dve", lambda e, ps=ps, j=j, n=n, nch=nch, tj=tj: e.tensor_tensor(
                    tm[tj][:, 0:n].rearrange("p (c q) -> p c q", q=128), ps[:, 0:n].rearrange("p (c q) -> p c q", q=128),
                    bsbc[:, l, j, :].unsqueeze(1).to_broadcast([128, nch, 128]), ALU.add),
                    reads=[R_ps[bank], R_sm], writes=[R_tm[tj]])
                P.op("dve", lambda e, j=j, n=n, t0=t0, tj=tj, sl=sl: e.tensor_tensor(
                    yb[sl][:, j, 0:n], tm[tj][:, 0:n], gu_[:, j, t0:t0 + n], ALU.mult),
                    reads=[R_tm[tj], R_gul[j]], writes=[R_yb[sl][j]])
            bb0 = t0 // T
            nb_ = n // T
            for j in range(2):
                P.dma("sp", yT[4 + j, :, t0:t0 + n],
                      yb[sl][:, j, 0:n], reads=[R_yb[sl][j]],
                      writes=[R_y[b][4 + j] for b in range(bb0, bb0 + nb_)], key=f"gyst{sl}{j}")

    def fourier_phase(l, mid=None):
        A.off = persist_off
        last = l == DEPTH - 1
        cs64 = A.alloc([384], BF16)
        dg = A.alloc([4, 128], BF16)
        R_cs = Res("cs64")
        R_dg = Res("dg")
        P.dma("sp", cs64, CS_d, writes=[R_cs])
        P.dma("sp", dg.rearrange("p a b -> p (a b)"), DG_d, writes=[R_dg])
        c0 = A.alloc([2, 32, 512], BF16)
        R_c0 = [[Res(f"c0_{cs}_{h}") for h in range(2)] for cs in range(2)]
        for cs in range(2):
            for h in range(2):
                P.dma("sp", c0[:, cs, 16 * h:16 * (h + 1), :].rearrange("p a b -> p (a b)"),
                      C0_d[cs, :, 16 * h * 512:16 * (h + 1) * 512], writes=[R_c0[cs][h]], key="c0")
        fT = A.alloc([2, NTOK], BF16)
        R_fTl = [Res("fT0"), Res("fT1")]
        for j in range(2):
            P.dma("sp", fT[:, j, :], zf[j], reads=[R_zf[j]], writes=[R_fTl[j]], key="fT")
        FCS = A.alloc([32, 2, 3, 128], BF16)
        R_F = [[Res(f"FCS{c}_{j}") for j in range(2)] for c in range(32)]
        UV = A.alloc([8, 2, 2, 512], BF16)
        R_UV = [[[Res(f"UV{r}_{j}_{u}") for u in range(2)] for j in range(2)] for r in range(8)]
        ost = [A.alloc([512], BF16) for _ in range(2)]
        R_ost = [Res("fo0"), Res("fo1")]
        ev = [0]

        def evac(dst, src, reads, writes):
            if ev[0] % 2 == 0:
                P.op("act", lambda e: e.activation(dst, src, AF.Copy), reads=reads, writes=writes)
            else:
                P.op("dve", lambda e: e.tensor_copy(dst, src), reads=reads, writes=writes)
            ev[0] += 1

        oc = [0]
        if not last:
            FCSc = A.alloc([2, 2, 3, 128], BF16)
            R_Fc = Res("FCSc")
            clc = A.alloc([2, 2, 256], BF16)
            R_clc = Res("clc")
            P.dma("sp", clc.rearrange("p a b c -> p (a b c)"), CLc_d, writes=[R_clc])
            for i in range(2):
                for j in range(2):
                    bank = (2 * i + j) % 2
                    ps = PS[bank]
                    P.op("pe", lambda e, i=i, j=j, ps=ps: e.matmul(ps[:, 0:384], fT[:, j, i * 128:(i + 1) * 128], cs64,
                                                                 start=True, stop=True),
                         reads=R_fTl + [R_cs], writes=[R_ps[bank]])
                    evac(FCSc[:, i, j].rearrange("p a b -> p (a b)"), ps[:, 0:384], [R_ps[bank]], [R_Fc])
            for j in range(2):
                bank = 2 + j
                ps = PS[bank]
                def emit(e, j=j, ps=ps):
                    k = 0
                    for cs in range(2):
                        for i in range(2):
                            ins = e.matmul(ps[:, 0:256], FCSc[:, i, j, cs, :], clc[:, cs, i, :], start=(k == 0), stop=(k == 3))
                            k += 1
                    return ins
                P.op("pe", emit, reads=[R_Fc, R_clc], writes=[R_ps[bank]])
                osl = oc[0] % 2
                oc[0] += 1
                evac(ost[osl][:, 0:256], ps[:, 0:256], [R_ps[bank]], [R_ost[osl]])
                P.dma("sp", yT[6 + j, :, 0:256], ost[osl][:, 0:256], reads=[R_ost[osl]],
                      writes=[R_y[0][6 + j]], key=f"fost{osl}")
        cnt = 0
        for c in range(32):
            r, qc = c // 4, c % 4
            tb = 256 + r + 1024 * qc
            for j in range(2):
                bank = cnt % 2
                cnt += 1
                ps = PS[bank]
                P.op("pe", lambda e, j=j, ps=ps, tb=tb: e.matmul(ps[:, 0:384], fT[:, j, tb:tb + 1017:8], cs64, start=True, stop=True),
                     reads=R_fTl + [R_cs], writes=[R_ps[bank]])
                evac(FCS[:, c, j].rearrange("p a b -> p (a b)"), ps[:, 0:384], [R_ps[bank]], [R_F[c][j]])
        if mid is not None:
            mid()
        cnt = 0
        for r in range(8):
            for j in range(2):
                for u in range(2):
                    bank = 2 + cnt % 4
                    cnt += 1
                    ps = PS[bank]
                    def emit(e, r=r, j=j, u=u, ps=ps):
                        k = 0
                        for qc in range(4):
                            c = 4 * r + qc
                            a0, a1 = (0, 1) if u == 0 else (1, 2)
                            e.matmul(ps[:, 0:512], FCS[:, c, j, a0, :], c0[:, 0, c, :], start=(k == 0), stop=False)
                            ins = e.matmul(ps[:, 0:512], FCS[:, c, j, a1, :], c0[:, 1, c, :], start=False, stop=(k == 3))
                            k += 1
                        return ins
                    P.op("pe", emit, reads=[R_F[4 * r + qc][j] for qc in range(4)] + R_c0[0] + R_c0[1],
                         writes=[R_ps[bank]])
                    evac(UV[:, r, j, u, :], ps[:, 0:512], [R_ps[bank]], [R_UV[r][j][u]])
        def cidx(v):
            if abs(v) < 1e-9:
                return None
            if abs(v - 1.0) < 1e-6:
                return 0
            if abs(v + 1.0) < 1e-6:
                return 1
            return 2 if v > 0 else 3
        cnt = 0
        for kb in range(8):
            for j in range(2):
                terms = []
                for r in range(8):
                    ang = 2.0 * np.pi * ((r * kb) % 8) / 8.0
                    iu = cidx(np.cos(ang))
                    iv = cidx(-np.sin(ang))
                    if iu is not None:
                        terms.append((iu, r, 0))
                    if iv is not None:
                        terms.append((iv, r, 1))
                bank = 6 + cnt % 2
                cnt += 1
                ps = PS[bank]
                def emit(e, j=j, ps=ps, terms=terms):
                    nt = len(terms)
                    for k, (ix, r, u) in enumerate(terms):
                        ins = e.matmul(ps[:, 0:512], dg[:, ix, :], UV[:, r, j, u, :], start=(k == 0), stop=(k == nt - 1))
                    return ins
                P.op("pe", emit, reads=[R_dg] + [R_UV[r][j][u] for (_, r, u) in terms], writes=[R_ps[bank]])
                osl = oc[0] % 2
                oc[0] += 1
                evac(ost[osl], ps[:, 0:512], [R_ps[bank]], [R_ost[osl]])
                bb0 = 1 + 2 * kb
                P.dma("sp", yT[6 + j, :, 256 + kb * 512:256 + (kb + 1) * 512],
                      ost[osl], reads=[R_ost[osl]],
                      writes=[R_y[bb0][6 + j], R_y[bb0 + 1][6 + j]], key=f"fost{osl}")

    def final_phase():
        A.off = persist_off
        hbuf = [A.alloc([KC, T], F32) for _ in range(3)]
        R_hbuf = [[Res(f"fh{i}_{m}") for m in range(KC)] for i in range(3)]
        obuf = [A.alloc([KC, T], F32) for _ in range(3)]
        R_ob = [[Res(f"fo{i}_{m}") for m in range(KC)] for i in range(3)]
        sq = [A.alloc([T], BF16) for _ in range(2)]
        R_sq = [Res("fsq0"), Res("fsq1")]
        rstd = [A.alloc([T], F32) for _ in range(2)]
        R_rstd = [Res("frs0"), Res("frs1")]
        fng = S("fng")
        for i in range(min(3, NB - 1)):
            P.dma("sp", hbuf[i].rearrange("p a b -> p (a b)"), hTb[1 + i], reads=[R_h[1 + i]], writes=R_hbuf[i], key=f"fl{i}")
        for i in range(NB - 1):
            b = 1 + i
            sl = i % 3
            for st in norm_steps(hbuf[sl], R_hbuf[sl], obuf[sl], R_ob[sl], sq, R_sq, rstd[i % 2], R_rstd[i % 2], None, None,
                                 i % 2, lambda m: fng[:, m:m + 1], None, mode="mixed"):
                st()
            o = P.dma("sp", outTb[b - 1], obuf[sl].rearrange("p a b -> p (a b)"), reads=R_ob[sl], key=f"fs{sl}")
            out_stores.append(o)
            if i + 3 < NB - 1:
                P.dma("sp", hbuf[sl].rearrange("p a b -> p (a b)"), hTb[b + 3], reads=[R_h[b + 3]], writes=R_hbuf[sl],
                      key=f"fl{sl}")

    phases = []
    phases.append(("prologue", prologue))
    for l in range(DEPTH):
        phases.append((f"ffn1_{l}", lambda l=l: ffn_phase(l, 0)))
        phases.append((f"mixin_{l}", lambda l=l: mixin_phase(l)))
        phases.append((f"lru_{l}", lambda l=l: lru_phase(l)))
        phases.append((f"fourier_{l}", lambda l=l: fourier_phase(l, mid=lambda: gmlp_phase(l, reset=False))))
        phases.append((f"ffn2_{l}", lambda l=l: ffn_phase(l, 2)))
    phases.append(("final", final_phase))
    for name, fn in phases:
        P.fence()
        fn()
        if stop_after is not None and name == stop_after:
            break
    finals = list(out_stores)
    for r in R_h + R_zxa + R_zga + R_zu + R_zf + [R_zv] + [x for row in R_y for x in row]:
        if r.w is not None and r.w.is_dma:
            finals.append(r.w)
    P.wait_all("sp", finals)
    P.emit()
    return nc, P


_CONST_CACHE = {}


def prepare_inputs(inputs):
    if "c" not in _CONST_CACHE:
        _CONST_CACHE["c"] = build_consts()
    C0, DG, CLc, CS = _CONST_CACHE["c"]
    f32 = lambda a: np.ascontiguousarray(np.asarray(a, np.float32))
    x = f32(inputs["x"])
    ctx = f32(inputs["ctx"])
    shared = {
        "w_mod": f32(inputs["w_mod"]),
        "ffn_w_gu": f32(inputs["ffn_w_gu"]),
        "ffn_w_down": f32(inputs["ffn_w_down"]),
        "w_in": f32(inputs["w_in"]),
        "w_out": f32(inputs["w_out"]),
        "lru_w_gates": f32(inputs["lru_w_gates"]),
        "gmlp_wsT": np.ascontiguousarray(np.transpose(f32(inputs["gmlp_ws"]), (0, 1, 3, 2))),
        "C0": C0, "DG": DG, "CLc": CLc, "CS": CS,
    }
    in_maps = []
    for b in range(8):
        xb = x[b].reshape(16, T, KC, 128).transpose(0, 3, 2, 1).reshape(16, 128, KC * T)
        cb = ctx[b].reshape(T, KC, 128).transpose(2, 1, 0).reshape(128, KC * T)
        m = dict(shared)
        m["xTb"] = np.ascontiguousarray(xb)
        m["ctxTb"] = np.ascontiguousarray(cb)
        m["smalls"] = build_smalls(b, inputs)
        in_maps.append(m)
    return in_maps


def kernel(**inputs):
    in_maps = prepare_inputs(inputs)
    nc, _ = build()
    res = run_bass_kernel_spmd(nc, in_maps, core_ids=list(range(8)))
    out = np.empty((8, L, D), np.float32)
    for b in range(8):
        o = np.asarray(res.results[b]["outTb"], np.float32).reshape(16, 128, KC, T)
        out[b] = o.transpose(0, 3, 2, 1).reshape(L, D)
    return out
```

```python
import contextlib
import numpy as np
import ml_dtypes
import concourse.bass as bass
import concourse.mybir as mybir
from concourse.bass_utils import run_bass_kernel_spmd

F32 = mybir.dt.float32
BF16 = mybir.dt.bfloat16
ALU = mybir.AluOpType
AF = mybir.ActivationFunctionType

D = 1024
KC = 8
FF = 2816
JC = 22
T = 256
NB = 17
NTOK = 4352
L = 4096
DEPTH = 2
EPS = 1e-6
SB_BASE = 16512
SB_END = 229344
ARENA_BYTES = SB_END - SB_BASE

SAME_ENGINE_SYNC = True


class Res:
    __slots__ = ("name", "w", "rs", "rd")

    def __init__(self, name):
        self.name = name
        self.w = None
        self.rs = {}
        self.rd = []


class Op:
    __slots__ = ("eng", "emit", "deps", "is_dma", "dsem", "needs_inc", "tok", "waits")

    def __init__(self, eng, emit, is_dma=False, dsem=None):
        self.eng = eng
        self.emit = emit
        self.deps = []
        self.is_dma = is_dma
        self.dsem = dsem
        self.needs_inc = False
        self.tok = None
        self.waits = []


class Prog:
    ENGS = ("pe", "act", "dve", "pool", "sp")

    def __init__(self, nc):
        self.nc = nc
        self.stack = contextlib.ExitStack()
        self.ops = []
        self.nsem = 0
        self.dsems = {}
        self.ntens = 0
        self.pending = {}
        self.last_c = {}
        self.dma_since = []

    def fence(self):
        deps = list(self.last_c.values()) + list(self.dma_since)
        self.dma_since = []
        for e in self.ENGS:
            self.pending[e] = list(self.pending.get(e, [])) + deps

    def sem(self, name):
        self.nsem += 1
        return self.stack.enter_context(self.nc.semaphore(f"{name}_{self.nsem}"))

    def dsem(self, key):
        if key not in self.dsems:
            self.dsems[key] = [self.sem("d"), 0]
        return self.dsems[key]

    def psum(self, shape, dtype=F32):
        self.ntens += 1
        return self.stack.enter_context(self.nc.psum_tensor(f"ps{self.ntens}", list(shape), dtype))

    def _track(self, op, reads, writes):
        deps = []
        for r in reads:
            if r.w is not None:
                deps.append(r.w)
        for w in writes:
            if w.w is not None and not (w.rs or w.rd):
                deps.append(w.w)
            deps.extend(w.rs.values())
            deps.extend(w.rd)
        if op.eng in self.pending:
            deps.extend(self.pending.pop(op.eng))
        seen = set()
        for d in deps:
            if id(d) in seen or d is op:
                continue
            seen.add(id(d))
            op.deps.append(d)
        if op.is_dma:
            self.dma_since.append(op)
        else:
            self.last_c[op.eng] = op
        for r in reads:
            if op.is_dma:
                r.rd.append(op)
            else:
                r.rs[op.eng] = op
        for w in writes:
            w.w = op
            w.rs = {}
            w.rd = []
        self.ops.append(op)
        return op

    def op(self, eng, emit, reads=(), writes=()):
        return self._track(Op(eng, emit), reads, writes)

    def dma(self, queue, out, in_, reads=(), writes=(), key=None):
        base = key if key is not None else (writes[0].name if writes else reads[0].name)
        ds = self.dsem(f"{base}@{queue}")
        o = Op(queue, lambda e, out=out, in_=in_: e.dma_start(out=out, in_=in_), is_dma=True, dsem=ds)
        return self._track(o, reads, writes)

    def wait_all(self, eng, ops):
        o = Op(eng, None)
        o.deps = list(ops)
        self.ops.append(o)

    def emit(self):
        nc = self.nc
        csem = {e: self.sem("c" + e) for e in self.ENGS}

        def skip(d, o):
            if d.is_dma:
                return False
            if d.eng == "pe" and o.eng == "pe":
                return True
            if (not SAME_ENGINE_SYNC) and d.eng == o.eng and not o.is_dma:
                return True
            return False

        for o in self.ops:
            for d in o.deps:
                if not skip(d, o) and not d.is_dma:
                    d.needs_inc = True
        cnt = {e: 0 for e in self.ENGS}
        for o in self.ops:
            if o.is_dma:
                o.dsem[1] += 16
                o.tok = (o.dsem[0], o.dsem[1])
            elif o.needs_inc:
                cnt[o.eng] += 1
                o.tok = (csem[o.eng], cnt[o.eng])
        seen = {e: {} for e in self.ENGS}
        issued = {}
        for o in self.ops:
            s = seen[o.eng]
            need = {}
            for d in o.deps:
                if d.tok is None or skip(d, o):
                    continue
                sem, val = d.tok
                k = id(sem)
                if d.is_dma and not o.is_dma:
                    val = max(val, issued.get(k, 0))
                if k not in need or need[k][1] < val:
                    need[k] = (sem, val)
            if o.is_dma:
                issued[id(o.tok[0])] = o.tok[1]
            for k, (sem, val) in need.items():
                if s.get(k, 0) >= val:
                    continue
                s[k] = val
                o.waits.append((sem, val))
        per = {e: [o for o in self.ops if o.eng == e] for e in self.ENGS}
        self.counts = {e: len(per[e]) for e in self.ENGS}
        self.sem_counts = cnt

        def run(engobj, lst):
            for o in lst:
                for sem, val in o.waits:
                    engobj.wait_ge(sem, val)
                if o.emit is None:
                    continue
                ins = o.emit(engobj)
                if o.is_dma:
                    ins.then_inc(o.dsem[0], 16)
                elif o.needs_inc:
                    ins.then_inc(o.tok[0], 1)

        with nc.Block() as block:
            @block.tensor
            def _(e):
                run(e, per["pe"])

            @block.scalar
            def _(e):
                run(e, per["act"])

            @block.vector
            def _(e):
                run(e, per["dve"])

            @block.gpsimd
            def _(e):
                run(e, per["pool"])

            @block.sync
            def _(e):
                run(e, per["sp"])
        self.stack.close()


class Arena:
    def __init__(self, nc):
        self.t = nc.alloc_sbuf_tensor_at("arena", [128, ARENA_BYTES // 4], F32, offset=SB_BASE)
        self.off = 0

    def alloc(self, shape, dtype, parts=128):
        esz = 4 if dtype == F32 else 2
        n = int(np.prod(shape))
        nb = (n * esz + 31) // 32 * 32
        o4 = self.off // 4
        v = self.t[0:parts, o4:o4 + nb // 4]
        if dtype != F32:
            v = v.bitcast(dtype)
        v = v[:, 0:n]
        if len(shape) == 2:
            v = v.rearrange("p (a b) -> p a b", b=shape[1])
        elif len(shape) == 3:
            v = v.rearrange("p (a b c) -> p a b c", b=shape[1], c=shape[2])
        elif len(shape) == 4:
            v = v.rearrange("p (a b c d) -> p a b c d", b=shape[1], c=shape[2], d=shape[3])
        self.off += nb
        assert self.off <= ARENA_BYTES, f"SBUF arena overflow {self.off} > {ARENA_BYTES}"
        return v


SM = {}
_o = 0
for _n, _w in [("c", 8), ("cctx", 8), ("bmod", 144), ("ng", 48), ("fng", 8), ("convw", 32), ("convb", 8),
               ("lrub", 32), ("lam", 16), ("pr", 256), ("pc", 256), ("bsbc", 512)]:
    SM[_n] = (_o, _w)
    _o += _w
NS = _o


def _pm(v):
    v = np.asarray(v, np.float32)
    n = v.shape[-1] // 128
    v = v.reshape(v.shape[:-1] + (n, 128))
    return np.moveaxis(v, -1, 0)


def build_smalls(b, inp):
    s = np.zeros((128, NS), np.float32)

    def put(name, arr):
        o, w = SM[name]
        s[:, o:o + w] = np.asarray(arr, np.float32).reshape(128, w)

    put("c", _pm(inp["c"][b]))
    put("cctx", _pm(inp["c_ctx"]))
    put("bmod", _pm(inp["b_mod"]))
    put("ng", _pm(inp["norm_g"]))
    put("fng", _pm(inp["final_norm_g"]))
    cw = _pm(inp["conv_w"])
    put("convw", np.transpose(cw, (0, 1, 3, 2)))
    put("convb", _pm(inp["conv_b"]))
    put("lrub", _pm(inp["lru_b_gates"]))
    put("lam", _pm(inp["lru_lambda"]))
    q = D // 4
    freqs = (1.0 / (10000.0 ** (np.arange(q, dtype=np.float32) / np.float32(q)))).astype(np.float32)
    pos = np.arange(64, dtype=np.float32)
    e = pos[:, None] * freqs[None, :]
    tab = np.concatenate([np.sin(e), np.cos(e)], axis=1).astype(np.float32)
    tabT = tab.T.reshape(4, 128, 64).transpose(1, 0, 2)
    put("pr", tabT)
    put("pc", tabT)
    bs = np.asarray(inp["gmlp_bs"], np.float32)
    bsbc = np.zeros((128, 2, 2, 128), np.float32)
    for l in range(2):
        for j in range(2):
            bsbc[0:64, l, j, :] = bs[l, 2 * j][None, :]
            bsbc[64:128, l, j, :] = bs[l, 2 * j + 1][None, :]
    put("bsbc", bsbc)
    return s


def build_consts():
    bf = ml_dtypes.bfloat16
    p = np.arange(128, dtype=np.int64)
    kp = np.arange(512, dtype=np.int64)
    C0 = np.empty((2, 128, 32, 512), np.float32)
    for r in range(8):
        for qc in range(4):
            n = 8 * (128 * qc + p) + r
            ang = (2.0 * np.pi / L) * ((n[:, None] * kp[None, :]) % L).astype(np.float64)
            C0[0, :, 4 * r + qc, :] = np.cos(ang) / 64.0
            C0[1, :, 4 * r + qc, :] = -np.sin(ang) / 64.0
    C0 = C0.reshape(2, 128, 32 * 512).astype(bf)
    DG = np.zeros((128, 4, 128), np.float32)
    eye = np.eye(128, dtype=np.float32)
    for i, v in enumerate((1.0, -1.0, np.sqrt(0.5), -np.sqrt(0.5))):
        DG[:, i, :] = eye * np.float32(v)
    DG = DG.reshape(128, 512).astype(bf)
    n2 = np.arange(256, dtype=np.int64)
    ang2 = (2.0 * np.pi / 256) * ((n2[:, None] * n2[None, :]) % 256).astype(np.float64)
    C2 = (np.cos(ang2) / 16.0).astype(np.float32).reshape(2, 128, 256)
    S2 = (-np.sin(ang2) / 16.0).astype(np.float32).reshape(2, 128, 256)
    CLc = np.stack([C2, S2], 0).transpose(2, 0, 1, 3).reshape(128, 2 * 2 * 256).astype(bf)
    m = np.arange(64, dtype=np.int64)
    a64 = (2.0 * np.pi / 64) * ((m[:, None] * m[None, :]) % 64).astype(np.float64)
    c64 = np.cos(a64) / 8.0
    s64 = np.sin(a64) / 8.0
    CS = np.zeros((128, 3, 128), np.float32)
    for g in range(2):
        CS[g * 64:(g + 1) * 64, 0, g * 64:(g + 1) * 64] = c64
        CS[g * 64:(g + 1) * 64, 1, g * 64:(g + 1) * 64] = s64
        CS[g * 64:(g + 1) * 64, 2, g * 64:(g + 1) * 64] = -c64
    CS = CS.reshape(128, 384).astype(bf)
    return C0, DG, CLc, CS


def build(stop_after=None, debug=False):
    nc = bass.Bass("TRN2", target_bir_lowering=False)
    P = Prog(nc)
    A = Arena(nc)

    def din(name, shape, dt=F32):
        return nc.dram_tensor(name, list(shape), dt, kind="ExternalInput").ap()

    skind = "ExternalOutput" if debug else "Internal"

    def dscratch(name, shape, dt):
        return nc.dram_tensor(name, list(shape), dt, kind=skind).ap()

    xTb = din("xTb", [16, 128, KC * T])
    ctxTb = din("ctxTb", [128, KC * T])
    smalls_d = din("smalls", [128, NS])
    w_mod = din("w_mod", [2, D, 9 * D])
    w_gu = din("ffn_w_gu", [2, 2, D, 2 * FF])
    w_dn = din("ffn_w_down", [2, 2, FF, D])
    w_in = din("w_in", [2, D, 1792])
    w_out = din("w_out", [2, D, D])
    w_gates = din("lru_w_gates", [2, 2, 2, 8, 64, 64])
    wsT_d = din("gmlp_wsT", [2, 4, 128, 128])
    C0_d = din("C0", [2, 128, 32 * 512], BF16)
    DG_d = din("DG", [128, 512], BF16)
    CLc_d = din("CLc", [128, 1024], BF16)
    CS_d = din("CS", [128, 384], BF16)
    outTb = nc.dram_tensor("outTb", [16, 128, KC * T], F32, kind="ExternalOutput").ap()

    hTb = dscratch("hTb", [NB, 128, KC * T], F32)
    zxa = dscratch("zxa", [4, 128, NTOK], BF16)
    zga = dscratch("zga", [4, 128, NTOK], BF16)
    zu = dscratch("zu", [2, 128, NTOK], BF16)
    zf = dscratch("zf", [2, 128, NTOK], BF16)
    zv = dscratch("zv", [128, 34, 256], BF16)
    yT = dscratch("yT", [KC, 128, NTOK], BF16)

    wgu_bf = [nc.dram_tensor(f"wgu_bf{i}", [128, KC * 2 * FF], BF16, kind="Internal").ap() for i in range(3)]
    wdn_bf = [nc.dram_tensor(f"wdn_bf{i}", [128, JC * D], BF16, kind="Internal").ap() for i in range(3)]
    R_wgu_bf = [[Res(f"wgubf{i}_{kc}") for kc in range(KC)] for i in range(3)]
    R_wdn_bf = [[Res(f"wdnbf{i}_{h}") for h in range(2)] for i in range(3)]
    JG = [(0, 6), (6, 12), (12, 17), (17, 22)]

    R_h = [Res(f"hTb{b}") for b in range(NB)]
    R_y = [[Res(f"yTb{b}_{m}") for m in range(KC)] for b in range(NB)]
    R_zxa = [Res(f"zxa{c}") for c in range(4)]
    R_zga = [Res(f"zga{c}") for c in range(4)]
    R_zu = [Res(f"zu{c}") for c in range(2)]
    R_zf = [Res(f"zf{c}") for c in range(2)]
    R_zv = Res("zv")
    out_stores = []

    sm = A.alloc([NS], F32)
    R_sm = Res("sm")

    def S(name):
        o, w = SM[name]
        return sm[:, o:o + w]

    MOD = A.alloc([2, 72, 2], F32)
    AV = A.alloc([2, 2, 3, 8], F32)
    GV = A.alloc([2, 2, 3, 8], F32)
    CH2 = A.alloc([2, 2, 4], F32)
    HB = A.alloc([2, 2, 2, 4], F32)
    SC = A.alloc([8, 2], BF16)
    sctmp = A.alloc([8, 2], F32)
    eps_t = A.alloc([1], F32)
    R_const = Res("const")
    R_modl = [Res("mod0"), Res("mod1")]
    R_sc = Res("silu_c")
    cur = {"l": 0}

    def RM():
        return R_modl[cur["l"]]
    persist_off = A.off

    PS = [P.psum([128, 512]) for _ in range(8)]
    R_ps = [Res(f"psbank{i}") for i in range(8)]

    P.dma("sp", sm, smalls_d, writes=[R_sm])
    P.op("dve", lambda e: e.memset(eps_t, EPS), writes=[R_const])

    def SHIFT(l, s, which, m):
        j = (3 * which) * 8 + m
        return MOD[:, l, j, s:s + 1]

    def prologue():
        P.op("act", lambda e: e.activation(sctmp[:, :, 0], S("c"), AF.Silu), reads=[R_sm], writes=[R_sc])
        P.op("act", lambda e: e.activation(sctmp[:, :, 1], S("cctx"), AF.Silu), reads=[R_sm], writes=[R_sc])
        P.op("dve", lambda e: e.tensor_copy(SC, sctmp), reads=[R_sc], writes=[R_sc])
        lamv = S("lam")
        ch2f = CH2.rearrange("p a b c -> p (a b c)")
        sp_x = A.alloc([16], F32)
        sp_z = A.alloc([16], F32)
        sp_z2 = A.alloc([16], F32)
        sp_p = A.alloc([16], F32)
        rc = [R_sm, R_const]
        P.op("act", lambda e: e.activation(sp_x, lamv, AF.Abs), reads=rc, writes=[R_const])
        P.op("act", lambda e: e.activation(sp_x, sp_x, AF.Exp, scale=-1.0), reads=rc, writes=[R_const])
        P.op("dve", lambda e: e.tensor_scalar(sp_z, sp_x, 2.0, None, ALU.add), reads=rc, writes=[R_const])
        P.op("dve", lambda e: e.reciprocal(sp_z, sp_z), reads=rc, writes=[R_const])
        P.op("dve", lambda e: e.tensor_tensor(sp_z, sp_z, sp_x, ALU.mult), reads=rc, writes=[R_const])
        P.op("dve", lambda e: e.tensor_tensor(sp_z2, sp_z, sp_z, ALU.mult), reads=rc, writes=[R_const])
        P.op("dve", lambda e: e.memset(sp_p, 1.0 / 15.0), reads=rc, writes=[R_const])
        for cf in (1.0 / 13, 1.0 / 11, 1.0 / 9, 1.0 / 7, 1.0 / 5, 1.0 / 3, 1.0):
            P.op("dve", lambda e: e.tensor_tensor(sp_p, sp_p, sp_z2, ALU.mult), reads=rc, writes=[R_const])
            P.op("dve", lambda e, cf=cf: e.tensor_scalar(sp_p, sp_p, float(cf), None, ALU.add), reads=rc, writes=[R_const])
        P.op("dve", lambda e: e.tensor_tensor(sp_p, sp_p, sp_z, ALU.mult), reads=rc, writes=[R_const])
        P.op("dve", lambda e: e.tensor_scalar(sp_x, lamv, -1.0, 0.0, ALU.mult, ALU.max), reads=rc, writes=[R_const])
        P.op("dve", lambda e: e.scalar_tensor_tensor(sp_p, sp_p, 2.0, sp_x, ALU.mult, ALU.add), reads=rc, writes=[R_const])
        P.op("dve", lambda e: e.tensor_scalar(ch2f, sp_p, -4.0, None, ALU.mult), reads=rc, writes=[R_const])
        hbf = HB.rearrange("p a b c d -> p (a b c d)")
        P.op("dve", lambda e: e.tensor_scalar(hbf, S("lrub"), 0.5, None, ALU.mult), reads=[R_sm], writes=[R_const])
        mark = A.off
        for st in mod_layer_steps(0, 0):
            st()
        A.off = mark

    def mod_layer_steps(l, bank, cw_=1152):
        NCH = 9216 // cw_
        JJ = cw_ // 128
        wbuf = [A.alloc([KC, cw_], BF16) for _ in range(2)]
        R_wb = [[Res(f"wmod{l}_{i}_{kc}") for kc in range(KC)] for i in range(2)]
        bm = S("bmod").rearrange("p (l j) -> p l j", l=2)
        ps = PS[bank]
        ng = S("ng").rearrange("p (l w m) -> p l w m", l=2, w=3)

        def chunk_dma(ch):
            bi = ch % 2
            P.dma("pool", wbuf[bi], w_mod[l].rearrange("(kc p) n -> p kc n", p=128)[:, :, ch * cw_:(ch + 1) * cw_],
                  writes=R_wb[bi], key=f"wmod{l}_{bi}")

        def chunk(ch):
            if ch == 0:
                chunk_dma(0)
            if ch + 1 < NCH:
                chunk_dma(ch + 1)
            bi = ch % 2
            for jj in range(JJ):
                j = ch * JJ + jj
                def emit(e, jj=jj, j=j):
                    for kc in range(KC):
                        ins = e.matmul(ps[:, 2 * j:2 * j + 2], wbuf[bi][:, kc, jj * 128:(jj + 1) * 128], SC[:, kc, :],
                                       start=(kc == 0), stop=(kc == KC - 1))
                    return ins
                P.op("pe", emit, reads=R_wb[bi] + [R_sc], writes=[R_ps[bank]])

        def epilogue():
            P.op("dve", lambda e: e.tensor_tensor(
                MOD[:, l], ps[:, 0:144].rearrange("p (j s) -> p j s", s=2),
                bm[:, l, :].unsqueeze(2).to_broadcast([128, 72, 2]), ALU.add),
                reads=[R_ps[bank], R_sm], writes=[R_modl[l]])
            for s_ in range(2):
                for w in range(3):
                    j0 = (3 * w + 1) * 8
                    P.op("dve", lambda e, s_=s_, w=w, j0=j0: e.scalar_tensor_tensor(
                        AV[:, l, s_, w, :], MOD[:, l, j0:j0 + 8, s_], 1.0, ng[:, l, w, :], ALU.add, ALU.mult),
                        reads=[R_modl[l], R_sm], writes=[R_modl[l]])
                    j2 = (3 * w + 2) * 8
                    P.op("dve", lambda e, s_=s_, w=w, j2=j2: e.tensor_scalar(
                        GV[:, l, s_, w, :], MOD[:, l, j2:j2 + 8, s_], (1.0 if w == 1 else 0.5), None, ALU.mult),
                        reads=[R_modl[l]], writes=[R_modl[l]])

        return [(lambda ch=ch: chunk(ch)) for ch in range(NCH)] + [epilogue]

    def norm_steps(hb, R_hb, nT, R_nT, sq, R_sq, rstd, R_rstd, tmp, R_tmp, stat_bank, scale_ap, shift_ap, n=T,
                   mode="pool"):
        ps = PS[stat_bank]
        steps = []

        nsq = len(sq)

        def sq_a(m0):
            for m in (m0, m0 + 1):
                if mode in ("pool", "split"):
                    P.op("pool", lambda e, m=m: e.tensor_tensor(sq[m % nsq][:, 0:n], hb[:, m, 0:n], hb[:, m, 0:n], ALU.mult),
                         reads=[R_hb[m]], writes=[R_sq[m % nsq]])
                else:
                    P.op("act", lambda e, m=m: e.activation(sq[m % nsq][:, 0:n], hb[:, m, 0:n], AF.Square),
                         reads=[R_hb[m]], writes=[R_sq[m % nsq]])

        def sq_b(m0):
            for m in (m0, m0 + 1):
                P.op("pe", lambda e, m=m: e.matmul(ps[:, 0:n], ones_mat, sq[m % nsq][:, 0:n],
                                                  start=(m == 0), stop=(m == KC - 1)),
                     reads=[R_sq[m % nsq], R_const], writes=[R_ps[stat_bank]])

        def sq_step(m0):
            sq_a(m0)
            sq_b(m0)

        def rstd_step():
            P.op("act", lambda e: e.activation(rstd[:, 0:n], ps[:, 0:n], AF.Ln, bias=eps_t[:, 0:1], scale=1.0 / D),
                 reads=[R_ps[stat_bank], R_const], writes=[R_rstd])
            P.op("act", lambda e: e.activation(rstd[:, 0:n], rstd[:, 0:n], AF.Exp, scale=-0.5),
                 reads=[R_rstd], writes=[R_rstd])

        def nrm_step(m):
            if mode == "split":
                P.op("dve", lambda e: e.tensor_tensor(tmp[m % 2][:, 0:n], hb[:, m, 0:n], rstd[:, 0:n], ALU.mult),
                     reads=[R_hb[m], R_rstd], writes=[R_tmp[m % 2]])
                P.op("pool", lambda e: e.tensor_scalar(nT[:, m, 0:n], tmp[m % 2][:, 0:n], scale_ap(m), shift_ap(m),
                                                       ALU.mult, ALU.add),
                     reads=[R_tmp[m % 2], RM()], writes=[R_nT[m]])
                return
            if mode != "pool":
                if shift_ap is None:
                    P.op("dve", lambda e: e.scalar_tensor_tensor(nT[:, m, 0:n], hb[:, m, 0:n], scale_ap(m), rstd[:, 0:n],
                                                                 ALU.mult, ALU.mult),
                         reads=[R_hb[m], R_rstd, RM()], writes=[R_nT[m]])
                    return
                P.op("dve", lambda e: e.tensor_tensor(tmp[m % 2][:, 0:n], hb[:, m, 0:n], rstd[:, 0:n], ALU.mult),
                     reads=[R_hb[m], R_rstd], writes=[R_tmp[m % 2]])
                P.op("act", lambda e: e.activation(nT[:, m, 0:n], tmp[m % 2][:, 0:n], AF.Identity,
                                                   bias=shift_ap(m), scale=scale_ap(m)),
                     reads=[R_tmp[m % 2], RM()], writes=[R_nT[m]])
                return
            P.op("pool", lambda e: e.tensor_tensor(tmp[m % 2][:, 0:n], hb[:, m, 0:n], rstd[:, 0:n], ALU.mult),
                 reads=[R_hb[m], R_rstd], writes=[R_tmp[m % 2]])
            if shift_ap is not None:
                P.op("pool", lambda e: e.tensor_scalar(nT[:, m, 0:n], tmp[m % 2][:, 0:n], scale_ap(m), shift_ap(m),
                                                       ALU.mult, ALU.add),
                     reads=[R_tmp[m % 2], RM()], writes=[R_nT[m]])
            else:
                P.op("pool", lambda e: e.tensor_scalar(nT[:, m, 0:n], tmp[m % 2][:, 0:n], scale_ap(m), None, ALU.mult),
                     reads=[R_tmp[m % 2], RM()], writes=[R_nT[m]])

        if nsq >= 8:
            for fn, m0 in ((sq_a, 0), (sq_a, 2), (sq_b, 0), (sq_a, 4), (sq_b, 2), (sq_a, 6), (sq_b, 4), (sq_b, 6)):
                steps.append(lambda fn=fn, m0=m0: fn(m0))
        else:
            for m0 in (0, 2, 4, 6):
                steps.append(lambda m0=m0: sq_step(m0))
        steps.append(rstd_step)
        for m in range(KC):
            steps.append(lambda m=m: nrm_step(m))
        return steps

    def norm_block(*a, **k):
        for st in norm_steps(*a, **k):
            st()

    ones_mat = A.alloc([128], BF16)
    persist_off = A.off
    P.op("dve", lambda e: e.memset(ones_mat, 1.0), writes=[R_const])

    def make_precast(si):
        fi = si + 1
        nl, nw = fi // 2, fi % 2
        dst_gu = wgu_bf[si].rearrange("p (kc n) -> p kc n", kc=KC)
        dst_dn = wdn_bf[si].rearrange("p (j n) -> p j n", j=JC)
        nsrc_dn = w_dn[nl, nw].rearrange("(j p) n -> p j n", p=128)
        parts = []
        for kc in range(KC):
            parts.append(lambda kc=kc: P.dma("pool", dst_gu[:, kc, :], w_gu[nl, nw, kc * 128:(kc + 1) * 128, :],
                                             writes=[R_wgu_bf[si][kc]], key="pcgu"))
        for h in range(2):
            parts.append(lambda h=h: P.dma("pool", dst_dn[:, 11 * h:11 * (h + 1), :], nsrc_dn[:, 11 * h:11 * (h + 1), :],
                                           writes=[R_wdn_bf[si][h]], key=f"pcdn{h}"))
        return parts

    def ffn_phase(l, which):
        cur["l"] = l
        A.off = persist_off
        first = (l == 0 and which == 0)
        is_c = which == 2
        last = (l == DEPTH - 1) and is_c
        wsel = 0 if which == 0 else 1
        wgu = A.alloc([KC, 2 * FF], BF16)
        wdn = A.alloc([JC, D], BF16)
        idx = 2 * l + (1 if is_c else 0)
        R_wgu_g = [[Res(f"wgu_g{g}_{u}") for u in range(2)] for g in range(len(JG))]
        R_wdn = [Res(f"wdn{h}") for h in range(2)]
        jgroup = {}
        for g, (j0, j1) in enumerate(JG):
            for j in range(j0, j1):
                jgroup[j] = g
        use_scratch = is_c
        if not use_scratch:
            src_gu = w_gu[l, wsel].rearrange("(kc p) n -> p kc n", p=128)
            src_dn = w_dn[l, wsel].rearrange("(j p) n -> p j n", p=128)
            wqs = ("pool", "pool")
            rd_gu, rd_dn = [], [[], []]
        else:
            src_gu = wgu_bf[idx - 1].rearrange("p (kc n) -> p kc n", kc=KC)
            src_dn = wdn_bf[idx - 1].rearrange("p (j n) -> p j n", j=JC)
            wqs = ("sp", "act")
            rd_gu, rd_dn = R_wgu_bf[idx - 1], [[R_wdn_bf[idx - 1][0]], [R_wdn_bf[idx - 1][1]]]

        def load_weights():
            for g, (j0, j1) in enumerate(JG):
                for u in range(2):
                    c0, c1 = u * FF + j0 * 128, u * FF + j1 * 128
                    P.dma(wqs[u], wgu[:, :, c0:c1], src_gu[:, :, c0:c1], reads=rd_gu, writes=[R_wgu_g[g][u]],
                          key=f"wgu{g}_{u}")
            for h in range(2):
                P.dma(wqs[h], wdn[:, 11 * h:11 * (h + 1), :], src_dn[:, 11 * h:11 * (h + 1), :], reads=rd_dn[h],
                      writes=[R_wdn[h]], key=f"wdn{h}")
        if is_c:
            wo = A.alloc([KC, D], BF16)
            R_wo = [Res(f"wo{kc}") for kc in range(KC)]
            for kc in range(KC):
                P.dma("pool", wo[:, kc, :], w_out[l, kc * 128:(kc + 1) * 128, :], writes=[R_wo[kc]], key="wo")
            ybuf = [A.alloc([KC, T], BF16) for _ in range(2)]
            R_yb = [Res(f"ybuf{i}") for i in range(2)]
        NH = 2 if is_c else 3
        hbuf = [A.alloc([KC, T], F32) for _ in range(NH)]
        R_hbuf = [[Res(f"hbuf{i}_{m}") for m in range(KC)] for i in range(NH)]
        nT = [A.alloc([KC, T], BF16) for _ in range(2)]
        R_nT = [[Res(f"nT{i}_{m}") for m in range(KC)] for i in range(2)]
        hid = A.alloc([JC, T], BF16)
        R_hid = [Res(f"hid{j}") for j in range(JC)]
        sq = [A.alloc([T], BF16) for _ in range(8)]
        R_sq = [Res(f"sq{i}") for i in range(8)]
        tmp = [A.alloc([T], F32) for _ in range(2)]
        R_tmp = [Res("tmp0"), Res("tmp1")]
        rstd = A.alloc([T], F32)
        R_rstd = Res("rstd")
        sg = [A.alloc([T], BF16) for _ in range(3)]
        R_sg = [Res(f"sg{i}") for i in range(3)]
        blocks = list(range(NB))
        if last:
            blocks = list(range(1, NB))
        last = False

        def load(b, slot):
            load1(b, slot)
            load2(b, slot)

        def load1(b, slot):
            rs = R_hbuf[slot]
            if first:
                src = ctxTb if b == 0 else xTb[b - 1]
                P.dma("sp", hbuf[slot].rearrange("p a b -> p (a b)"), src, writes=rs, key=f"hload{slot}")

        def pos_steps(b, slot):
            rs = R_hbuf[slot]
            steps = []
            if first and b > 0:
                def f1():
                    r0 = (b - 1) * 4
                    pr = S("pr").rearrange("p (m r) -> p m r", m=4)
                    pc = S("pc").rearrange("p (m r) -> p m r", m=4)
                    for m in range(4):
                        hv = hbuf[slot][:, m, :].rearrange("p (r c) -> p r c", c=64)
                        P.op("dve", lambda e, hv=hv, m=m, r0=r0: e.tensor_tensor(
                            hv, hv, pr[:, m, r0:r0 + 4].unsqueeze(2).to_broadcast([128, 4, 64]), ALU.add),
                            reads=[rs[m], R_sm], writes=[rs[m]])
                    for m in range(4, 8):
                        hv = hbuf[slot][:, m, :].rearrange("p (r c) -> p r c", c=64)
                        P.op("dve", lambda e, hv=hv, m=m: e.tensor_tensor(
                            hv, hv, pc[:, m - 4, :].unsqueeze(1).to_broadcast([128, 4, 64]), ALU.add),
                            reads=[rs[m], R_sm], writes=[rs[m]])
                steps.append(f1)
            return steps

        def load2(b, slot):
            rs = R_hbuf[slot]
            if not first:
                P.dma("sp", hbuf[slot].rearrange("p a b -> p (a b)"), hTb[b], reads=[R_h[b]], writes=rs,
                      key=f"hload{slot}")
            if is_c:
                P.dma("sp", ybuf[slot], yT[:, :, b * T:(b + 1) * T].rearrange("m p t -> p m t"), reads=R_y[b], writes=[R_yb[slot]],
                      key=f"yload{slot}")

        def wout_m(b, slot, m):
            s = 0 if b > 0 else 1
            if True:
                bank = 6 + (m % 2)
                ps = PS[bank]
                def emit(e, m=m, ps=ps):
                    for kc in range(KC):
                        ins = e.matmul(ps[:, 0:T], wo[:, kc, m * 128:(m + 1) * 128], ybuf[slot][:, kc, :],
                                       start=(kc == 0), stop=(kc == KC - 1))
                    return ins
                P.op("pe", emit, reads=R_wo + [R_yb[slot]], writes=[R_ps[bank]])
                P.op("dve", lambda e, m=m, ps=ps, s=s: e.scalar_tensor_tensor(
                    hbuf[slot][:, m, :], ps[:, 0:T], GV[:, l, s, 1, m:m + 1], hbuf[slot][:, m, :], ALU.mult, ALU.add),
                    reads=[R_ps[bank], RM(), R_hbuf[slot][m]], writes=[R_hbuf[slot][m]])

        def pre_steps(b, slot, ns):
            s = 0 if b > 0 else 1
            steps = pos_steps(b, slot)
            if is_c:
                for m in range(KC):
                    steps.append(lambda m=m: wout_m(b, slot, m))
            steps += norm_steps(hbuf[slot], R_hbuf[slot], nT[ns], R_nT[ns], sq, R_sq, rstd, R_rstd, tmp, R_tmp, 0,
                                lambda m: AV[:, l, s, which, m:m + 1], lambda m: SHIFT(l, s, which, m))
            return steps

        def gu(b, slot, j, cnt):
            bank = 1 + (cnt % 3)
            ps = PS[bank]
            def emit(e, j=j, ps=ps):
                for kc in range(KC):
                    e.matmul(ps[:, 0:T], wgu[:, kc, j * 128:(j + 1) * 128], nT[slot][:, kc, :],
                             start=(kc == 0), stop=(kc == KC - 1))
                for kc in range(KC):
                    ins = e.matmul(ps[:, T:2 * T], wgu[:, kc, FF + j * 128:FF + (j + 1) * 128], nT[slot][:, kc, :],
                                   start=(kc == 0), stop=(kc == KC - 1))
                return ins
            P.op("pe", emit, reads=R_wgu_g[jgroup[j]] + R_nT[slot], writes=[R_ps[bank]])
            si = cnt % 3
            P.op("act", lambda e, ps=ps, si=si: e.activation(sg[si], ps[:, 0:T], AF.Silu),
                 reads=[R_ps[bank]], writes=[R_sg[si]])
            P.op("dve", lambda e, ps=ps, si=si, j=j: e.tensor_tensor(hid[:, j, :], sg[si], ps[:, T:2 * T], ALU.mult),
                 reads=[R_sg[si], R_ps[bank]], writes=[R_hid[j]])

        def down(b, slot, m):
            s = 0 if b > 0 else 1
            bank = 4 + (m % 2)
            ps = PS[bank]
            def emit(e, m=m, ps=ps):
                for j in range(JC):
                    ins = e.matmul(ps[:, 0:T], wdn[:, j, m * 128:(m + 1) * 128], hid[:, j, :],
                                   start=(j == 0), stop=(j == JC - 1))
                return ins
            P.op("pe", emit, reads=R_wdn + R_hid, writes=[R_ps[bank]])
            P.op("dve", lambda e, m=m, ps=ps, s=s: e.scalar_tensor_tensor(
                hbuf[slot][:, m, :], ps[:, 0:T], GV[:, l, s, which, m:m + 1], hbuf[slot][:, m, :], ALU.mult, ALU.add),
                reads=[R_ps[bank], RM(), R_hbuf[slot][m]], writes=[R_hbuf[slot][m]])

        def store(b, slot):
            if last:
                fng = S("fng")
                norm_block(hbuf[slot], R_hbuf[slot], hbuf[slot], R_hbuf[slot], sq, R_sq, rstd, R_rstd, tmp,
                           R_tmp, 0, lambda m: fng[:, m:m + 1], None)
                o = P.dma("sp", outTb[b - 1], hbuf[slot].rearrange("p a b -> p (a b)"), reads=R_hbuf[slot],
                          key=f"hstore{slot}")
                out_stores.append(o)
            else:
                P.dma("sp", hTb[b], hbuf[slot].rearrange("p a b -> p (a b)"), reads=R_hbuf[slot], writes=[R_h[b]],
                      key=f"hstore{slot}")

        nblk = len(blocks)
        for i in range(min(NH, nblk)):
            load(blocks[i], i)
        load_weights()
        pc_parts = make_precast(idx) if (not is_c) else []
        for st in pre_steps(blocks[0], 0, 0):
            st()
        cnt = 0
        for i, b in enumerate(blocks):
            slot = i % NH
            ns = i % 2
            if i >= 2 and pc_parts:
                pc_parts.pop(0)()
            pend = pre_steps(blocks[i + 1], (i + 1) % NH, 1 - ns) if i + 1 < nblk else []
            j0 = 8
            per_j = 2 if is_c else 1
            for j in range(JC):
                gu(b, ns, j, cnt)
                cnt += 1
                if j >= j0:
                    for _ in range(per_j):
                        if pend:
                            pend.pop(0)()
            while pend:
                pend.pop(0)()
            for m in range(KC):
                down(b, slot, m)
            store(b, slot)
            if i + NH < nblk:
                load(blocks[i + NH], slot)

    def mixin_phase(l):
        cur["l"] = l
        A.off = persist_off
        win = A.alloc([KC, 1792], BF16)
        R_win = [Res(f"win{kc}") for kc in range(KC)]
        for kc in range(KC):
            P.dma("pool", win[:, kc, :], w_in[l, kc * 128:(kc + 1) * 128, :], writes=[R_win[kc]], key="win")
        hbuf = [A.alloc([KC, T], F32) for _ in range(3)]
        R_hbuf = [[Res(f"a2h{i}_{m}") for m in range(KC)] for i in range(3)]
        nT = [A.alloc([KC, T], BF16) for _ in range(3)]
        R_nT = [[Res(f"a2n{i}_{m}") for m in range(KC)] for i in range(3)]
        sq = [A.alloc([T], BF16) for _ in range(8)]
        R_sq = [Res(f"a2sq{i}") for i in range(8)]
        tmp = [A.alloc([T], F32) for _ in range(2)]
        R_tmp = [Res("a2t0"), Res("a2t1")]
        rstd = A.alloc([T], F32)
        R_rstd = Res("a2rstd")
        GB = 4
        zst = [A.alloc([12, GB * T], BF16) for _ in range(2)]
        R_zst = [[Res(f"zst{i}_{c}") for c in range(12)] for i in range(2)]
        zvst = [A.alloc([2 * GB, 256], BF16) for _ in range(2)]
        R_zvst = [[Res(f"zvst{i}_{c}") for c in range(2 * GB)] for i in range(2)]
        fm_cols = [0, 1, 2, 3, 4, 5, 6, 7, 8, 9, 12, 13]
        cnt = 0

        def load(b, slot):
            P.dma("sp", hbuf[slot].rearrange("p a b -> p (a b)"), hTb[b], reads=[R_h[b]], writes=R_hbuf[slot],
                  key=f"a2load{slot}")

        def norm(b, slot):
            s = 0 if b > 0 else 1
            return norm_steps(hbuf[slot], R_hbuf[slot], nT[slot], R_nT[slot], sq, R_sq, rstd, R_rstd, tmp, R_tmp, 0,
                              lambda m: AV[:, l, s, 1, m:m + 1], lambda m: SHIFT(l, s, 1, m), mode="split")

        load(0, 0)
        load(1, 1)
        load(2, 2)
        for st in norm(0, 0):
            st()
        for st in norm(1, 1):
            st()
        cntbox = [0]

        mod_next = []

        def do_block(b, slot, zs):
            if b >= 1 and mod_next:
                mod_next.pop(0)()
            gi = b % GB
            g0 = (b // GB) * GB
            gn = min(GB, NB - g0)
            co = gi * T
            pend = norm(b + 2, (b + 2) % 3) if b + 2 < NB else []
            for ci, c in enumerate(fm_cols):
                bank = 1 + (cntbox[0] % 4)
                cntbox[0] += 1
                ps = PS[bank]
                def emit(e, c=c, ps=ps):
                    for kc in range(KC):
                        ins = e.matmul(ps[:, 0:T], win[:, kc, c * 128:(c + 1) * 128], nT[slot][:, kc, :],
                                       start=(kc == 0), stop=(kc == KC - 1))
                    return ins
                P.op("pe", emit, reads=R_win + R_nT[slot], writes=[R_ps[bank]])
                if 4 <= ci < 10:
                    P.op("act", lambda e, ps=ps, ci=ci: e.activation(zst[zs][:, ci, co:co + T], ps[:, 0:T], AF.Gelu_apprx_tanh),
                         reads=[R_ps[bank]], writes=[R_zst[zs][ci]])
                else:
                    P.op("dve", lambda e, ps=ps, ci=ci: e.tensor_copy(zst[zs][:, ci, co:co + T], ps[:, 0:T]),
                         reads=[R_ps[bank]], writes=[R_zst[zs][ci]])
                for _ in range(2 if ci >= 7 else 1):
                    if pend:
                        pend.pop(0)()
            for hf in range(2):
                bank = 5 + hf
                ps = PS[bank]
                def emit(e, hf=hf, ps=ps):
                    for kc in range(KC):
                        ins = e.matmul(ps[:, 0:256], nT[slot][:, kc, hf * 128:(hf + 1) * 128], win[:, kc, 1280:1536],
                                       start=(kc == 0), stop=(kc == KC - 1))
                    return ins
                P.op("pe", emit, reads=R_win + R_nT[slot], writes=[R_ps[bank]])
                P.op("act", lambda e, ps=ps, hf=hf: e.activation(zvst[zs][:, 2 * gi + hf, :], ps[:, 0:256], AF.Gelu_apprx_tanh),
                     reads=[R_ps[bank]], writes=[R_zvst[zs][2 * gi + hf]])
            while pend:
                pend.pop(0)()
            if gi == gn - 1:
                t0 = g0 * T
                nn = gn * T
                P.dma("sp", zxa[:, :, t0:t0 + nn].rearrange("c p t -> p c t"), zst[zs][:, 0:4, 0:nn],
                      reads=R_zst[zs][0:4], writes=R_zxa, key=f"zs0_{zs}")
                P.dma("sp", zga[:, :, t0:t0 + nn].rearrange("c p t -> p c t"), zst[zs][:, 4:8, 0:nn],
                      reads=R_zst[zs][4:8], writes=R_zga, key=f"zs1_{zs}")
                P.dma("sp", zu[:, :, t0:t0 + nn].rearrange("c p t -> p c t"), zst[zs][:, 8:10, 0:nn],
                      reads=R_zst[zs][8:10], writes=R_zu, key=f"zs2_{zs}")
                P.dma("sp", zf[:, :, t0:t0 + nn].rearrange("c p t -> p c t"), zst[zs][:, 10:12, 0:nn],
                      reads=R_zst[zs][10:12], writes=R_zf, key=f"zs3_{zs}")
                P.dma("sp", zv[:, 2 * g0:2 * g0 + 2 * gn, :], zvst[zs][:, 0:2 * gn, :],
                      reads=R_zvst[zs][0:2 * gn], writes=[R_zv], key=f"zs4_{zs}")
            if b + 3 < NB:
                load(b + 3, slot)

        for b in range(NB):
            do_block(b, b % 3, (b // GB) % 2)
        while mod_next:
            mod_next.pop(0)()

    def lru_phase(l):
        A.off = persist_off
        last = l == DEPTH - 1
        wg = [A.alloc([2, 2, 128], BF16) for _ in range(2)]
        wgf = A.alloc([2, 2, 128], F32)
        R_wg = [Res("wg0"), Res("wg1")]
        R_wgf = [Res(f"wgf{i}") for i in range(8)]
        ident = A.alloc([128], BF16)
        R_ident = Res("ident")
        P.dma("sp", ident, DG_d[:, 0:128], writes=[R_ident])
        dk = [A.alloc([4, 128], BF16) for _ in range(2)]
        R_dk = [Res("dk0"), Res("dk1")]
        padc = [A.alloc([2 + 256 + 1], BF16) for _ in range(2)]
        padl = [A.alloc([2 + L + 1], BF16) for _ in range(2)]
        R_padc = [Res("padc0"), Res("padc1")]
        R_padl = [Res("padl0"), Res("padl1")]
        xc = A.alloc([NTOK], F32)
        xcb = A.alloc([NTOK], BF16)
        Ab2 = [A.alloc([NTOK], F32) for _ in range(2)]
        Sb2 = [A.alloc([NTOK], F32) for _ in range(2)]
        Bb2 = [A.alloc([NTOK], F32) for _ in range(2)]
        Hf = A.alloc([NTOK], F32)
        Hb = Sb2[1]
        gga = A.alloc([NTOK], BF16)
        yst = Bb2[1][:, 0:NTOK // 2].bitcast(BF16)
        thr = [A.alloc([512], F32) for _ in range(2)]
        thi = [A.alloc([512], F32) for _ in range(2)]
        R_thr = [Res("thr0"), Res("thr1")]
        R_thi = [Res("thi0"), Res("thi1")]
        segs = [(0, 256)] + [(256 + 512 * k, 512) for k in range(8)]
        R_xc = [Res(f"xc{i}") for i in range(9)]
        R_xcb = [Res(f"xcb{i}") for i in range(9)]
        R_A2 = [[Res(f"A{q}_{i}") for i in range(9)] for q in range(2)]
        R_S2 = [[Res(f"S{q}_{i}") for i in range(9)] for q in range(2)]
        R_B2 = [[Res(f"B{q}_{i}") for i in range(9)] for q in range(2)]
        R_Hf = Res("Hf")
        R_gga = Res("gga")
        mod_next = mod_layer_steps(l + 1, 7, 384) if l + 1 < DEPTH else []
        cw = S("convw").rearrange("p (l c k) -> p l c k", l=2, c=4)
        cb = S("convb").rearrange("p (l c) -> p l c", l=2)
        for q in range(2):
            P.op("pool", lambda e, q=q: e.memset(padc[q], 0.0), writes=[R_padc[q]])
            P.op("pool", lambda e, q=q: e.memset(padl[q], 0.0), writes=[R_padl[q]])
        P.op("pool", lambda e: e.memset(wgf.rearrange("p a b c -> p (a b c)"), 0.0), writes=R_wgf)
        cntbox = [0]

        def prefetch(cc):
            q = cc % 2
            for d in range(2):
                for g in range(2):
                    for hh in range(2):
                        P.dma("sp", wgf[hh * 64:(hh + 1) * 64, d, g, hh * 64:(hh + 1) * 64], w_gates[l, d, g, 2 * cc + hh],
                              writes=[R_wgf[d * 4 + g * 2 + hh]], key="wgf")
            P.op("pool", lambda e: e.tensor_copy(wg[q].rearrange("p a b c -> p (a b c)"),
                                                 wgf.rearrange("p a b c -> p (a b c)")),
                 reads=R_wgf, writes=[R_wg[q]])
            P.dma("sp", padc[q][:, 2:258], zxa[cc, :, 0:256], reads=[R_zxa[cc]], writes=[R_padc[q]], key=f"padc{q}")
            P.dma("sp", padl[q][:, 2:2 + L], zxa[cc, :, 256:NTOK], reads=[R_zxa[cc]], writes=[R_padl[q]], key=f"padl{q}")
            for k in range(4):
                P.op("pool", lambda e, k=k: e.tensor_scalar(dk[q][:, k, :], ident, cw[:, l, cc, k:k + 1], None, ALU.mult),
                     reads=[R_ident, R_sm], writes=[R_dk[q]])

        def do_cc(cc):
            q = cc % 2
            if not last:
                P.dma("sp", gga, zga[cc], reads=[R_zga[cc]], writes=[R_gga], key="gga")
            else:
                P.dma("sp", gga[:, 256:NTOK], zga[cc, :, 256:NTOK], reads=[R_zga[cc]], writes=[R_gga], key="gga")
            for si, (t0, n) in enumerate(segs):
                if si == 0:
                    src = lambda k, t0=t0, n=n: padc[q][:, k:k + n]
                else:
                    src = lambda k, t0=t0, n=n: padl[q][:, t0 - 256 + k:t0 - 256 + k + n]
                bank = 6
                ps = PS[bank]
                def emit(e, src=src, n=n, ps=ps):
                    for k in range(4):
                        ins = e.matmul(ps[:, 0:n], dk[q][:, k, :], src(k), start=(k == 0), stop=(k == 3))
                    return ins
                P.op("pe", emit, reads=[R_dk[q], R_padc[q] if si == 0 else R_padl[q]], writes=[R_ps[bank]])
                P.op("dve", lambda e, t0=t0, n=n, ps=ps: e.tensor_scalar(xc[:, t0:t0 + n], ps[:, 0:n], cb[:, l, cc:cc + 1], None,
                                                                        ALU.add),
                     reads=[R_ps[bank], R_sm], writes=[R_xc[si]])
                P.op("dve", lambda e, t0=t0, n=n: e.tensor_copy(xcb[:, t0:t0 + n], xc[:, t0:t0 + n]),
                     reads=[R_xc[si]], writes=[R_xcb[si]])

            if cc + 1 < 4:
                prefetch(cc + 1)

            def do_dir(d):
                Ab, Sb, Bb = Ab2[d], Sb2[d], Bb2[d]
                R_A, R_S, R_B = R_A2[d], R_S2[d], R_B2[d]
                for si in range(9):
                    t0, n = segs[si]
                    if cntbox[0] % 2 == 1 and mod_next:
                        mod_next.pop(0)()
                    bank = 2 * (cntbox[0] % 3)
                    b2 = cntbox[0] % 2
                    cntbox[0] += 1
                    psr, psi = PS[bank], PS[bank + 1]
                    P.op("pe", lambda e, psr=psr, t0=t0, n=n: e.matmul(psr[:, 0:n], wg[q][:, d, 0, :], xcb[:, t0:t0 + n],
                                                                      start=True, stop=True),
                         reads=[R_wg[q], R_xcb[si]], writes=[R_ps[bank]])
                    P.op("pe", lambda e, psi=psi, t0=t0, n=n: e.matmul(psi[:, 0:n], wg[q][:, d, 1, :], xcb[:, t0:t0 + n],
                                                                      start=True, stop=True),
                         reads=[R_wg[q], R_xcb[si]], writes=[R_ps[bank + 1]])
                    P.op("act", lambda e, psr=psr, n=n, b2=b2: e.activation(
                        thr[b2][:, 0:n], psr[:, 0:n], AF.Tanh, bias=HB[:, l, d, 0, cc:cc + 1], scale=0.5),
                        reads=[R_ps[bank], R_const], writes=[R_thr[b2]])
                    P.op("act", lambda e, psi=psi, n=n, b2=b2: e.activation(
                        thi[b2][:, 0:n], psi[:, 0:n], AF.Tanh, bias=HB[:, l, d, 1, cc:cc + 1], scale=0.5),
                        reads=[R_ps[bank + 1], R_const], writes=[R_thi[b2]])
                    P.op("act", lambda e, t0=t0, n=n, b2=b2: e.activation(
                        Ab[:, t0:t0 + n], thr[b2][:, 0:n], AF.Exp, bias=CH2[:, l, d, cc:cc + 1], scale=CH2[:, l, d, cc:cc + 1]),
                        reads=[R_thr[b2], R_const], writes=[R_A[si]])
                    P.op("dve", lambda e, t0=t0, n=n: e.scalar_tensor_tensor(
                        Sb[:, t0:t0 + n], Ab[:, t0:t0 + n], -1.0, Ab[:, t0:t0 + n], ALU.mult, ALU.mult),
                        reads=[R_A[si]], writes=[R_S[si]])
                    P.op("dve", lambda e, t0=t0, n=n, b2=b2: e.scalar_tensor_tensor(
                        Bb[:, t0:t0 + n], thi[b2][:, 0:n], 1.0, xc[:, t0:t0 + n], ALU.add, ALU.mult),
                        reads=[R_thi[b2], R_xc[si]], writes=[R_B[si]])
                for (a0, a1, rs) in [(0, 2304, R_S[0:5]), (2304, NTOK, R_S[5:9])]:
                    P.op("act", lambda e, a0=a0, a1=a1: e.activation(Sb[:, a0:a1], Sb[:, a0:a1], AF.Sqrt, bias=0.25, scale=0.25),
                         reads=rs, writes=rs)
                for si in range(9):
                    t0, n = segs[si]
                    P.op("dve", lambda e, t0=t0, n=n: e.tensor_tensor(Bb[:, t0:t0 + n], Bb[:, t0:t0 + n], Sb[:, t0:t0 + n], ALU.mult),
                         reads=[R_S[si], R_B[si]], writes=[R_B[si]])
                if d == 0:
                    P.op("dve", lambda e: e.tensor_tensor_scan(Hf[:, 0:256], Ab[:, 0:256], Bb[:, 0:256], 0.0, ALU.mult, ALU.add),
                         reads=[R_A[0], R_B[0]], writes=[R_Hf])
                    P.op("dve", lambda e: e.tensor_tensor_scan(Hf[:, 256:NTOK], Ab[:, 256:NTOK], Bb[:, 256:NTOK], Hf[:, 255:256],
                                                              ALU.mult, ALU.add),
                         reads=R_A[1:] + R_B[1:] + [R_Hf], writes=[R_Hf])
                else:
                    P.op("dve", lambda e: e.tensor_tensor_scan(Hb[:, 0:256][:, ::-1], Ab[:, 0:256][:, ::-1],
                                                              Bb[:, 0:256][:, ::-1], 0.0, ALU.mult, ALU.add),
                         reads=[R_A[0], R_B[0]], writes=[R_S[0]])
                    P.op("dve", lambda e: e.tensor_tensor_scan(Hb[:, 256:NTOK][:, ::-1], Ab[:, 256:NTOK][:, ::-1],
                                                              Bb[:, 256:NTOK][:, ::-1], Hb[:, 0:1], ALU.mult, ALU.add),
                         reads=R_A[1:] + R_B[1:] + [R_S[0]], writes=R_S[1:])

            for d in range(2):
                do_dir(d)
            c0 = 256 if last else 0
            for (a0, a1) in [(c0, 2304), (2304, NTOK)]:
                P.op("dve", lambda e, a0=a0, a1=a1: e.tensor_tensor(Hf[:, a0:a1], Hf[:, a0:a1], Hb[:, a0:a1], ALU.add),
                     reads=[R_Hf] + R_S2[1], writes=[R_Hf])
                P.op("pool", lambda e, a0=a0, a1=a1: e.tensor_tensor(yst[:, a0:a1], Hf[:, a0:a1], gga[:, a0:a1], ALU.mult),
                     reads=[R_Hf, R_gga], writes=R_B2[1])
            b0 = 1 if last else 0
            P.dma("sp", yT[cc, :, b0 * 256:NTOK],
                  yst[:, b0 * 256:NTOK], reads=R_B2[1],
                  writes=[R_y[b][cc] for b in range(b0, NB)], key="ysta")

        prefetch(0)
        for cc in range(4):
            do_cc(cc)
        while mod_next:
            mod_next.pop(0)()

    def gmlp_phase(l, reset=True):
        if reset:
            A.off = persist_off
        last = l == DEPTH - 1
        wsT = A.alloc([4, 128], BF16)
        R_wsl = [Res(f"wsT{g}") for g in range(4)]
        for g in range(4):
            P.dma("pool", wsT[:, g, :], wsT_d[l, g], writes=[R_wsl[g]], key="wsT")
        gu_ = A.alloc([2, NTOK], BF16)
        R_gul = [Res("gmlp_u0"), Res("gmlp_u1")]
        for j in range(2):
            P.dma("sp", gu_[:, j, :], zu[j], reads=[R_zu[j]], writes=[R_gul[j]], key="gmu")
        vb = [A.alloc([4, 256], BF16) for _ in range(2)]
        R_vb = [Res("vb0"), Res("vb1")]
        tm = [A.alloc([512], F32) for _ in range(2)]
        R_tm = [Res("gtm0"), Res("gtm1")]
        yb = [A.alloc([2, 512], BF16) for _ in range(2)]
        R_yb = [[Res(f"gyb{i}_{j}") for j in range(2)] for i in range(2)]
        bsbc = S("bsbc").rearrange("p (l j n) -> p l j n", l=2, j=2)
        batches = [] if last else [(0, 2)]
        batches += [(2 + 4 * k, 4) for k in range(8)]
        for bi, (c0, nch) in enumerate(batches):
            sl = bi % 2
            n = nch * 128
            t0 = c0 * 128
            P.dma("sp", vb[sl][:, 0:nch, :], zv[:, c0:c0 + nch, :], reads=[R_zv], writes=[R_vb[sl]],
                  key=f"vb{sl}")
            for j in range(2):
                bank = (2 * bi + j) % 4
                ps = PS[bank]
                def emit(e, j=j, ps=ps, nch=nch, sl=sl):
                    for c4 in range(nch):
                        for gg in range(2):
                            g = 2 * j + gg
                            ins = e.matmul(ps[gg * 64:(gg + 1) * 64, c4 * 128:(c4 + 1) * 128],
                                           vb[sl][:, c4, g * 64:(g + 1) * 64], wsT[:, g, :], start=True, stop=True)
                    return ins
                P.op("pe", emit, reads=[R_vb[sl]] + R_wsl, writes=[R_ps[bank]])
                tj = (2 * bi + j) % 2
                P.op("dve", lambda e, ps=ps, j=j, n=n, nch=nch, tj=tj: e.tensor_tensor(
                    tm[tj][:, 0:n].rearrange("p (c q) -> p c q", q=128), ps[:, 0:n].rearrange("p (c q) -> p c q", q=128),
                    bsbc[:, l, j, :].unsqueeze(1).to_broadcast([128, nch, 128]), ALU.add),
                    reads=[R_ps[bank], R_sm], writes=[R_tm[tj]])
                P.op("dve", lambda e, j=j, n=n, t0=t0, tj=tj, sl=sl: e.tensor_tensor(
                    yb[sl][:, j, 0:n], tm[tj][:, 0:n], gu_[:, j, t0:t0 + n], ALU.mult),
                    reads=[R_tm[tj], R_gul[j]], writes=[R_yb[sl][j]])
            bb0 = t0 // T
            nb_ = n // T
            for j in range(2):
                P.dma("sp", yT[4 + j, :, t0:t0 + n],
                      yb[sl][:, j, 0:n], reads=[R_yb[sl][j]],
                      writes=[R_y[b][4 + j] for b in range(bb0, bb0 + nb_)], key=f"gyst{sl}{j}")

    def fourier_phase(l, mid=None):
        A.off = persist_off
        last = l == DEPTH - 1
        cs64 = A.alloc([384], BF16)
        dg = A.alloc([4, 128], BF16)
        R_cs = Res("cs64")
        R_dg = Res("dg")
        P.dma("sp", cs64, CS_d, writes=[R_cs])
        P.dma("sp", dg.rearrange("p a b -> p (a b)"), DG_d, writes=[R_dg])
        c0 = A.alloc([2, 32, 512], BF16)
        R_c0 = [[Res(f"c0_{cs}_{h}") for h in range(2)] for cs in range(2)]
        for cs in range(2):
            for h in range(2):
                P.dma("sp", c0[:, cs, 16 * h:16 * (h + 1), :].rearrange("p a b -> p (a b)"),
                      C0_d[cs, :, 16 * h * 512:16 * (h + 1) * 512], writes=[R_c0[cs][h]], key="c0")
        fT = A.alloc([2, NTOK], BF16)
        R_fTl = [Res("fT0"), Res("fT1")]
        for j in range(2):
            P.dma("sp", fT[:, j, :], zf[j], reads=[R_zf[j]], writes=[R_fTl[j]], key="fT")
        FCS = A.alloc([32, 2, 3, 128], BF16)
        R_F = [[Res(f"FCS{c}_{j}") for j in range(2)] for c in range(32)]
        UV = A.alloc([8, 2, 2, 512], BF16)
        R_UV = [[[Res(f"UV{r}_{j}_{u}") for u in range(2)] for j in range(2)] for r in range(8)]
        ost = [A.alloc([512], BF16) for _ in range(2)]
        R_ost = [Res("fo0"), Res("fo1")]
        ev = [0]

        def evac(dst, src, reads, writes):
            if ev[0] % 2 == 0:
                P.op("act", lambda e: e.activation(dst, src, AF.Copy), reads=reads, writes=writes)
            else:
                P.op("dve", lambda e: e.tensor_copy(dst, src), reads=reads, writes=writes)
            ev[0] += 1

        oc = [0]
        if not last:
            FCSc = A.alloc([2, 2, 3, 128], BF16)
            R_Fc = Res("FCSc")
            clc = A.alloc([2, 2, 256], BF16)
            R_clc = Res("clc")
            P.dma("sp", clc.rearrange("p a b c -> p (a b c)"), CLc_d, writes=[R_clc])
            for i in range(2):
                for j in range(2):
                    bank = (2 * i + j) % 2
                    ps = PS[bank]
                    P.op("pe", lambda e, i=i, j=j, ps=ps: e.matmul(ps[:, 0:384], fT[:, j, i * 128:(i + 1) * 128], cs64,
                                                                 start=True, stop=True),
                         reads=R_fTl + [R_cs], writes=[R_ps[bank]])
                    evac(FCSc[:, i, j].rearrange("p a b -> p (a b)"), ps[:, 0:384], [R_ps[bank]], [R_Fc])
            for j in range(2):
                bank = 2 + j
                ps = PS[bank]
                def emit(e, j=j, ps=ps):
                    k = 0
                    for cs in range(2):
                        for i in range(2):
                            ins = e.matmul(ps[:, 0:256], FCSc[:, i, j, cs, :], clc[:, cs, i, :], start=(k == 0), stop=(k == 3))
                            k += 1
                    return ins
                P.op("pe", emit, reads=[R_Fc, R_clc], writes=[R_ps[bank]])
                osl = oc[0] % 2
                oc[0] += 1
                evac(ost[osl][:, 0:256], ps[:, 0:256], [R_ps[bank]], [R_ost[osl]])
                P.dma("sp", yT[6 + j, :, 0:256], ost[osl][:, 0:256], reads=[R_ost[osl]],
                      writes=[R_y[0][6 + j]], key=f"fost{osl}")
        cnt = 0
        for c in range(32):
            r, qc = c // 4, c % 4
            tb = 256 + r + 1024 * qc
            for j in range(2):
                bank = cnt % 2
                cnt += 1
                ps = PS[bank]
                P.op("pe", lambda e, j=j, ps=ps, tb=tb: e.matmul(ps[:, 0:384], fT[:, j, tb:tb + 1017:8], cs64, start=True, stop=True),
                     reads=R_fTl + [R_cs], writes=[R_ps[bank]])
                evac(FCS[:, c, j].rearrange("p a b -> p (a b)"), ps[:, 0:384], [R_ps[bank]], [R_F[c][j]])
        if mid is not None:
            mid()
        cnt = 0
        for r in range(8):
            for j in range(2):
                for u in range(2):
                    bank = 2 + cnt % 4
                    cnt += 1
                    ps = PS[bank]
                    def emit(e, r=r, j=j, u=u, ps=ps):
                        k = 0
                        for qc in range(4):
                            c = 4 * r + qc
                            a0, a1 = (0, 1) if u == 0 else (1, 2)
                            e.matmul(ps[:, 0:512], FCS[:, c, j, a0, :], c0[:, 0, c, :], start=(k == 0), stop=False)
                            ins = e.matmul(ps[:, 0:512], FCS[:, c, j, a1, :], c0[:, 1, c, :], start=False, stop=(k == 3))
                            k += 1
                        return ins
                    P.op("pe", emit, reads=[R_F[4 * r + qc][j] for qc in range(4)] + R_c0[0] + R_c0[1],
                         writes=[R_ps[bank]])
                    evac(UV[:, r, j, u, :], ps[:, 0:512], [R_ps[bank]], [R_UV[r][j][u]])
        def cidx(v):
            if abs(v) < 1e-9:
                return None
            if abs(v - 1.0) < 1e-6:
                return 0
            if abs(v + 1.0) < 1e-6:
                return 1
            return 2 if v > 0 else 3
        cnt = 0
        for kb in range(8):
            for j in range(2):
                terms = []
                for r in range(8):
                    ang = 2.0 * np.pi * ((r * kb) % 8) / 8.0
                    iu = cidx(np.cos(ang))
                    iv = cidx(-np.sin(ang))
                    if iu is not None:
                        terms.append((iu, r, 0))
                    if iv is not None:
                        terms.append((iv, r, 1))
                bank = 6 + cnt % 2
                cnt += 1
                ps = PS[bank]
                def emit(e, j=j, ps=ps, terms=terms):
                    nt = len(terms)
                    for k, (ix, r, u) in enumerate(terms):
                        ins = e.matmul(ps[:, 0:512], dg[:, ix, :], UV[:, r, j, u, :], start=(k == 0), stop=(k == nt - 1))
                    return ins
                P.op("pe", emit, reads=[R_dg] + [R_UV[r][j][u] for (_, r, u) in terms], writes=[R_ps[bank]])
                osl = oc[0] % 2
                oc[0] += 1
                evac(ost[osl], ps[:, 0:512], [R_ps[bank]], [R_ost[osl]])
                bb0 = 1 + 2 * kb
                P.dma("sp", yT[6 + j, :, 256 + kb * 512:256 + (kb + 1) * 512],
                      ost[osl], reads=[R_ost[osl]],
                      writes=[R_y[bb0][6 + j], R_y[bb0 + 1][6 + j]], key=f"fost{osl}")

    def final_phase():
        A.off = persist_off
        hbuf = [A.alloc([KC, T], F32) for _ in range(3)]
        R_hbuf = [[Res(f"fh{i}_{m}") for m in range(KC)] for i in range(3)]
        obuf = [A.alloc([KC, T], F32) for _ in range(3)]
        R_ob = [[Res(f"fo{i}_{m}") for m in range(KC)] for i in range(3)]
        sq = [A.alloc([T], BF16) for _ in range(2)]
        R_sq = [Res("fsq0"), Res("fsq1")]
        rstd = [A.alloc([T], F32) for _ in range(2)]
        R_rstd = [Res("frs0"), Res("frs1")]
        fng = S("fng")
        for i in range(min(3, NB - 1)):
            P.dma("sp", hbuf[i].rearrange("p a b -> p (a b)"), hTb[1 + i], reads=[R_h[1 + i]], writes=R_hbuf[i], key=f"fl{i}")
        for i in range(NB - 1):
            b = 1 + i
            sl = i % 3
            for st in norm_steps(hbuf[sl], R_hbuf[sl], obuf[sl], R_ob[sl], sq, R_sq, rstd[i % 2], R_rstd[i % 2], None, None,
                                 i % 2, lambda m: fng[:, m:m + 1], None, mode="mixed"):
                st()
            o = P.dma("sp", outTb[b - 1], obuf[sl].rearrange("p a b -> p (a b)"), reads=R_ob[sl], key=f"fs{sl}")
            out_stores.append(o)
            if i + 3 < NB - 1:
                P.dma("sp", hbuf[sl].rearrange("p a b -> p (a b)"), hTb[b + 3], reads=[R_h[b + 3]], writes=R_hbuf[sl],
                      key=f"fl{sl}")

    phases = []
    phases.append(("prologue", prologue))
    for l in range(DEPTH):
        phases.append((f"ffn1_{l}", lambda l=l: ffn_phase(l, 0)))
        phases.append((f"mixin_{l}", lambda l=l: mixin_phase(l)))
        phases.append((f"lru_{l}", lambda l=l: lru_phase(l)))
        phases.append((f"fourier_{l}", lambda l=l: fourier_phase(l, mid=lambda: gmlp_phase(l, reset=False))))
        phases.append((f"ffn2_{l}", lambda l=l: ffn_phase(l, 2)))
    phases.append(("final", final_phase))
    for name, fn in phases:
        P.fence()
        fn()
        if stop_after is not None and name == stop_after:
            break
    finals = list(out_stores)
    for r in R_h + R_zxa + R_zga + R_zu + R_zf + [R_zv] + [x for row in R_y for x in row]:
        if r.w is not None and r.w.is_dma:
            finals.append(r.w)
    P.wait_all("sp", finals)
    P.emit()
    return nc, P


_CONST_CACHE = {}


def prepare_inputs(inputs):
    if "c" not in _CONST_CACHE:
        _CONST_CACHE["c"] = build_consts()
    C0, DG, CLc, CS = _CONST_CACHE["c"]
    f32 = lambda a: np.ascontiguousarray(np.asarray(a, np.float32))
    x = f32(inputs["x"])
    ctx = f32(inputs["ctx"])
    shared = {
        "w_mod": f32(inputs["w_mod"]),
        "ffn_w_gu": f32(inputs["ffn_w_gu"]),
        "ffn_w_down": f32(inputs["ffn_w_down"]),
        "w_in": f32(inputs["w_in"]),
        "w_out": f32(inputs["w_out"]),
        "lru_w_gates": f32(inputs["lru_w_gates"]),
        "gmlp_wsT": np.ascontiguousarray(np.transpose(f32(inputs["gmlp_ws"]), (0, 1, 3, 2))),
        "C0": C0, "DG": DG, "CLc": CLc, "CS": CS,
    }
    in_maps = []
    for b in range(8):
        xb = x[b].reshape(16, T, KC, 128).transpose(0, 3, 2, 1).reshape(16, 128, KC * T)
        cb = ctx[b].reshape(T, KC, 128).transpose(2, 1, 0).reshape(128, KC * T)
        m = dict(shared)
        m["xTb"] = np.ascontiguousarray(xb)
        m["ctxTb"] = np.ascontiguousarray(cb)
        m["smalls"] = build_smalls(b, inputs)
        in_maps.append(m)
    return in_maps


def kernel(**inputs):
    in_maps = prepare_inputs(inputs)
    nc, _ = build()
    res = run_bass_kernel_spmd(nc, in_maps, core_ids=list(range(8)))
    out = np.empty((8, L, D), np.float32)
    for b in range(8):
        o = np.asarray(res.results[b]["outTb"], np.float32).reshape(16, 128, KC, T)
        out[b] = o.transpose(0, 3, 2, 1).reshape(L, D)
    return out
```
